# Optimizing a Trainium2 kernel written in Bass

```python
import math
import jax, jax.numpy as jnp
from jax import lax
import numpy as np

D_MODEL = 4096
BATCH = 1
SEQ = 8192
DEPTH = 2

N_EVEN = (DEPTH + 1) // 2
N_ODD = DEPTH // 2

GDN_HEAD_DIM = 128
GDN_WIDTH = D_MODEL // 2
GDN_HEADS = GDN_WIDTH // GDN_HEAD_DIM
CONV_WIDTH = 4
GDN_CHUNK = 64
FOX_HEAD_DIM = 128
FOX_WIDTH = D_MODEL - GDN_WIDTH
FOX_HEADS = FOX_WIDTH // FOX_HEAD_DIM
FOX_BLOCK = 128
MIX_WIDTH = GDN_WIDTH + FOX_WIDTH
PROJ_SIZES = (GDN_WIDTH, GDN_WIDTH, GDN_WIDTH, GDN_WIDTH, GDN_HEADS, GDN_HEADS,
              FOX_WIDTH, FOX_WIDTH, FOX_WIDTH, FOX_HEADS)
PROJ_TOTAL = 4 * GDN_WIDTH + 2 * GDN_HEADS + 3 * FOX_WIDTH + FOX_HEADS
S5_GROUP = 16
S5_GROUPS = D_MODEL // S5_GROUP
S5_STATE = 64
S5_CHUNK = 128
S5_MAX_RE = -1e-4
FFN_HIDDEN = -(-8 * D_MODEL // (3 * 256)) * 256
RMS_EPS = 1e-6
L2_EPS = 1e-6

kernel_name = "hybrid_gdn_fox_s5_sandwich"


def _rmsnorm(x, g):
    xf = x.astype(jnp.float32)
    y = xf * lax.rsqrt(jnp.mean(xf * xf, axis=-1, keepdims=True) + RMS_EPS)
    return (y * g.astype(jnp.float32)).astype(x.dtype)


def _l2norm(x):
    return x * lax.rsqrt(jnp.sum(x * x, axis=-1, keepdims=True) + L2_EPS)


def _causal_dwconv(x, w):
    k, c = w.shape
    return lax.conv_general_dilated(
        x, w[:, None, :].astype(x.dtype), window_strides=(1,), padding=[(k - 1, 0)],
        dimension_numbers=('NWC', 'WIO', 'NWC'), feature_group_count=c)


def _gated_delta_rule(q, k, v, g, beta):
    bsz, t, h, dk = q.shape
    dv = v.shape[-1]
    c = GDN_CHUNK
    n = t // c
    q = q * (dk ** -0.5)

    def chunks(a):
        return a.reshape(bsz, n, c, h, -1).transpose(0, 3, 1, 2, 4)

    q, k, v = chunks(q), chunks(k), chunks(v)
    g = jnp.cumsum(g.reshape(bsz, n, c, h).transpose(0, 3, 1, 2), axis=-1)
    beta = beta.reshape(bsz, n, c, h).transpose(0, 3, 1, 2)
    tri = jnp.tril(jnp.ones((c, c), dtype=bool))
    strict = jnp.tril(jnp.ones((c, c), dtype=bool), -1)
    decay = jnp.exp(jnp.where(tri, g[..., :, None] - g[..., None, :], -jnp.inf))
    kb = k * beta[..., None]
    vb = v * beta[..., None]
    lower = jnp.where(strict, jnp.einsum('bhnid,bhnjd->bhnij', kb, k) * decay, 0.0)
    eye = jnp.eye(c, dtype=q.dtype)
    rhs = jnp.concatenate([vb, kb * jnp.exp(g)[..., None]], axis=-1)
    sol = lax.linalg.triangular_solve(eye + lower, rhs, left_side=True, lower=True,
                                      unit_diagonal=True)
    u, w = sol[..., :dv], sol[..., dv:]
    attn = jnp.einsum('bhnid,bhnjd->bhnij', q, k) * decay

    def step(state, inp):
        qc, kc, uc, wc, gc, ac = inp
        v_new = uc - jnp.einsum('bhck,bhkv->bhcv', wc, state)
        o = (jnp.einsum('bhck,bhkv->bhcv', qc * jnp.exp(gc)[..., None], state)
             + jnp.einsum('bhcs,bhsv->bhcv', ac, v_new))
        g_last = gc[..., -1]
        state = (state * jnp.exp(g_last)[..., None, None]
                 + jnp.einsum('bhck,bhcv->bhkv',
                              kc * jnp.exp(g_last[..., None] - gc)[..., None], v_new))
        return state, o

    xs = tuple(jnp.moveaxis(a, 2, 0) for a in (q, k, u, w, g, attn))
    s0 = jnp.zeros((bsz, h, dk, dv), dtype=q.dtype)
    _, o = lax.scan(step, s0, xs)
    return o.transpose(1, 0, 3, 2, 4).reshape(bsz, t, h, dv)


def _forgetting_attention(q, k, v, f_logit):
    bsz, t, _ = q.shape
    h, d = FOX_HEADS, FOX_HEAD_DIM
    q = q.astype(jnp.float32).reshape(bsz, t, h, d) * (d ** -0.5)
    k = k.astype(jnp.float32).reshape(bsz, t, h, d)
    v = v.astype(jnp.float32).reshape(bsz, t, h, d)
    cum = jnp.cumsum(jax.nn.log_sigmoid(f_logit.astype(jnp.float32)), axis=1)
    cum_k = cum.transpose(0, 2, 1)
    nb = t // FOX_BLOCK
    q_blocks = q.reshape(bsz, nb, FOX_BLOCK, h, d).transpose(1, 0, 2, 3, 4)
    c_blocks = cum_k.reshape(bsz, h, nb, FOX_BLOCK).transpose(2, 0, 1, 3)
    k_pos = jnp.arange(t)

    def block(args):
        qi, ci, bi = args
        q_pos = bi * FOX_BLOCK + jnp.arange(FOX_BLOCK)
        s = jnp.einsum('bqhd,bkhd->bhqk', qi, k) + (ci[..., :, None] - cum_k[..., None, :])
        s = jnp.where(k_pos[None, :] <= q_pos[:, None], s, -jnp.inf)
        p = jax.nn.softmax(s, axis=-1)
        return jnp.einsum('bhqk,bkhd->bqhd', p, v)

    o = lax.map(block, (q_blocks, c_blocks, jnp.arange(nb)))
    return o.transpose(1, 0, 2, 3, 4).reshape(bsz, t, h * d)


def _even_mixer(hn, w_in, conv_w, a_log, dt_bias, o_norm, f_bias, w_out):
    bsz, t, _ = hn.shape
    f32 = jnp.float32
    proj = hn @ w_in
    split_at = np.cumsum(PROJ_SIZES)[:-1].tolist()
    aq, ak, av, az, aa, ab, fq, fk, fv, ff = jnp.split(proj, split_at, axis=-1)
    qkv = jax.nn.silu(_causal_dwconv(jnp.concatenate([aq, ak, av], axis=-1), conv_w)).astype(f32)
    aq, ak, av = jnp.split(qkv, [GDN_WIDTH, 2 * GDN_WIDTH], axis=-1)
    heads = (bsz, t, GDN_HEADS, GDN_HEAD_DIM)
    aq = _l2norm(aq.reshape(heads))
    ak = _l2norm(ak.reshape(heads))
    av = av.reshape(heads)
    g = -jnp.exp(a_log.astype(f32)) * jax.nn.softplus(aa.astype(f32) + dt_bias.astype(f32))
    beta = jax.nn.sigmoid(ab.astype(f32))
    o_a = _gated_delta_rule(aq, ak, av, g, beta)
    o_a = _rmsnorm(o_a, o_norm) * jax.nn.silu(az.astype(f32).reshape(heads))
    o_b = _forgetting_attention(fq, fk, fv, ff + f_bias)
    o = jnp.concatenate([o_a.reshape(bsz, t, GDN_WIDTH), o_b], axis=-1).astype(hn.dtype)
    return o @ w_out


def _s5_mixer(hn, a_re, a_im, log_step, b_re, b_im, c_re, c_im, d_skip, w_glu_a, w_glu_b):
    bsz, t, d = hn.shape
    f32 = jnp.float32
    u = hn.astype(f32)
    lam_re = jnp.minimum(a_re.astype(f32), S5_MAX_RE)
    lam_im = a_im.astype(f32)
    dt = jnp.exp(log_step.astype(f32))[:, None]
    mag = jnp.exp(lam_re * dt)
    ab_re = mag * jnp.cos(lam_im * dt)
    ab_im = mag * jnp.sin(lam_im * dt)
    nr, ni = ab_re - 1.0, ab_im
    den = lam_re * lam_re + lam_im * lam_im
    f_re = (nr * lam_re + ni * lam_im) / den
    f_im = (ni * lam_re - nr * lam_im) / den
    b_re, b_im = b_re.astype(f32), b_im.astype(f32)
    bb_re = f_re[..., None] * b_re - f_im[..., None] * b_im
    bb_im = f_re[..., None] * b_im + f_im[..., None] * b_re
    c_re, c_im = c_re.astype(f32), c_im.astype(f32)
    L = S5_CHUNK
    nc = t // L
    u_chunks = u.reshape(bsz, nc, L, S5_GROUPS, S5_GROUP).transpose(1, 2, 0, 3, 4)
    a_seq_re = jnp.broadcast_to(ab_re, (L, bsz, S5_GROUPS, S5_STATE))
    a_seq_im = jnp.broadcast_to(ab_im, (L, bsz, S5_GROUPS, S5_STATE))

    def combine(e1, e2):
        a1r, a1i, b1r, b1i = e1
        a2r, a2i, b2r, b2i = e2
        return (a2r * a1r - a2i * a1i, a2r * a1i + a2i * a1r,
                a2r * b1r - a2i * b1i + b2r, a2r * b1i + a2i * b1r + b2i)

    def step(carry, uc):
        sr0, si0 = carry
        bu_r = jnp.einsum('lbgp,gnp->lbgn', uc, bb_re)
        bu_i = jnp.einsum('lbgp,gnp->lbgn', uc, bb_im)
        pr, pi_, xr, xi = lax.associative_scan(combine, (a_seq_re, a_seq_im, bu_r, bu_i), axis=0)
        sr = xr + pr * sr0 - pi_ * si0
        si = xi + pr * si0 + pi_ * sr0
        y = jnp.einsum('gpn,lbgn->lbgp', c_re, sr) - jnp.einsum('gpn,lbgn->lbgp', c_im, si)
        return (sr[-1], si[-1]), y.reshape(L, bsz, d)

    s0 = jnp.zeros((bsz, S5_GROUPS, S5_STATE), dtype=f32)
    _, ys = lax.scan(step, (s0, s0), u_chunks)
    y = ys.transpose(2, 0, 1, 3).reshape(bsz, t, d) + d_skip.astype(f32) * u
    y = jax.nn.gelu(y).astype(hn.dtype)
    return (y @ w_glu_a) * jax.nn.sigmoid(y @ w_glu_b)


def _swiglu(h, w_gate, w_up, w_down):
    return (jax.nn.silu(h @ w_gate) * (h @ w_up)) @ w_down


def setup_inputs(seed: int = 0) -> dict:
    key = jax.random.key(seed)
    ks = jax.random.split(key, 24)
    f32 = jnp.float32
    nrm = jax.random.normal
    uni = jax.random.uniform
    d = D_MODEL
    ne, no = N_EVEN, N_ODD
    g, n, p = S5_GROUPS, S5_STATE, S5_GROUP
    x = nrm(ks[0], (BATCH, SEQ, d), f32)
    norm_g = 1.0 + 0.02 * nrm(ks[1], (DEPTH, 4, d), f32)
    w_in = nrm(ks[2], (ne, d, PROJ_TOTAL), f32) * d ** -0.5
    gdn_conv_w = nrm(ks[3], (ne, CONV_WIDTH, 3 * GDN_WIDTH), f32) * CONV_WIDTH ** -0.5
    gdn_A_log = jnp.log(uni(ks[4], (ne, GDN_HEADS), f32, 1.0, 16.0))
    dt0 = jnp.exp(uni(ks[5], (ne, GDN_HEADS), f32, math.log(1e-3), math.log(1e-1)))
    gdn_dt_bias = dt0 + jnp.log(-jnp.expm1(-dt0))
    gdn_o_norm = 1.0 + 0.02 * nrm(ks[6], (ne, GDN_HEAD_DIM), f32)
    fox_f_bias = uni(ks[7], (ne, FOX_HEADS), f32, 1.0, 3.0)
    w_out = nrm(ks[8], (ne, MIX_WIDTH, d), f32) * MIX_WIDTH ** -0.5
    s5_A_re = -0.5 + 0.01 * nrm(ks[9], (no, g, n), f32)
    s5_A_im = math.pi * jnp.arange(n, dtype=f32)[None, None, :] + 0.01 * nrm(ks[10], (no, g, n), f32)
    s5_log_step = uni(ks[11], (no, g), f32, math.log(1e-3), math.log(1e-1))
    s5_B_re = nrm(ks[12], (no, g, n, p), f32) * (2 * p) ** -0.5
    s5_B_im = nrm(ks[13], (no, g, n, p), f32) * (2 * p) ** -0.5
    s5_C_re = nrm(ks[14], (no, g, p, n), f32) * (2 * n) ** -0.5
    s5_C_im = nrm(ks[15], (no, g, p, n), f32) * (2 * n) ** -0.5
    s5_D = nrm(ks[16], (no, d), f32)
    s5_w_glu_a = nrm(ks[17], (no, d, d), f32) * d ** -0.5
    s5_w_glu_b = nrm(ks[18], (no, d, d), f32) * d ** -0.5
    ffn_w_gate = nrm(ks[19], (DEPTH, d, FFN_HIDDEN), f32) * d ** -0.5
    ffn_w_up = nrm(ks[20], (DEPTH, d, FFN_HIDDEN), f32) * d ** -0.5
    ffn_w_down = nrm(ks[21], (DEPTH, FFN_HIDDEN, d), f32) * FFN_HIDDEN ** -0.5
    return {"x": x, "norm_g": norm_g, "w_in": w_in, "gdn_conv_w": gdn_conv_w,
            "gdn_A_log": gdn_A_log, "gdn_dt_bias": gdn_dt_bias, "gdn_o_norm": gdn_o_norm,
            "fox_f_bias": fox_f_bias, "w_out": w_out, "s5_A_re": s5_A_re, "s5_A_im": s5_A_im,
            "s5_log_step": s5_log_step, "s5_B_re": s5_B_re, "s5_B_im": s5_B_im,
            "s5_C_re": s5_C_re, "s5_C_im": s5_C_im, "s5_D": s5_D, "s5_w_glu_a": s5_w_glu_a,
            "s5_w_glu_b": s5_w_glu_b, "ffn_w_gate": ffn_w_gate, "ffn_w_up": ffn_w_up,
            "ffn_w_down": ffn_w_down}


def reference(x, norm_g, w_in, gdn_conv_w, gdn_A_log, gdn_dt_bias, gdn_o_norm, fox_f_bias,
              w_out, s5_A_re, s5_A_im, s5_log_step, s5_B_re, s5_B_im, s5_C_re, s5_C_im, s5_D,
              s5_w_glu_a, s5_w_glu_b, ffn_w_gate, ffn_w_up, ffn_w_down):
    for layer in range(DEPTH):
        gains = norm_g[layer]
        i = layer // 2
        hn = _rmsnorm(x, gains[0])
        if layer % 2 == 0:
            m = _even_mixer(hn, w_in[i], gdn_conv_w[i], gdn_A_log[i], gdn_dt_bias[i],
                            gdn_o_norm[i], fox_f_bias[i], w_out[i])
        else:
            m = _s5_mixer(hn, s5_A_re[i], s5_A_im[i], s5_log_step[i], s5_B_re[i], s5_B_im[i],
                          s5_C_re[i], s5_C_im[i], s5_D[i], s5_w_glu_a[i], s5_w_glu_b[i])
        x = x + _rmsnorm(m.astype(x.dtype), gains[1])
        f = _swiglu(_rmsnorm(x, gains[2]), ffn_w_gate[layer], ffn_w_up[layer], ffn_w_down[layer])
        x = x + _rmsnorm(f.astype(x.dtype), gains[3])
    return x
```

```python
import contextlib
import math


import numpy as np
import concourse.bass as bass
import concourse.mybir as mybir

F32 = mybir.dt.float32
BF16 = mybir.dt.bfloat16
I32 = mybir.dt.int32
AF = mybir.ActivationFunctionType
ALU = mybir.AluOpType
AX = mybir.AxisListType

ENGS = ["pe", "dve", "act", "pool", "sp"]
ATTR = {"pe": "tensor", "dve": "vector", "act": "scalar", "pool": "gpsimd", "sp": "sync"}


class Prog:
    def __init__(self, nc, nslots=6):
        self.nc = nc
        self.sem = {e: nc.alloc_semaphore("s_" + e) for e in ENGS}
        self.base = {e: 0 for e in ENGS}
        self.queues = {}
        for qn, issuer in (("sp", "sp"), ("pool", "pool"), ("act", "act")):
            self.queues[qn] = dict(
                issuer=issuer,
                sems=[nc.alloc_semaphore("q_%s_%d" % (qn, i)) for i in range(nslots)],
                cnt=[0] * nslots,
                n=0,
            )
        self.nslots = nslots
        self.maxwait = {e: {} for e in ENGS}
        self._reset()

    def _reset(self):
        self.ops = []
        self.res = {}
        self.stream = {e: [] for e in ENGS}

    def _deps(self, reads, writes):
        deps = set()
        for r in reads:
            st = self.res.get(r)
            if st and st["w"] is not None:
                deps.add(st["w"])
        for w in writes:
            st = self.res.get(w)
            if st:
                if st["w"] is not None:
                    deps.add(st["w"])
                deps.update(st["r"])
        return deps

    def _commit(self, oid, reads, writes):
        for r in reads:
            st = self.res.setdefault(r, {"w": None, "r": []})
            st["r"].append(oid)
        for w in writes:
            self.res[w] = {"w": oid, "r": []}

    def op(self, eng, fn, reads=(), writes=()):
        deps = self._deps(reads, writes)
        oid = len(self.ops)
        self.ops.append(dict(eng=eng, fn=fn, deps=deps, kind="c", marked=False))
        self.stream[eng].append(oid)
        self._commit(oid, reads, writes)
        return oid

    def dma(self, q, out, in_, reads=(), writes=(), **kw):
        Q = self.queues[q]
        slot = Q["n"] % self.nslots
        Q["n"] += 1
        prev = Q["cnt"][slot]
        Q["cnt"][slot] += 1
        deps = self._deps(reads, writes)
        oid = len(self.ops)
        self.ops.append(dict(eng=Q["issuer"], kind="dma", q=q, slot=slot, val=Q["cnt"][slot] * 16,
                             prev=prev * 16, deps=deps, out=out, in_=in_, kw=kw))
        self.stream[Q["issuer"]].append(oid)
        self._commit(oid, reads, writes)
        return oid

    def flush(self):
        ops = self.ops
        for o in ops:
            for d in o["deps"]:
                p = ops[d]
                if p["kind"] == "c":
                    if p["eng"] == o["eng"] and p["eng"] == "pe":
                        continue
                    p["marked"] = True
        cum = {e: self.base[e] for e in ENGS}
        for e in ENGS:
            for oid in self.stream[e]:
                o = ops[oid]
                if o["kind"] == "c" and o["marked"]:
                    cum[e] += 1
                    o["val"] = cum[e]
        prog = self
        nc = self.nc
        with nc.Block() as block:
            for e in ENGS:
                def body(h, e=e):
                    mw = prog.maxwait[e]
                    def wait(sem, key, val):
                        if mw.get(key, 0) >= val:
                            return
                        mw[key] = val
                        h.wait_ge(sem, val)
                    for oid in prog.stream[e]:
                        o = ops[oid]
                        for d in sorted(o["deps"]):
                            p = ops[d]
                            if p["kind"] == "dma":
                                Q = prog.queues[p["q"]]
                                wait(Q["sems"][p["slot"]], ("q", p["q"], p["slot"]), p["val"])
                            else:
                                if not p["marked"]:
                                    continue
                                if p["eng"] == e and e == "pe":
                                    continue
                                wait(prog.sem[p["eng"]], p["eng"], p["val"])
                        if o["kind"] == "dma":
                            Q = prog.queues[o["q"]]
                            if o["prev"] > 0:
                                wait(Q["sems"][o["slot"]], ("q", o["q"], o["slot"]), o["prev"])
                            ins = h.dma_start(out=o["out"], in_=o["in_"], **o["kw"])
                            ins.then_inc(Q["sems"][o["slot"]], 16)
                        else:
                            ins = o["fn"](h)
                            if o["marked"]:
                                ins.then_inc(prog.sem[e], 1)
                    for qn, Q in prog.queues.items():
                        if Q["issuer"] == e:
                            for s in range(prog.nslots):
                                if Q["cnt"][s] > 0:
                                    wait(Q["sems"][s], ("q", qn, s), Q["cnt"][s] * 16)
                getattr(block, ATTR[e])(body)
        for e in ENGS:
            self.base[e] = cum[e]
        self._reset()


D = 4096
HID = 11008
NHC = HID // 128
TOK = 1024
NTQ = TOK // 128
EPS = 1e-6


def bcast_rows(ap_row, nparts):
    return bass.AP(ap_row.tensor, ap_row.offset, [[0, nparts]] + [list(x) for x in ap_row.ap[1:]])


_UID = [0]


def emit_tok_block(P, nc, ps, actT, ident, *, inT, xres, wmix, gains, wg, wu, wd, xout, hnT_out, scr, pfx):
    glu = len(wmix) == 2
    pfx0 = pfx
    ps_bf = [p.bitcast(BF16) for p in ps]
    NPART = 3
    parts = [(0, 29), (29, 58), (58, 86)]

    NB = 4
    CW = 256
    with nc.sbuf_tensor(pfx + "wbuf", [128, NB, 32, CW], BF16) as wbuf, \
         nc.sbuf_tensor(pfx + "stg", [128, 4, CW], F32) as stg, \
         nc.sbuf_tensor(pfx + "sig", [128, 4, CW], F32) as sig:
        inT_v = inT.rearrange("(k p) t -> p k t", p=128)
        for k4 in range(4):
            P.dma("sp", actT[:, k4 * 8:(k4 + 1) * 8, :], inT_v[:, k4 * 8:(k4 + 1) * 8, :], writes=[("actT", k4)])
        wv = [w.rearrange("(k p) n -> p k n", p=128) for w in wmix]
        li = 0
        gi = 0
        for db in range(D // CW):
            bufs = []
            for mi in range(len(wmix)):
                b = li % NB
                li += 1
                for hf in range(2):
                    P.dma("pool", wbuf[:, b, hf * 16:(hf + 1) * 16, :], wv[mi][:, hf * 16:(hf + 1) * 16, db * CW:(db + 1) * CW],
                          writes=[("wbuf", b, hf)])
                bufs.append(b)
            for tq in range(NTQ):
                banks = []
                for mi in range(len(wmix)):
                    bank = (gi * len(wmix) + mi) % 8
                    banks.append(bank)
                    b = bufs[mi]
                    for kc in range(32):
                        P.op("pe", lambda h, bank=bank, b=b, kc=kc, tq=tq: h.matmul(
                            ps[bank][:, 0:CW], actT[:, kc, tq * 128:(tq + 1) * 128], wbuf[:, b, kc, :],
                            start=(kc == 0), stop=(kc == 31)),
                            reads=[("actT", kc // 8), ("wbuf", b, kc // 16)], writes=[("ps", bank)])
                s = gi % 4
                if glu:
                    P.op("act", lambda h, s=s, bk=banks[1]: h.activation(out=sig[:, s, :], in_=ps[bk][:, 0:CW], func=AF.Sigmoid),
                         reads=[("ps", banks[1])], writes=[("sig", s)])
                    P.op("dve", lambda h, s=s, bk=banks[0]: h.tensor_tensor(out=stg[:, s, :], in0=ps[bk][:, 0:CW], in1=sig[:, s, :], op=ALU.mult),
                         reads=[("ps", banks[0]), ("sig", s)], writes=[("stg", s)])
                else:
                    if gi % 2 == 0:
                        P.op("act", lambda h, s=s, bk=banks[0]: h.activation(out=stg[:, s, :], in_=ps[bk][:, 0:CW], func=AF.Copy),
                             reads=[("ps", banks[0])], writes=[("stg", s)])
                    else:
                        P.op("dve", lambda h, s=s, bk=banks[0]: h.tensor_copy(out=stg[:, s, :], in_=ps[bk][:, 0:CW]),
                             reads=[("ps", banks[0])], writes=[("stg", s)])
                P.dma("sp", scr["m"][tq * 128:(tq + 1) * 128, db * CW:(db + 1) * CW], stg[:, s, :], reads=[("stg", s)],
                      writes=[("m", tq)])
                gi += 1
        P.flush()

    def norm_pass(srcs, res_ap, g_post, g_next, x_store, to_actT, hn_dst):
        _UID[0] += 1
        pfx = pfx0 + "n%d_" % _UID[0]
        with nc.sbuf_tensor(pfx + "mt", [128, 2, D], F32) as mt, \
             nc.sbuf_tensor(pfx + "xt", [128, 2, D], F32) as xt, \
             nc.sbuf_tensor(pfx + "gb", [128, 2, D], F32) as gb, \
             nc.sbuf_tensor(pfx + "junk", [128, D], BF16) as junk, \
             nc.sbuf_tensor(pfx + "hn", [128, D], BF16) as hn, \
             nc.sbuf_tensor(pfx + "hst", [128, 2, 4, 128], BF16) as hst, \
             nc.sbuf_tensor(pfx + "st", [128, 8], F32) as st, \
             nc.sbuf_tensor(pfx + "cst", [128, 2], F32) as cst:
            P.dma("sp", gb[:, 0, :], bcast_rows(g_post, 128), writes=[("gb", 0)])
            if g_next is not None:
                P.dma("sp", gb[:, 1, :], bcast_rows(g_next, 128), writes=[("gb", 1)])
            P.op("dve", lambda h: h.memset(cst[:, 0:1], -0.5), writes=["cst"])
            ev = 0
            for tq in range(NTQ):
                b = tq % 2
                rows = slice(tq * 128, (tq + 1) * 128)
                P.dma("sp", mt[:, b, :], srcs[0][rows, :], writes=[("mt", b)])
                P.dma("sp", xt[:, b, :], res_ap[rows, :], writes=[("xt", b)])
                for extra in srcs[1:]:
                    pass
                if len(srcs) > 1:
                    raise NotImplementedError
                P.op("act", lambda h, b=b: h.activation(out=junk[:], in_=mt[:, b, :], func=AF.Square, accum_out=st[:, 0:1]),
                     reads=[("mt", b)], writes=["junk", ("st", 0)])
                P.op("dve", lambda h: h.tensor_scalar(out=st[:, 1:2], in0=st[:, 0:1], scalar1=1.0 / D, scalar2=EPS, op0=ALU.mult, op1=ALU.add),
                     reads=[("st", 0)], writes=[("st", 1)])
                P.op("pool", lambda h: h.tensor_tensor(out=st[:, 2:3], in0=st[:, 1:2], in1=cst[:, 0:1], op=ALU.pow),
                     reads=[("st", 1), "cst"], writes=[("st", 2)])
                P.op("dve", lambda h, b=b: h.scalar_tensor_tensor(out=mt[:, b, :], in0=mt[:, b, :], scalar=st[:, 2:3], in1=gb[:, 0, :],
                                                                 op0=ALU.mult, op1=ALU.mult),
                     reads=[("mt", b), ("st", 2), ("gb", 0)], writes=[("mt", b)])
                P.op("pool", lambda h, b=b: h.tensor_tensor(out=mt[:, b, :], in0=mt[:, b, :], in1=xt[:, b, :], op=ALU.add),
                     reads=[("mt", b), ("xt", b)], writes=[("mt", b)])
                P.dma("sp", x_store[rows, :], mt[:, b, :], reads=[("mt", b)], writes=[("xs", tq)])
                if g_next is None:
                    continue
                P.op("act", lambda h, b=b: h.activation(out=junk[:], in_=mt[:, b, :], func=AF.Square, accum_out=st[:, 3:4]),
                     reads=[("mt", b)], writes=["junk", ("st", 3)])
                P.op("dve", lambda h: h.tensor_scalar(out=st[:, 4:5], in0=st[:, 3:4], scalar1=1.0 / D, scalar2=EPS, op0=ALU.mult, op1=ALU.add),
                     reads=[("st", 3)], writes=[("st", 4)])
                P.op("pool", lambda h: h.tensor_tensor(out=st[:, 5:6], in0=st[:, 4:5], in1=cst[:, 0:1], op=ALU.pow),
                     reads=[("st", 4), "cst"], writes=[("st", 5)])
                P.op("dve", lambda h, b=b: h.scalar_tensor_tensor(out=hn[:], in0=mt[:, b, :], scalar=st[:, 5:6], in1=gb[:, 1, :],
                                                                 op0=ALU.mult, op1=ALU.mult),
                     reads=[("mt", b), ("st", 5), ("gb", 1)], writes=["hn"])
                for k4 in range(8):
                    bank = ev % 8
                    for j in range(4):
                        kc = k4 * 4 + j
                        P.op("pe", lambda h, bank=bank, j=j, kc=kc: h.transpose(
                            out=ps_bf[bank][:, j * 128:(j + 1) * 128], in_=hn[:, kc * 128:(kc + 1) * 128], identity=ident[:]),
                            reads=["hn", "ident"], writes=[("ps", bank)])
                    src = ps_bf[bank][:, 0:512].rearrange("p (a b) -> p a b", a=4)
                    if to_actT:
                        dst = actT[:, k4 * 4:(k4 + 1) * 4, tq * 128:(tq + 1) * 128]
                        wr = [("actT", k4 // 2, tq)]
                    else:
                        hb = ev % 2
                        dst = hst[:, hb, :, :]
                        wr = [("hst", hb)]
                    if ev % 2 == 0:
                        P.op("act", lambda h, dst=dst, src=src: h.activation(out=dst, in_=src, func=AF.Copy),
                             reads=[("ps", bank)], writes=wr)
                    else:
                        P.op("dve", lambda h, dst=dst, src=src: h.tensor_copy(out=dst, in_=src),
                             reads=[("ps", bank)], writes=wr)
                    if not to_actT:
                        dv = hn_dst.rearrange("(k p) t -> p k t", p=128)[:, k4 * 4:(k4 + 1) * 4, tq * 128:(tq + 1) * 128]
                        P.dma("sp", dv, hst[:, hb, :, :], reads=[("hst", hb)], writes=[("hno", k4, tq)])
                    ev += 1
            P.flush()

    norm_pass([scr["m"]], xres, gains[0], gains[1], scr["x1"], True, None)

    wgv = wg.rearrange("(k p) n -> p k n", p=128)
    wuv = wu.rearrange("(k p) n -> p k n", p=128)
    wdv = wd.rearrange("(c p) n -> p c n", p=128)
    with nc.sbuf_tensor(pfx + "hT", [128, 29, TOK], BF16) as hT, \
         nc.sbuf_tensor(pfx + "wgb", [128, 2, 32, 128], BF16) as wgb, \
         nc.sbuf_tensor(pfx + "wub", [128, 2, 32, 128], BF16) as wub, \
         nc.sbuf_tensor(pfx + "wdb", [128, 3, 4, 512], BF16) as wdb, \
         nc.sbuf_tensor(pfx + "sg", [128, 2, 512], BF16) as sg, \
         nc.sbuf_tensor(pfx + "fst", [128, 4, 512], F32) as fst:
        hci = 0
        wdi = 0
        evi = 0
        for pi, (h0, h1) in enumerate(parts):
            nh = h1 - h0
            for hl in range(nh):
                hc = h0 + hl
                b = hci % 2
                hci += 1
                for hf in range(2):
                    P.dma("pool", wgb[:, b, hf * 16:(hf + 1) * 16, :], wgv[:, hf * 16:(hf + 1) * 16, hc * 128:(hc + 1) * 128],
                          writes=[("wgb", b, hf)])
                    P.dma("pool", wub[:, b, hf * 16:(hf + 1) * 16, :], wuv[:, hf * 16:(hf + 1) * 16, hc * 128:(hc + 1) * 128],
                          writes=[("wub", b, hf)])
                for tt in range(2):
                    gbank = tt * 2
                    ubank = tt * 2 + 1
                    for (bank, wb, nm) in ((gbank, wgb, "wgb"), (ubank, wub, "wub")):
                        for kc in range(32):
                            P.op("pe", lambda h, bank=bank, wb=wb, b=b, kc=kc, tt=tt: h.matmul(
                                ps[bank][:, :], wb[:, b, kc, :], actT[:, kc, tt * 512:(tt + 1) * 512],
                                start=(kc == 0), stop=(kc == 31)),
                                reads=[(nm, b, kc // 16)] + [("actT", kc // 8, tt * 4 + q) for q in range(4)], writes=[("ps", bank)])
                    P.op("act", lambda h, tt=tt, gbank=gbank: h.activation(out=sg[:, tt, :], in_=ps[gbank][:, :], func=AF.Silu),
                         reads=[("ps", gbank)], writes=[("sg", tt)])
                    P.op("dve", lambda h, tt=tt, ubank=ubank, hl=hl: h.tensor_tensor(
                        out=hT[:, hl, tt * 512:(tt + 1) * 512], in0=ps[ubank][:, :], in1=sg[:, tt, :], op=ALU.mult),
                        reads=[("ps", ubank), ("sg", tt)], writes=[("hT", hl, tt)])
            groups = [(a, min(a + 4, nh)) for a in range(0, nh, 4)]
            for db in range(8):
                for (a0, a1) in groups:
                    wb_i = wdi % 3
                    wdi += 1
                    P.dma("pool", wdb[:, wb_i, 0:a1 - a0, :], wdv[:, h0 + a0:h0 + a1, db * 512:(db + 1) * 512], writes=[("wdb", wb_i)])
                    for hl in range(a0, a1):
                        for tq in range(NTQ):
                            P.op("pe", lambda h, tq=tq, hl=hl, wb_i=wb_i, a0=a0: h.matmul(
                                ps[tq][:, :], hT[:, hl, tq * 128:(tq + 1) * 128], wdb[:, wb_i, hl - a0, :],
                                start=(hl == 0), stop=(hl == nh - 1)),
                                reads=[("hT", hl, tq // 4), ("wdb", wb_i)], writes=[("ps", tq)])
                for tq in range(NTQ):
                    s = evi % 4
                    if evi % 2 == 0:
                        P.op("act", lambda h, s=s, tq=tq: h.activation(out=fst[:, s, :], in_=ps[tq][:, :], func=AF.Copy),
                             reads=[("ps", tq)], writes=[("fst", s)])
                    else:
                        P.op("dve", lambda h, s=s, tq=tq: h.tensor_copy(out=fst[:, s, :], in_=ps[tq][:, :]),
                             reads=[("ps", tq)], writes=[("fst", s)])
                    evi += 1
                    dst = scr["f"][tq * 128:(tq + 1) * 128, db * 512:(db + 1) * 512]
                    if pi == 0:
                        P.dma("sp", dst, fst[:, s, :], reads=[("fst", s)], writes=[("f", tq, db)])
                    else:
                        P.dma("pool", dst, fst[:, s, :], reads=[("fst", s)], writes=[("f", tq, db)], accum_op=ALU.add)
        P.flush()

    norm_pass([scr["f"]], scr["x1"], gains[2], gains[3], xout, False, hnT_out)


T = 8192
NCOL = 1798


def emit_A1(P, nc, ps, *, xT, wsl, g0col, gq, fq, rows, pfx="a1_"):
    TW = 256
    NT = T // TW
    with nc.sbuf_tensor(pfx + "W", [128, 32, NCOL], BF16) as Wbf, \
         nc.sbuf_tensor(pfx + "wst", [128, 2, NCOL], F32) as wst, \
         nc.sbuf_tensor(pfx + "gc", [128, 32], F32) as gcol, \
         nc.sbuf_tensor(pfx + "xt", [128, 2, 32, TW], BF16) as xt, \
         nc.sbuf_tensor(pfx + "sq", [128, 4, TW], BF16) as sq, \
         nc.sbuf_tensor(pfx + "ones", [128, 128], BF16) as ones, \
         nc.sbuf_tensor(pfx + "rb", [128, 2, TW], F32) as rb, \
         nc.sbuf_tensor(pfx + "stf", [128, 4, TW], F32) as stf, \
         nc.sbuf_tensor(pfx + "stb", [128, 4, TW], BF16) as stb:
        P.dma("sp", gcol[:], g0col, writes=["gcol"])
        P.op("pool", lambda h: h.memset(ones[:], 1.0), writes=["ones"])
        wv = wsl.rearrange("(k p) n -> p k n", p=128)
        for kc in range(32):
            b = kc % 2
            P.dma("sp", wst[:, b, :], wv[:, kc, :], writes=[("wst", b)])
            P.op("dve", lambda h, b=b, kc=kc: h.tensor_scalar(out=Wbf[:, kc, :], in0=wst[:, b, :], scalar1=gcol[:, kc:kc + 1], scalar2=None,
                                                             op0=ALU.mult),
                 reads=[("wst", b), "gcol"], writes=[("W", kc)])
        xv = xT.rearrange("(k p) t -> p k t", p=128)
        bankc = 0
        sti = 0
        for it in range(NT):
            xb = it % 2
            cols = slice(it * TW, (it + 1) * TW)
            for k4 in range(4):
                P.dma("pool", xt[:, xb, k4 * 8:(k4 + 1) * 8, :], xv[:, k4 * 8:(k4 + 1) * 8, cols], writes=[("xt", xb, k4)])
            for kc in range(32):
                s = kc % 4
                P.op("act", lambda h, s=s, kc=kc, xb=xb: h.activation(out=sq[:, s, :], in_=xt[:, xb, kc, :], func=AF.Square),
                     reads=[("xt", xb, kc // 8)], writes=[("sq", s)])
                P.op("pe", lambda h, s=s, kc=kc: h.matmul(ps[7][:, 0:TW], ones[:], sq[:, s, :], start=(kc == 0), stop=(kc == 31)),
                     reads=[("sq", s), "ones"], writes=[("ps", 7)])
            r = it % 2
            P.op("dve", lambda h, r=r: h.tensor_scalar(out=rb[:, r, :], in0=ps[7][:, 0:TW], scalar1=1.0 / D, scalar2=EPS, op0=ALU.mult, op1=ALU.add),
                 reads=[("ps", 7)], writes=[("rb", r)])
            P.op("act", lambda h, r=r: h.activation(out=rb[:, r, :], in_=rb[:, r, :], func=AF.Sqrt), reads=[("rb", r)], writes=[("rb", r)])
            P.op("dve", lambda h, r=r: h.reciprocal(out=rb[:, r, :], in_=rb[:, r, :]), reads=[("rb", r)], writes=[("rb", r)])
            for cg in range(15):
                bank = bankc % 7
                bankc += 1
                M = 128 if cg < 14 else 6
                for kc in range(32):
                    P.op("pe", lambda h, bank=bank, kc=kc, cg=cg, M=M, xb=xb: h.matmul(
                        ps[bank][0:M, 0:TW], Wbf[:, kc, cg * 128:cg * 128 + M], xt[:, xb, kc, :], start=(kc == 0), stop=(kc == 31)),
                        reads=[("W", kc), ("xt", xb, kc // 8)], writes=[("ps", bank)])
                s = sti % 4
                sti += 1
                if 8 <= cg < 14:
                    P.op("dve", lambda h, s=s, bank=bank, r=r: h.tensor_tensor(out=stb[:, s, :], in0=ps[bank][:, 0:TW], in1=rb[:, r, :], op=ALU.mult),
                         reads=[("ps", bank), ("rb", r)], writes=[("stb", s)])
                    P.dma("sp", fq[cg - 8, :, cols], stb[:, s, :], reads=[("stb", s)], writes=[("fq", cg, it)])
                else:
                    P.op("dve", lambda h, s=s, bank=bank, r=r, M=M: h.tensor_tensor(out=stf[0:M, s, :], in0=ps[bank][0:M, 0:TW], in1=rb[0:M, r, :],
                                                                                    op=ALU.mult),
                         reads=[("ps", bank), ("rb", r)], writes=[("stf", s)])
                    if cg < 8:
                        P.dma("sp", gq[cg, :, cols], stf[:, s, :], reads=[("stf", s)], writes=[("gq", cg, it)])
                    else:
                        P.dma("sp", rows[:, cols], stf[0:6, s, :], reads=[("stf", s)], writes=[("rows", it)])
        P.flush()


NKB = T // 128


def emit_fox_head(P, nc, ps, cst, *, qT_d, kT_d, vT_d, f_d, nfb_col, oT_d, pfx):
    scale = 128 ** -0.5
    ps_bf = [p.bitcast(BF16) for p in ps]
    with nc.sbuf_tensor(pfx + "qT", [128, T], BF16) as qT, \
         nc.sbuf_tensor(pfx + "kT", [128, T], BF16) as kT, \
         nc.sbuf_tensor(pfx + "va", [128, NKB, 130], BF16) as va, \
         nc.sbuf_tensor(pfx + "cq", [128, T], F32) as cq, \
         nc.sbuf_tensor(pfx + "ck", [128, NKB], F32) as ck, \
         nc.sbuf_tensor(pfx + "c1", [64, 128], F32) as c1, \
         nc.sbuf_tensor(pfx + "c2", [64, 128], F32) as c2, \
         nc.sbuf_tensor(pfx + "car", [64, 1], F32) as car, \
         nc.sbuf_tensor(pfx + "tmp", [128, 3, 512], F32) as tmp, \
         nc.sbuf_tensor(pfx + "pT", [128, 3, 512], BF16) as pT, \
         nc.sbuf_tensor(pfx + "on", [128, 2, 128], BF16) as on, \
         nc.sbuf_tensor(pfx + "rs", [128, 4], F32) as rs, \
         nc.sbuf_tensor(pfx + "ost", [128, 2, 512], BF16) as ost:
        for hf in range(2):
            sl = slice(hf * 4096, (hf + 1) * 4096)
            P.dma("sp", qT[:, sl], qT_d[:, sl], writes=[("qT", hf)])
            P.dma("sp", kT[:, sl], kT_d[:, sl], writes=[("kT", hf)])
        vT = cq.bitcast(BF16)
        P.dma("sp", vT[:, 0:T], vT_d, writes=["cq"])
        P.op("pool", lambda h: h.memset(va[:, :, 128:130], 1.0), writes=[("va1",)])
        for g in range(NKB // 4):
            bank = 4 + g % 4
            for j in range(4):
                kb = g * 4 + j
                P.op("pe", lambda h, bank=bank, j=j, kb=kb: h.transpose(out=ps_bf[bank][:, j * 128:(j + 1) * 128],
                                                                       in_=vT[:, kb * 128:(kb + 1) * 128], identity=cst["ident_bf"][:]),
                     reads=["cq", "ident_bf"], writes=[("ps", bank)])
            src = ps_bf[bank][:, 0:512].rearrange("p (a b) -> p a b", a=4)
            eng = "act" if g % 2 == 0 else "dve"
            if eng == "act":
                P.op("act", lambda h, g=g, src=src: h.activation(out=va[:, g * 4:(g + 1) * 4, 0:128], in_=src, func=AF.Copy),
                     reads=[("ps", bank)], writes=[("va", g)])
            else:
                P.op("dve", lambda h, g=g, src=src: h.tensor_copy(out=va[:, g * 4:(g + 1) * 4, 0:128], in_=src),
                     reads=[("ps", bank)], writes=[("va", g)])
        P.dma("sp", c1[:], f_d.rearrange("o (k p) -> (o k) p", p=128), writes=["c1"])
        P.op("act", lambda h: h.activation(out=c1[:], in_=c1[:], func=AF.Exp, scale=-1.0, bias=nfb_col), reads=["c1", "nfb"], writes=["c1"])
        P.op("act", lambda h: h.activation(out=c1[:], in_=c1[:], func=AF.Ln, scale=1.0, bias=1.0), reads=["c1"], writes=["c1"])
        P.op("dve", lambda h: h.tensor_tensor_scan(out=c2[:], data0=cst["ones_f"][0:64, :], data1=c1[:], initial=0.0, op0=ALU.mult, op1=ALU.add),
             reads=["c1", "ones_f"], writes=["c2"])
        P.op("pe", lambda h: h.matmul(ps[0][0:64, 0:1], cst["lt64"][:], c2[:, 127:128], start=True, stop=True),
             reads=["c2", "lt64"], writes=[("ps", 0)])
        P.op("dve", lambda h: h.tensor_copy(out=car[:], in_=ps[0][0:64, 0:1]), reads=[("ps", 0)], writes=["car"])
        P.op("dve", lambda h: h.tensor_scalar(out=c2[:], in0=c2[:], scalar1=car[:, 0:1], scalar2=None, op0=ALU.add), reads=["c2", "car"], writes=["c2"])
        P.op("pe", lambda h: h.matmul(ps[1][:, 0:64], c2[:], cst["ident_f"][0:64, 0:64], start=True, stop=True),
             reads=["c2", "ident_f"], writes=[("ps", 1)])
        P.op("dve", lambda h: h.tensor_copy(out=ck[:], in_=ps[1][:, 0:64]), reads=[("ps", 1)], writes=["ck"])
        for g in range(NKB // 4):
            bank = 4 + g % 4
            for j in range(4):
                kb = g * 4 + j
                P.op("pe", lambda h, bank=bank, j=j, kb=kb: h.matmul(ps[bank][:, j * 128:(j + 1) * 128],
                                                                    cst["ident_f"][0:64, kb:kb + 1].to_broadcast([64, 128]), c2[:],
                                                                    start=True, stop=True),
                     reads=["c2", "ident_f"], writes=[("ps", bank)])
            P.op("act", lambda h, g=g, bank=bank: h.activation(out=cq[:, g * 512:(g + 1) * 512], in_=ps[bank][:, :], func=AF.Copy, scale=-1.0),
                 reads=[("ps", bank)], writes=["cq"])
        it = 0
        for qt in range(T // 512):
            nkb = 4 * qt + 4
            for kb in range(nkb):
                j = kb - 4 * qt
                c0 = max(j, 0) * 128
                bank = 4 + it % 4
                s = it % 3
                it += 1
                q0 = qt * 512
                P.op("pe", lambda h, bank=bank, kb=kb, c0=c0, q0=q0: h.matmul(ps[bank][:, c0:512], kT[:, kb * 128:(kb + 1) * 128],
                                                                             qT[:, q0 + c0:q0 + 512], start=True, stop=True),
                     reads=[("kT", kb // 32), ("qT", qt // 8)], writes=[("ps", bank)])
                P.op("dve", lambda h, bank=bank, s=s, c0=c0, q0=q0: h.scalar_tensor_tensor(
                    out=tmp[:, s, c0:512], in0=ps[bank][:, c0:512], scalar=scale, in1=cq[:, q0 + c0:q0 + 512], op0=ALU.mult, op1=ALU.add),
                    reads=[("ps", bank), "cq"], writes=[("tmp", s)])
                if j >= 0:
                    P.op("pool", lambda h, s=s, c0=c0: h.tensor_tensor(out=tmp[:, s, c0:c0 + 128], in0=tmp[:, s, c0:c0 + 128],
                                                                       in1=cst["negmask"][:], op=ALU.add),
                         reads=[("tmp", s), "negmask"], writes=[("tmp", s)])
                P.op("act", lambda h, s=s, c0=c0, kb=kb: h.activation(out=pT[:, s, c0:512], in_=tmp[:, s, c0:512], func=AF.Exp,
                                                                       bias=ck[:, kb:kb + 1], scale=1.0),
                     reads=[("tmp", s), "ck"], writes=[("pT", s)])
                for i in range(max(j, 0), 4):
                    P.op("pe", lambda h, i=i, s=s, kb=kb, qt=qt: h.matmul(ps[i][:, 0:129], pT[:, s, i * 128:(i + 1) * 128], va[:, kb, 0:129],
                                                                          start=(kb == 0), stop=(kb == 4 * qt + i)),
                         reads=[("pT", s), ("va", kb // 4), ("va1",)], writes=[("ps", i)])
            ob = qt % 2
            for i in range(4):
                P.op("dve", lambda h, i=i: h.reciprocal(out=rs[:, i:i + 1], in_=ps[i][:, 128:129]), reads=[("ps", i)], writes=[("rs", i)])
                o2 = i % 2
                P.op("act", lambda h, i=i, o2=o2: h.activation(out=on[:, o2, :], in_=ps[i][:, 0:128], func=AF.Copy, scale=rs[:, i:i + 1]),
                     reads=[("ps", i), ("rs", i)], writes=[("on", o2)])
                P.op("pe", lambda h, i=i, o2=o2: h.transpose(out=ps_bf[i][:, 512:640], in_=on[:, o2, :], identity=cst["ident_bf"][:]),
                     reads=[("on", o2), "ident_bf"], writes=[("ps", i)])
                P.op("dve", lambda h, i=i, ob=ob: h.tensor_copy(out=ost[:, ob, i * 128:(i + 1) * 128], in_=ps_bf[i][:, 512:640]),
                     reads=[("ps", i)], writes=[("ost", ob)])
            P.dma("sp", oT_d[:, qt * 512:(qt + 1) * 512], ost[:, ob, :], reads=[("ost", ob)], writes=[("oT", qt)])
        P.flush()


SEG = 1024
NSEG = T // SEG
NM = SEG // 128
L2_EPS = 1e-6
RMS_EPS = 1e-6


def OP(P, eng, method, reads, writes, *args, **kw):
    return P.op(eng, lambda h: getattr(h, method)(*args, **kw), reads, writes)


def emit_gdn_head(P, nc, ps, cst, *, gq_d, a_d, b_d, gpar, oT_d, pfx):
    ps_bf = [p.bitcast(BF16) for p in ps]
    bankc = [0]

    def nb():
        b = bankc[0] % 8
        bankc[0] += 1
        return b

    es = contextlib.ExitStack()

    def sb(name, shape, dt):
        return es.enter_context(nc.sbuf_tensor(pfx + name, shape, dt))

    a_sc = sb("a_sc", [64, 128], F32); b_sc = sb("b_sc", [64, 128], F32)
    g_sc = sb("g_sc", [64, 128], F32); gc_sc = sb("gc_sc", [64, 128], F32)
    t_sc = sb("t_sc", [64, 128], F32); t2_sc = sb("t2_sc", [64, 128], F32)
    bg_sc = sb("bg_sc", [64, 128], F32); ekd_sc = sb("ekd_sc", [64, 128], F32)
    egl_sc = sb("egl_sc", [64, 2], F32); nea = sb("nea", [128, 1], F32)
    gccol = sb("gccol", [128, 64], F32); nbcol = sb("nbcol", [128, 64], F32); bcol = sb("bcol", [128, 64], F32)
    bgcol = sb("bgcol", [128, 64], F32); ekdcol = sb("ekdcol", [128, 64], F32)
    eglb = sb("eglb", [128, 128], F32)
    S32 = sb("S32", [128, 128], F32); Sbf = sb("Sbf", [128, 128], BF16)
    xr = [sb("xr%d" % i, [128, SEG + 3], F32) for i in range(3)]
    cv = [sb("cv%d" % i, [128, SEG], F32) for i in range(3)]
    sqt = sb("sqt", [128, SEG], F32); rinv = sb("rinv", [128, SEG], F32)
    qnT = sb("qnT", [128, SEG], BF16); knT = sb("knT", [128, SEG], BF16); vT = sb("vT", [128, SEG], BF16); qgT = sb("qgT", [128, SEG], BF16)
    gcb = sb("gcb", [128, SEG], F32); nbb = sb("nbb", [128, SEG], F32); egb = sb("egb", [128, SEG], F32)
    dif = sb("dif", [128, SEG], F32); e1 = sb("e1", [128, SEG], F32); e2 = sb("e2", [128, SEG], F32)
    dmti = sb("dmti", [128, SEG], F32); dmtb = sb("dmtb", [128, SEG], F32)
    Y = [sb("Y%d" % i, [128, SEG], BF16) for i in range(2)]
    YT = [sb("YT%d" % i, [128, SEG], BF16) for i in range(2)]
    P32 = sb("P32", [128, SEG], F32); Pbf = [sb("Pbf%d" % i, [128, SEG], BF16) for i in range(2)]
    attnT = sb("attnT", [128, SEG], BF16)
    kbg = sb("kbg", [128, NM, 128], BF16); kd = sb("kd", [128, NM, 128], BF16); vb = sb("vb", [128, NM, 128], BF16)
    u_sb = sb("u_sb", [128, NM, 128], F32); wT = sb("wT", [128, SEG], BF16)
    vn = sb("vn", [128, 2, 128], BF16)
    oT_sb = sb("oT_sb", [128, SEG], F32); zt = sb("zt", [128, SEG], F32); gout = sb("gout", [128, SEG], BF16)

    ident_f = cst["ident_f"]; ident_bf = cst["ident_bf"]; ones_f = cst["ones_f"]

    def v3(ap):
        return ap.rearrange("p (m j) -> p m j", m=NM)

    def colb(col, m0):
        return col[:, m0:m0 + NM].unsqueeze(2).to_broadcast([128, NM, 128])

    def maskb(mk):
        return mk.unsqueeze(1).to_broadcast([128, NM, 128])

    P.dma("sp", a_sc[:], a_d.rearrange("o (k p) -> (o k) p", p=128), writes=["a_sc"])
    P.dma("sp", b_sc[:], b_d.rearrange("o (k p) -> (o k) p", p=128), writes=["b_sc"])
    OP(P, "act", "activation", ["gpar"], ["nea"], out=nea[:], in_=gpar[:, 13:14], func=AF.Exp)
    OP(P, "dve", "tensor_scalar", ["nea"], ["nea"], out=nea[:], in0=nea[:], scalar1=-1.0, scalar2=None, op0=ALU.mult)
    OP(P, "act", "activation", ["a_sc", "gpar"], ["t_sc"], out=t_sc[:], in_=a_sc[:], func=AF.Exp, bias=gpar[0:64, 14:15], scale=1.0)
    OP(P, "act", "activation", ["t_sc"], ["t_sc"], out=t_sc[:], in_=t_sc[:], func=AF.Ln, bias=1.0, scale=1.0)
    OP(P, "dve", "tensor_scalar", ["t_sc", "nea"], ["g_sc"], out=g_sc[:], in0=t_sc[:], scalar1=nea[0:64, 0:1], scalar2=None, op0=ALU.mult)
    OP(P, "dve", "tensor_tensor_scan", ["g_sc", "rst"], ["gc_sc"], out=gc_sc[:], data0=cst["rst"], data1=g_sc[:], initial=0.0,
       op0=ALU.mult, op1=ALU.add)
    OP(P, "act", "activation", ["b_sc"], ["b_sc"], out=b_sc[:], in_=b_sc[:], func=AF.Sigmoid)
    OP(P, "act", "activation", ["gc_sc"], ["t_sc"], out=t_sc[:], in_=gc_sc[:], func=AF.Exp)
    OP(P, "dve", "tensor_tensor", ["t_sc", "b_sc"], ["bg_sc"], out=bg_sc[:], in0=t_sc[:], in1=b_sc[:], op=ALU.mult)
    gc3 = gc_sc[:].rearrange("p (a b) -> p a b", a=2)
    OP(P, "dve", "tensor_tensor", ["gc_sc"], ["t2_sc"], out=t2_sc[:].rearrange("p (a b) -> p a b", a=2),
       in0=gc3[:, :, 63:64].to_broadcast([64, 2, 64]), in1=gc3, op=ALU.subtract)
    OP(P, "act", "activation", ["t2_sc"], ["ekd_sc"], out=ekd_sc[:], in_=t2_sc[:], func=AF.Exp)
    OP(P, "act", "activation", ["gc_sc"], ["egl_sc"], out=egl_sc[:].unsqueeze(2), in_=gc3[:, :, 63:64], func=AF.Exp)
    for (src, srcn, dst, nm, sc) in ((gc_sc, "gc_sc", gccol, "gccol", 1.0), (b_sc, "b_sc", nbcol, "nbcol", -1.0), (b_sc, "b_sc", bcol, "bcol", 1.0),
                                     (bg_sc, "bg_sc", bgcol, "bgcol", 1.0), (ekd_sc, "ekd_sc", ekdcol, "ekdcol", 1.0)):
        bk = nb()
        OP(P, "pe", "matmul", [srcn, "ident_f"], [("ps", bk)], ps[bk][:, 0:64], src[:], ident_f[0:64, 0:64],
           start=True, stop=True)
        OP(P, "act", "activation", [("ps", bk)], [nm], out=dst[:], in_=ps[bk][:, 0:64], func=AF.Copy, scale=sc)
    bk = nb()
    for m in range(64):
        OP(P, "pe", "matmul", ["egl_sc", "ident_f"], [("ps", bk)], ps[bk][:, 2 * m:2 * m + 2], ident_f[0:64, m:m + 1].to_broadcast([64, 128]),
           egl_sc[:], start=True, stop=True)
    OP(P, "dve", "tensor_copy", [("ps", bk)], ["eglb"], out=eglb[:], in_=ps[bk][:, 0:128])
    OP(P, "dve", "memset", [], ["S32"], S32[:], 0.0)
    OP(P, "pool", "memset", [], ["Sbf"], Sbf[:], 0.0)

    names3 = ["q", "k", "v"]
    for sg in range(NSEG):
        t0 = sg * SEG
        m0 = sg * NM
        for i in range(3):
            if sg == 0:
                OP(P, "pool", "memset", [], [("xr", i)], xr[i][:, 0:3], 0.0)
                P.dma("sp", xr[i][:, 3:SEG + 3], gq_d[i, :, 0:SEG], writes=[("xr", i)])
            else:
                P.dma("sp", xr[i][:, :], gq_d[i, :, t0 - 3:t0 + SEG], writes=[("xr", i)])
            OP(P, "act", "activation", [("xr", i), "gpar"], [("cv", i)], out=cv[i][:], in_=xr[i][:, 3:SEG + 3], func=AF.Copy,
               scale=gpar[:, 4 * i + 3:4 * i + 4])
            for j in (2, 1, 0):
                OP(P, "dve", "scalar_tensor_tensor", [("xr", i), ("cv", i), "gpar"], [("cv", i)], out=cv[i][:], in0=xr[i][:, j:j + SEG],
                   scalar=gpar[:, 4 * i + j:4 * i + j + 1], in1=cv[i][:], op0=ALU.mult, op1=ALU.add)
            OP(P, "act", "activation", [("cv", i)], [("cv", i)], out=cv[i][:], in_=cv[i][:], func=AF.Silu)
        P.dma("sp", zt[:], gq_d[3, :, t0:t0 + SEG], writes=["zt"])
        OP(P, "pool", "tensor_copy", [("cv", 2)], ["vT"], out=vT[:], in_=cv[2][:])
        for (src, srcn, dst, dstn, sc) in ((gc_sc, "gc_sc", gcb, "gcb", 1.0), (b_sc, "b_sc", nbb, "nbb", -1.0)):
            for hf in range(NM // 4):
                bk = nb()
                for j in range(4):
                    m = m0 + hf * 4 + j
                    OP(P, "pe", "matmul", [srcn, "ident_f"], [("ps", bk)], ps[bk][:, j * 128:(j + 1) * 128],
                       ident_f[0:64, m:m + 1].to_broadcast([64, 128]), src[:], start=True, stop=True)
                OP(P, "act", "activation", [("ps", bk)], [dstn], out=dst[:, hf * 512:(hf + 1) * 512], in_=ps[bk][:, :], func=AF.Copy, scale=sc)
        OP(P, "act", "activation", ["gcb"], ["egb"], out=egb[:], in_=gcb[:], func=AF.Exp)
        for i in range(2):
            OP(P, "act", "activation", [("cv", i)], ["sqt"], out=sqt[:], in_=cv[i][:], func=AF.Square)
            for hf in range(SEG // 512):
                bk = nb()
                OP(P, "pe", "matmul", ["sqt", "ones_f"], [("ps", bk)], ps[bk][:, :], ones_f, sqt[:, hf * 512:(hf + 1) * 512], start=True, stop=True)
                OP(P, "dve", "tensor_scalar", [("ps", bk)], ["rinv"], out=rinv[:, hf * 512:(hf + 1) * 512], in0=ps[bk][:, :], scalar1=L2_EPS,
                   scalar2=None, op0=ALU.add)
            OP(P, "act", "activation", ["rinv"], ["rinv"], out=rinv[:], in_=rinv[:], func=AF.Sqrt)
            OP(P, "dve", "reciprocal", ["rinv"], ["rinv"], out=rinv[:], in_=rinv[:])
            if i == 0:
                OP(P, "dve", "scalar_tensor_tensor", [("cv", 0), "rinv"], ["qnT"], out=qnT[:], in0=cv[0][:], scalar=128 ** -0.5, in1=rinv[:],
                   op0=ALU.mult, op1=ALU.mult)
                OP(P, "dve", "tensor_tensor", ["qnT", "egb"], ["qgT"], out=qgT[:], in0=qnT[:], in1=egb[:], op=ALU.mult)
            else:
                OP(P, "dve", "tensor_tensor", [("cv", 1), "rinv"], ["knT"], out=knT[:], in0=cv[1][:], in1=rinv[:], op=ALU.mult)
        OP(P, "dve", "tensor_tensor", ["gccol", "gcb"], ["dif"], out=v3(dif[:]), in0=colb(gccol, m0), in1=v3(gcb[:]), op=ALU.subtract)
        OP(P, "dve", "tensor_scalar", ["dif"], ["e1"], out=e1[:], in0=dif[:], scalar1=0.0, scalar2=None, op0=ALU.min)
        OP(P, "act", "activation", ["e1"], ["e1"], out=e1[:], in_=e1[:], func=AF.Exp)
        OP(P, "pool", "tensor_tensor", ["e1", "maskL"], ["e1"], out=v3(e1[:]), in0=v3(e1[:]), in1=maskb(cst["maskL"]), op=ALU.mult)
        OP(P, "dve", "tensor_tensor", ["e1", "nbcol"], ["e1"], out=v3(e1[:]), in0=v3(e1[:]), in1=colb(nbcol, m0), op=ALU.mult)
        OP(P, "dve", "tensor_scalar", ["dif"], ["e2"], out=e2[:], in0=dif[:], scalar1=0.0, scalar2=None, op0=ALU.max)
        OP(P, "act", "activation", ["e2"], ["e2"], out=e2[:], in_=e2[:], func=AF.Exp, scale=-1.0)
        OP(P, "pool", "tensor_tensor", ["e2", "maskUi"], ["dmti"], out=v3(dmti[:]), in0=v3(e2[:]), in1=maskb(cst["maskUi"]), op=ALU.mult)
        OP(P, "pool", "tensor_tensor", ["nbb", "maskUs"], ["dmtb"], out=v3(dmtb[:]), in0=v3(nbb[:]), in1=maskb(cst["maskUs"]), op=ALU.mult)
        OP(P, "dve", "tensor_tensor", ["dmtb", "e2"], ["dmtb"], out=dmtb[:], in0=dmtb[:], in1=e2[:], op=ALU.mult)
        for hf in range(NM // 4):
            bkk = nb(); bqk = nb()
            for j in range(4):
                m = hf * 4 + j
                sl = slice(m * 128, (m + 1) * 128)
                OP(P, "pe", "matmul", ["knT"], [("ps", bkk)], ps[bkk][:, j * 128:(j + 1) * 128], knT[:, sl], knT[:, sl], start=True, stop=True)
                OP(P, "pe", "matmul", ["knT", "qnT"], [("ps", bqk)], ps[bqk][:, j * 128:(j + 1) * 128], knT[:, sl], qnT[:, sl], start=True, stop=True)
            hs = slice(hf * 512, (hf + 1) * 512)
            OP(P, "dve", "tensor_tensor", [("ps", bkk), "e1"], [("Y", 0)], out=Y[0][:, hs], in0=ps[bkk][:, :], in1=e1[:, hs], op=ALU.mult)
            OP(P, "dve", "tensor_tensor", [("ps", bkk), "dmtb"], [("YT", 0)], out=YT[0][:, hs], in0=ps[bkk][:, :], in1=dmtb[:, hs], op=ALU.mult)
            OP(P, "dve", "tensor_tensor", [("ps", bqk), "dmti"], ["attnT"], out=attnT[:, hs], in0=ps[bqk][:, :], in1=dmti[:, hs], op=ALU.mult)
        for hf in range(NM // 4):
            bk_k = nb(); bk_v = nb()
            for j in range(4):
                m = hf * 4 + j
                sl = slice(m * 128, (m + 1) * 128)
                OP(P, "pe", "transpose", ["knT", "ident_bf"], [("ps", bk_k)], out=ps_bf[bk_k][:, j * 128:(j + 1) * 128], in_=knT[:, sl], identity=ident_bf)
                OP(P, "pe", "transpose", ["vT", "ident_bf"], [("ps", bk_v)], out=ps_bf[bk_v][:, j * 128:(j + 1) * 128], in_=vT[:, sl], identity=ident_bf)
            ms = slice(hf * 4, hf * 4 + 4)
            srck = ps_bf[bk_k][:, 0:512].rearrange("p (a b) -> p a b", a=4)
            srcv = ps_bf[bk_v][:, 0:512].rearrange("p (a b) -> p a b", a=4)

            def cb4(col):
                return col[:, m0 + hf * 4:m0 + hf * 4 + 4].unsqueeze(2).to_broadcast([128, 4, 128])
            OP(P, "dve", "tensor_tensor", [("ps", bk_k), "bgcol"], ["kbg"], out=kbg[:, ms, :], in0=srck, in1=cb4(bgcol), op=ALU.mult)
            OP(P, "dve", "tensor_tensor", [("ps", bk_k), "ekdcol"], ["kd"], out=kd[:, ms, :], in0=srck, in1=cb4(ekdcol), op=ALU.mult)
            OP(P, "dve", "tensor_tensor", [("ps", bk_v), "bcol"], ["vb"], out=vb[:, ms, :], in0=srcv, in1=cb4(bcol), op=ALU.mult)
        OP(P, "dve", "tensor_tensor", [("YT", 0), "ident_f"], ["P32"], out=v3(P32[:]), in0=v3(YT[0][:]), in1=maskb(ident_f), op=ALU.add)
        OP(P, "act", "activation", ["P32"], [("Pbf", 0)], out=Pbf[0][:], in_=P32[:], func=AF.Copy)
        pcur = 0
        for lvl in range(1, 6):
            cu = (lvl - 1) % 2
            nx = lvl % 2
            for hf in range(NM // 4):
                hs = slice(hf * 512, (hf + 1) * 512)
                by = nb()
                for j in range(4):
                    sl = slice((hf * 4 + j) * 128, (hf * 4 + j + 1) * 128)
                    OP(P, "pe", "matmul", [("YT", cu), ("Y", cu)], [("ps", by)], ps[by][:, j * 128:(j + 1) * 128], YT[cu][:, sl], Y[cu][:, sl],
                       start=True, stop=True)
                OP(P, "act", "activation", [("ps", by)], [("Y", nx)], out=Y[nx][:, hs], in_=ps[by][:, :], func=AF.Copy)
                if lvl < 5:
                    byt = nb()
                    for j in range(4):
                        sl = slice((hf * 4 + j) * 128, (hf * 4 + j + 1) * 128)
                        OP(P, "pe", "matmul", [("YT", cu), ("Y", cu)], [("ps", byt)], ps[byt][:, j * 128:(j + 1) * 128], Y[cu][:, sl], YT[cu][:, sl],
                           start=True, stop=True)
                    OP(P, "dve", "tensor_copy", [("ps", byt)], [("YT", nx)], out=YT[nx][:, hs], in_=ps[byt][:, :])
            for hf in range(NM // 4):
                hs = slice(hf * 512, (hf + 1) * 512)
                bp = nb()
                for j in range(4):
                    sl = slice((hf * 4 + j) * 128, (hf * 4 + j + 1) * 128)
                    OP(P, "pe", "matmul", [("Y", nx), ("Pbf", pcur)], [("ps", bp)], ps[bp][:, j * 128:(j + 1) * 128], Y[nx][:, sl], Pbf[pcur][:, sl],
                       start=True, stop=True)
                OP(P, "dve", "tensor_tensor", [("ps", bp), "P32"], ["P32"], out=P32[:, hs], in0=ps[bp][:, :], in1=P32[:, hs], op=ALU.add)
            pn = 1 - pcur
            OP(P, "act", "activation", ["P32"], [("Pbf", pn)], out=Pbf[pn][:], in_=P32[:], func=AF.Copy)
            pcur = pn
        TT = Pbf[pcur]
        ttn = ("Pbf", pcur)
        for hf in range(NM // 4):
            bu = nb(); bw = nb()
            for j in range(4):
                m = hf * 4 + j
                sl = slice(m * 128, (m + 1) * 128)
                OP(P, "pe", "matmul", [ttn, "vb"], [("ps", bu)], ps[bu][:, j * 128:(j + 1) * 128], TT[:, sl], vb[:, m, :], start=True, stop=True)
                OP(P, "pe", "matmul", [ttn, "kbg"], [("ps", bw)], ps[bw][:, j * 128:(j + 1) * 128], kbg[:, m, :], TT[:, sl], start=True, stop=True)
            OP(P, "act", "activation", [("ps", bu)], ["u_sb"], out=u_sb[:, hf * 4:hf * 4 + 4, :],
               in_=ps[bu][:, :].rearrange("p (a b) -> p a b", a=4), func=AF.Copy)
            OP(P, "dve", "tensor_copy", [("ps", bw)], ["wT"], out=wT[:, hf * 512:(hf + 1) * 512], in_=ps[bw][:, :])
        for half in range(2):
            bo = nb()
            for cl in range(8):
                cc = half * 8 + cl
                m = cc // 2
                hh = cc % 2
                R = slice(64 * hh, 64 * hh + 64)
                C = slice(m * 128 + 64 * hh, m * 128 + 64 * hh + 64)
                cg = sg * 16 + cc
                bws = nb()
                while bws == bo:
                    bws = nb()
                OP(P, "pe", "matmul", ["wT", "Sbf"], [("ps", bws)], ps[bws][:, 0:128], wT[:, m * 128:(m + 1) * 128], Sbf[:], start=True, stop=True)
                vi = cc % 2
                OP(P, "dve", "tensor_tensor", ["u_sb", ("ps", bws)], [("vn", vi)], out=vn[R, vi, :], in0=u_sb[R, m, :], in1=ps[bws][R, 0:128],
                   op=ALU.subtract)
                OP(P, "pe", "matmul", ["Sbf", "qgT"], [("ps", bo)], ps[bo][:, cl * 64:(cl + 1) * 64], Sbf[:], qgT[:, C], start=True, stop=False)
                OP(P, "pe", "matmul", [("vn", vi), "attnT"], [("ps", bo)], ps[bo][:, cl * 64:(cl + 1) * 64], vn[R, vi, :], attnT[R, C],
                   start=False, stop=True)
                bsd = nb()
                while bsd == bo:
                    bsd = nb()
                OP(P, "pe", "matmul", ["kd", ("vn", vi)], [("ps", bsd)], ps[bsd][:, 0:128], kd[R, m, :], vn[R, vi, :], start=True, stop=True)
                OP(P, "dve", "scalar_tensor_tensor", ["S32", "eglb", ("ps", bsd)], ["S32"], out=S32[:], in0=S32[:], scalar=eglb[:, cg:cg + 1],
                   in1=ps[bsd][:, 0:128], op0=ALU.mult, op1=ALU.add)
                OP(P, "act", "activation", ["S32"], ["Sbf"], out=Sbf[:], in_=S32[:], func=AF.Copy)
            OP(P, "act", "activation", [("ps", bo)], ["oT_sb"], out=oT_sb[:, half * 512:(half + 1) * 512], in_=ps[bo][:, :], func=AF.Copy)
        OP(P, "act", "activation", ["oT_sb"], ["sqt"], out=sqt[:], in_=oT_sb[:], func=AF.Square)
        for hf in range(SEG // 512):
            bk = nb()
            OP(P, "pe", "matmul", ["sqt", "ones_f"], [("ps", bk)], ps[bk][:, :], ones_f, sqt[:, hf * 512:(hf + 1) * 512], start=True, stop=True)
            OP(P, "dve", "tensor_scalar", [("ps", bk)], ["rinv"], out=rinv[:, hf * 512:(hf + 1) * 512], in0=ps[bk][:, :], scalar1=1.0 / 128,
               scalar2=RMS_EPS, op0=ALU.mult, op1=ALU.add)
        OP(P, "act", "activation", ["rinv"], ["rinv"], out=rinv[:], in_=rinv[:], func=AF.Sqrt)
        OP(P, "dve", "reciprocal", ["rinv"], ["rinv"], out=rinv[:], in_=rinv[:])
        OP(P, "dve", "scalar_tensor_tensor", ["oT_sb", "rinv", "gpar"], ["oT_sb"], out=oT_sb[:], in0=oT_sb[:], scalar=gpar[:, 12:13], in1=rinv[:],
           op0=ALU.mult, op1=ALU.mult)
        OP(P, "act", "activation", ["zt"], ["zt"], out=zt[:], in_=zt[:], func=AF.Silu)
        OP(P, "dve", "tensor_tensor", ["oT_sb", "zt"], ["gout"], out=gout[:], in0=oT_sb[:], in1=zt[:], op=ALU.mult)
        P.dma("sp", oT_d[:, t0:t0 + SEG], gout[:], reads=["gout"], writes=[("oTd", sg)])
    P.flush()
    es.close()


LB = 512
NTB = T // LB
S5_MAX_RE = -1e-4
USE_GELU_TANH_AF = False


def OP(P, eng, method, reads, writes, *args, **kw):
    return P.op(eng, lambda h: getattr(h, method)(*args, **kw), reads, writes)


def emit_s5(P, nc, ps, cst, *, uT_d, apar_d, b_d, c_d, dcol_d, yT_d, pfx="s5_"):
    es = contextlib.ExitStack()
    uid = [0]

    def sb(name, shape, dt):
        return es.enter_context(nc.sbuf_tensor(pfx + name, shape, dt))

    def small(name):
        return sb(name, [128, 16], F32)

    bankc = [0]

    def nb():
        b = bankc[0] % 8
        bankc[0] += 1
        return b

    ident_f = cst["ident_f"]
    apar = sb("apar", [128, 16, 3], F32)
    Bt = sb("Bt", [128, 2, 16, 16], F32)
    Ct = sb("Ct", [128, 2, 16, 16], F32)
    dcol = sb("dcol", [128, 4], F32)
    P.dma("sp", apar[:], apar_d, writes=["apar"])
    P.dma("sp", Bt[:], b_d, writes=["Bt"])
    P.dma("sp", Ct[:], c_d, writes=["Ct"])
    P.dma("sp", dcol[:], dcol_d, writes=["dcol"])

    lre = small("lre"); lim = small("lim"); dt = small("dt"); rho = small("rho"); th = small("th")
    c1 = small("c1"); s1 = small("s1"); t1 = small("t1"); t2 = small("t2"); t3 = small("t3")
    fre = small("fre"); fim = small("fim"); cc = small("cc"); ss = small("ss")
    halfpi = sb("halfpi", [128, 1], F32)
    OP(P, "dve", "memset", [], ["halfpi"], halfpi[:], math.pi / 2)
    OP(P, "dve", "tensor_scalar", ["apar"], ["lre"], out=lre[:], in0=apar[:, :, 0], scalar1=S5_MAX_RE, scalar2=None, op0=ALU.min)
    OP(P, "dve", "tensor_copy", ["apar"], ["lim"], out=lim[:], in_=apar[:, :, 1])
    OP(P, "act", "activation", ["apar"], ["dt"], out=dt[:], in_=apar[:, :, 2], func=AF.Exp)
    OP(P, "dve", "tensor_tensor", ["lre", "dt"], ["t1"], out=t1[:], in0=lre[:], in1=dt[:], op=ALU.mult)
    OP(P, "act", "activation", ["t1"], ["rho"], out=rho[:], in_=t1[:], func=AF.Exp)
    OP(P, "dve", "tensor_tensor", ["lim", "dt"], ["th"], out=th[:], in0=lim[:], in1=dt[:], op=ALU.mult)
    OP(P, "dve", "tensor_scalar", ["th"], ["t1"], out=t1[:], in0=th[:], scalar1=1.0 / (2 * math.pi), scalar2=12582912.0, op0=ALU.mult, op1=ALU.add)
    OP(P, "dve", "tensor_scalar", ["t1"], ["t1"], out=t1[:], in0=t1[:], scalar1=-12582912.0, scalar2=None, op0=ALU.add)
    C1 = 6.28125
    C2 = 2 * math.pi - C1
    OP(P, "dve", "scalar_tensor_tensor", ["t1", "th"], ["t2"], out=t2[:], in0=t1[:], scalar=-C1, in1=th[:], op0=ALU.mult, op1=ALU.add)
    OP(P, "dve", "scalar_tensor_tensor", ["t1", "t2"], ["t2"], out=t2[:], in0=t1[:], scalar=-C2, in1=t2[:], op0=ALU.mult, op1=ALU.add)
    OP(P, "dve", "tensor_scalar", ["t2"], ["t2"], out=t2[:], in0=t2[:], scalar1=math.pi, scalar2=-math.pi, op0=ALU.min, op1=ALU.max)
    OP(P, "act", "activation", ["t2"], ["s1"], out=s1[:], in_=t2[:], func=AF.Sin)
    OP(P, "dve", "tensor_scalar", ["t2"], ["t3"], out=t3[:], in0=t2[:], scalar1=-1.0, scalar2=None, op0=ALU.mult)
    OP(P, "dve", "tensor_tensor", ["t2", "t3"], ["t3"], out=t3[:], in0=t3[:], in1=t2[:], op=ALU.max)
    OP(P, "act", "activation", ["t3", "halfpi"], ["c1"], out=c1[:], in_=t3[:], func=AF.Sin, scale=-1.0, bias=halfpi[:, 0:1])
    OP(P, "dve", "tensor_tensor", ["rho", "c1"], ["t1"], out=t1[:], in0=rho[:], in1=c1[:], op=ALU.mult)
    OP(P, "dve", "tensor_scalar", ["t1"], ["t1"], out=t1[:], in0=t1[:], scalar1=-1.0, scalar2=None, op0=ALU.add)
    OP(P, "dve", "tensor_tensor", ["rho", "s1"], ["t2"], out=t2[:], in0=rho[:], in1=s1[:], op=ALU.mult)
    OP(P, "dve", "tensor_tensor", ["lre"], ["t3"], out=t3[:], in0=lre[:], in1=lre[:], op=ALU.mult)
    OP(P, "dve", "tensor_tensor", ["lim"], ["cc"], out=cc[:], in0=lim[:], in1=lim[:], op=ALU.mult)
    OP(P, "dve", "tensor_tensor", ["t3", "cc"], ["t3"], out=t3[:], in0=t3[:], in1=cc[:], op=ALU.add)
    OP(P, "dve", "reciprocal", ["t3"], ["t3"], out=t3[:], in_=t3[:])
    OP(P, "dve", "tensor_tensor", ["t1", "lre"], ["fre"], out=fre[:], in0=t1[:], in1=lre[:], op=ALU.mult)
    OP(P, "dve", "tensor_tensor", ["t2", "lim"], ["cc"], out=cc[:], in0=t2[:], in1=lim[:], op=ALU.mult)
    OP(P, "dve", "tensor_tensor", ["fre", "cc"], ["fre"], out=fre[:], in0=fre[:], in1=cc[:], op=ALU.add)
    OP(P, "dve", "tensor_tensor", ["fre", "t3"], ["fre"], out=fre[:], in0=fre[:], in1=t3[:], op=ALU.mult)
    OP(P, "dve", "tensor_tensor", ["t2", "lre"], ["fim"], out=fim[:], in0=t2[:], in1=lre[:], op=ALU.mult)
    OP(P, "dve", "tensor_tensor", ["t1", "lim"], ["cc"], out=cc[:], in0=t1[:], in1=lim[:], op=ALU.mult)
    OP(P, "dve", "tensor_tensor", ["fim", "cc"], ["fim"], out=fim[:], in0=fim[:], in1=cc[:], op=ALU.subtract)
    OP(P, "dve", "tensor_tensor", ["fim", "t3"], ["fim"], out=fim[:], in0=fim[:], in1=t3[:], op=ALU.mult)

    def bc16(x):
        return x[:].unsqueeze(2).to_broadcast([128, 16, 16])

    bbr = sb("bbr", [128, 16, 16], F32); bbi = sb("bbi", [128, 16, 16], F32); tq = sb("tq", [128, 16, 16], F32)
    OP(P, "dve", "tensor_tensor", ["Bt", "fre"], ["bbr"], out=bbr[:], in0=Bt[:, 0], in1=bc16(fre), op=ALU.mult)
    OP(P, "dve", "tensor_tensor", ["Bt", "fim"], ["tq"], out=tq[:], in0=Bt[:, 1], in1=bc16(fim), op=ALU.mult)
    OP(P, "dve", "tensor_tensor", ["bbr", "tq"], ["bbr"], out=bbr[:], in0=bbr[:], in1=tq[:], op=ALU.subtract)
    OP(P, "dve", "tensor_tensor", ["Bt", "fre"], ["bbi"], out=bbi[:], in0=Bt[:, 1], in1=bc16(fre), op=ALU.mult)
    OP(P, "dve", "tensor_tensor", ["Bt", "fim"], ["tq"], out=tq[:], in0=Bt[:, 0], in1=bc16(fim), op=ALU.mult)
    OP(P, "dve", "tensor_tensor", ["bbi", "tq"], ["bbi"], out=bbi[:], in0=bbi[:], in1=tq[:], op=ALU.add)
    ccr = sb("ccr", [128, 16, 16], F32); cci = sb("cci", [128, 16, 16], F32)
    OP(P, "dve", "tensor_tensor", ["Ct", "c1"], ["ccr"], out=ccr[:], in0=Ct[:, 0], in1=bc16(c1), op=ALU.mult)
    OP(P, "dve", "tensor_tensor", ["Ct", "s1"], ["tq"], out=tq[:], in0=Ct[:, 1], in1=bc16(s1), op=ALU.mult)
    OP(P, "dve", "tensor_tensor", ["ccr", "tq"], ["ccr"], out=ccr[:], in0=ccr[:], in1=tq[:], op=ALU.add)
    OP(P, "dve", "tensor_tensor", ["Ct", "s1"], ["cci"], out=cci[:], in0=Ct[:, 0], in1=bc16(s1), op=ALU.mult)
    OP(P, "dve", "tensor_tensor", ["Ct", "c1"], ["tq"], out=tq[:], in0=Ct[:, 1], in1=bc16(c1), op=ALU.mult)
    OP(P, "dve", "tensor_tensor", ["cci", "tq"], ["cci"], out=cci[:], in0=cci[:], in1=tq[:], op=ALU.subtract)

    Mre = sb("Mre", [128, 16, 128], F32); Mim = sb("Mim", [128, 16, 128], F32)
    CTr = sb("CTr", [128, 16, 128], BF16); CTi = sb("CTi", [128, 16, 128], BF16)
    BTr = sb("BTr", [128, 16, 128], BF16); BTi = sb("BTi", [128, 16, 128], BF16)
    for (tl, nm) in ((Mre, "Mre"), (Mim, "Mim")):
        OP(P, "pool", "memset", [], [nm], tl[:], 0.0)
    for (tl, nm) in ((CTr, "CTr"), (CTi, "CTi")):
        OP(P, "pool", "memset", [], [nm], tl[:], 0.0)

    def place(dst, gl):
        full = dst[gl * 64:(gl + 1) * 64, :, :]
        t = full.tensor
        a = full.ap
        pstep = a[0][0]
        return bass.AP(t, full.offset + gl * 16, [[pstep, 64], [512, 4], [160, 4], [1, 16]])

    def src4(x, gl):
        return x[gl * 64:(gl + 1) * 64, :, :].rearrange("p (a b) q -> p a b q", a=4)

    for gl in range(2):
        OP(P, "dve", "tensor_copy", ["bbr", "Mre"], ["Mre"], out=place(Mre, gl), in_=src4(bbr, gl))
        OP(P, "dve", "tensor_copy", ["bbi", "Mim"], ["Mim"], out=place(Mim, gl), in_=src4(bbi, gl))
        OP(P, "dve", "tensor_copy", ["ccr", "CTr"], ["CTr"], out=place(CTr, gl), in_=src4(ccr, gl))
        OP(P, "dve", "tensor_copy", ["cci", "CTi"], ["CTi"], out=place(CTi, gl), in_=src4(cci, gl))
    for (M, mn, BT, bn) in ((Mre, "Mre", BTr, "BTr"), (Mim, "Mim", BTi, "BTi")):
        for g4 in range(4):
            bk = nb()
            for j in range(4):
                k = g4 * 4 + j
                OP(P, "pe", "matmul", [mn, "ident_f"], [("ps", bk)], ps[bk][:, j * 128:(j + 1) * 128], M[:, k, :], ident_f, start=True, stop=True)
            OP(P, "act", "activation", [("ps", bk)], [bn], out=BT[:, g4 * 4:g4 * 4 + 4, :],
               in_=ps[bk][:, :].rearrange("p (a b) -> p a b", a=4), func=AF.Copy)

    NTAB = LB + 1
    tabC = sb("tabC", [128, 16, NTAB], F32); tabS = sb("tabS", [128, 16, NTAB], F32)
    ta = sb("ta", [128, 16, 256], F32); tb_ = sb("tb", [128, 16, 256], F32)
    OP(P, "dve", "memset", [], ["tabC"], tabC[:, :, 0:1], 1.0)
    OP(P, "dve", "memset", [], ["tabS"], tabS[:, :, 0:1], 0.0)
    OP(P, "dve", "tensor_copy", ["c1"], ["cc"], out=cc[:], in_=c1[:])
    OP(P, "dve", "tensor_copy", ["s1"], ["ss"], out=ss[:], in_=s1[:])
    n = 1
    while n < NTAB:
        w = min(n, NTAB - n)

        def bcw(x, w=w):
            return x[:].unsqueeze(2).to_broadcast([128, 16, w])
        src_c = tabC[:, :, 0:w]
        src_s = tabS[:, :, 0:w]
        OP(P, "dve", "tensor_tensor", ["tabC", "cc"], ["ta"], out=ta[:, :, 0:w], in0=src_c, in1=bcw(cc), op=ALU.mult)
        OP(P, "dve", "tensor_tensor", ["tabS", "ss"], ["tb"], out=tb_[:, :, 0:w], in0=src_s, in1=bcw(ss), op=ALU.mult)
        OP(P, "dve", "tensor_tensor", ["ta", "tb"], ["tabC"], out=tabC[:, :, n:n + w], in0=ta[:, :, 0:w], in1=tb_[:, :, 0:w], op=ALU.subtract)
        OP(P, "dve", "tensor_tensor", ["tabC", "ss"], ["ta"], out=ta[:, :, 0:w], in0=src_c, in1=bcw(ss), op=ALU.mult)
        OP(P, "dve", "tensor_tensor", ["tabS", "cc"], ["tb"], out=tb_[:, :, 0:w], in0=src_s, in1=bcw(cc), op=ALU.mult)
        OP(P, "dve", "tensor_tensor", ["ta", "tb"], ["tabS"], out=tabS[:, :, n:n + w], in0=ta[:, :, 0:w], in1=tb_[:, :, 0:w], op=ALU.add)
        n += w
        if n < NTAB:
            OP(P, "dve", "tensor_tensor", ["cc", "ss"], ["t1"], out=t1[:], in0=cc[:], in1=ss[:], op=ALU.mult)
            OP(P, "dve", "tensor_tensor", ["cc"], ["t2"], out=t2[:], in0=cc[:], in1=cc[:], op=ALU.mult)
            OP(P, "dve", "tensor_tensor", ["ss"], ["t3"], out=t3[:], in0=ss[:], in1=ss[:], op=ALU.mult)
            OP(P, "dve", "tensor_tensor", ["t2", "t3"], ["cc"], out=cc[:], in0=t2[:], in1=t3[:], op=ALU.subtract)
            OP(P, "dve", "tensor_scalar", ["t1"], ["ss"], out=ss[:], in0=t1[:], scalar1=2.0, scalar2=None, op0=ALU.mult)

    ut = sb("ut", [128, 2, 4, LB], BF16)
    wk = [sb("wk%d" % i, [128, LB], F32) for i in range(4)]
    cre = sb("cre", [128, 2, LB], F32); cim = sb("cim", [128, 2, LB], F32)
    vre = sb("vre", [128, 2, LB], F32); vim = sb("vim", [128, 2, LB], F32)
    sre = sb("sre", [128, 2, LB], F32); sim = sb("sim", [128, 2, LB], F32)
    sreb = sb("sreb", [128, 2, LB], BF16); simb = sb("simb", [128, 2, LB], BF16)
    car = sb("car", [128, 16, 2], F32)
    yw = [sb("yw%d" % i, [128, LB], F32) for i in range(3)]
    yo = sb("yo", [128, 2, LB], BF16)
    OP(P, "dve", "memset", [], [("car", k, j) for k in range(16) for j in range(2)], car[:], 0.0)
    uv = uT_d.rearrange("(c p) t -> p c t", p=128)
    yv = yT_d.rearrange("(c p) t -> p c t", p=128)
    it = 0
    for tb in range(NTB):
        ub = tb % 2
        tsl = slice(tb * LB, (tb + 1) * LB)
        P.dma("sp", ut[:, ub, :, :], uv[:, :, tsl], writes=[("ut", ub)])
        for kc in range(4):
            by = nb()
            for kq in range(4):
                k = kc * 4 + kq
                s = it % 2
                it += 1
                br = nb()
                while br == by:
                    br = nb()
                bi = nb()
                while bi == by:
                    bi = nb()
                OP(P, "pe", "matmul", ["BTr", ("ut", ub)], [("ps", br)], ps[br][:, :], BTr[:, k, :], ut[:, ub, kc, :], start=True, stop=True)
                OP(P, "pe", "matmul", ["BTi", ("ut", ub)], [("ps", bi)], ps[bi][:, :], BTi[:, k, :], ut[:, ub, kc, :], start=True, stop=True)
                tc0 = tabC[:, k, 0:LB]; ts0 = tabS[:, k, 0:LB]
                tc1 = tabC[:, k, 1:LB + 1]; ts1 = tabS[:, k, 1:LB + 1]
                OP(P, "dve", "tensor_tensor", [("ps", br), "tabC"], [("wk", 0)], out=wk[0][:], in0=ps[br][:, :], in1=tc0, op=ALU.mult)
                OP(P, "dve", "tensor_tensor", [("ps", bi), "tabS"], [("wk", 1)], out=wk[1][:], in0=ps[bi][:, :], in1=ts0, op=ALU.mult)
                OP(P, "pool", "tensor_tensor", [("wk", 0), ("wk", 1)], [("cre", s)], out=cre[:, s, :], in0=wk[0][:], in1=wk[1][:], op=ALU.add)
                OP(P, "dve", "tensor_tensor", [("ps", bi), "tabC"], [("wk", 2)], out=wk[2][:], in0=ps[bi][:, :], in1=tc0, op=ALU.mult)
                OP(P, "dve", "tensor_tensor", [("ps", br), "tabS"], [("wk", 3)], out=wk[3][:], in0=ps[br][:, :], in1=ts0, op=ALU.mult)
                OP(P, "pool", "tensor_tensor", [("wk", 2), ("wk", 3)], [("cim", s)], out=cim[:, s, :], in0=wk[2][:], in1=wk[3][:], op=ALU.subtract)
                rb = rho[:, k:k + 1].to_broadcast([128, LB])
                OP(P, "dve", "tensor_tensor_scan", [("cre", s), "rho", ("car", k, 0)], [("vre", s)], out=vre[:, s, :], data0=rb, data1=cre[:, s, :],
                   initial=car[:, k, 0:1], op0=ALU.mult, op1=ALU.add)
                OP(P, "dve", "tensor_tensor_scan", [("cim", s), "rho", ("car", k, 1)], [("vim", s)], out=vim[:, s, :], data0=rb, data1=cim[:, s, :],
                   initial=car[:, k, 1:2], op0=ALU.mult, op1=ALU.add)
                OP(P, "pool", "tensor_tensor", [("vre", s), "tabC"], [("wk", 0)], out=wk[0][:], in0=vre[:, s, :], in1=tc1, op=ALU.mult)
                OP(P, "pool", "tensor_tensor", [("vim", s), "tabS"], [("wk", 1)], out=wk[1][:], in0=vim[:, s, :], in1=ts1, op=ALU.mult)
                OP(P, "pool", "tensor_tensor", [("wk", 0), ("wk", 1)], [("sre", s)], out=sre[:, s, :], in0=wk[0][:], in1=wk[1][:], op=ALU.subtract)
                OP(P, "pool", "tensor_tensor", [("vre", s), "tabS"], [("wk", 2)], out=wk[2][:], in0=vre[:, s, :], in1=ts1, op=ALU.mult)
                OP(P, "pool", "tensor_tensor", [("vim", s), "tabC"], [("wk", 3)], out=wk[3][:], in0=vim[:, s, :], in1=tc1, op=ALU.mult)
                OP(P, "pool", "tensor_tensor", [("wk", 2), ("wk", 3)], [("sim", s)], out=sim[:, s, :], in0=wk[2][:], in1=wk[3][:], op=ALU.add)
                OP(P, "act", "activation", [("sre", s)], [("car", k, 0)], out=car[:, k, 0:1], in_=sre[:, s, LB - 1:LB], func=AF.Copy)
                OP(P, "act", "activation", [("sim", s)], [("car", k, 1)], out=car[:, k, 1:2], in_=sim[:, s, LB - 1:LB], func=AF.Copy)
                OP(P, "act", "activation", [("sre", s)], [("sreb", s)], out=sreb[:, s, :], in_=sre[:, s, :], func=AF.Copy)
                OP(P, "act", "activation", [("sim", s)], [("simb", s)], out=simb[:, s, :], in_=sim[:, s, :], func=AF.Copy)
                OP(P, "pe", "matmul", ["CTr", ("sreb", s)], [("ps", by)], ps[by][:, :], CTr[:, k, :], sreb[:, s, :], start=(kq == 0), stop=False)
                OP(P, "pe", "matmul", ["CTi", ("simb", s)], [("ps", by)], ps[by][:, :], CTi[:, k, :], simb[:, s, :], start=False, stop=(kq == 3))
            yb = (tb * 4 + kc) % 2
            OP(P, "dve", "scalar_tensor_tensor", [("ps", by), ("ut", ub), "dcol"], [("yw", 0)], out=yw[0][:], in0=ut[:, ub, kc, :],
               scalar=dcol[:, kc:kc + 1], in1=ps[by][:, :], op0=ALU.mult, op1=ALU.add)
            if USE_GELU_TANH_AF:
                OP(P, "act", "activation", [("yw", 0)], [("yo", yb)], out=yo[:, yb, :], in_=yw[0][:], func=AF.Gelu_apprx_tanh)
            else:
                OP(P, "dve", "tensor_tensor", [("yw", 0)], [("yw", 1)], out=yw[1][:], in0=yw[0][:], in1=yw[0][:], op=ALU.mult)
                OP(P, "dve", "tensor_scalar", [("yw", 1)], [("yw", 1)], out=yw[1][:], in0=yw[1][:], scalar1=0.044715 * math.sqrt(2 / math.pi),
                   scalar2=math.sqrt(2 / math.pi), op0=ALU.mult, op1=ALU.add)
                OP(P, "dve", "tensor_tensor", [("yw", 0), ("yw", 1)], [("yw", 1)], out=yw[1][:], in0=yw[1][:], in1=yw[0][:], op=ALU.mult)
                OP(P, "act", "activation", [("yw", 1)], [("yw", 2)], out=yw[2][:], in_=yw[1][:], func=AF.Tanh)
                OP(P, "dve", "tensor_scalar", [("yw", 2)], [("yw", 2)], out=yw[2][:], in0=yw[2][:], scalar1=1.0, scalar2=0.5, op0=ALU.add, op1=ALU.mult)
                OP(P, "dve", "tensor_tensor", [("yw", 2), ("yw", 0)], [("yo", yb)], out=yo[:, yb, :], in0=yw[2][:], in1=yw[0][:], op=ALU.mult)
            P.dma("sp", yv[:, kc, tsl], yo[:, yb, :], reads=[("yo", yb)], writes=[("yT", tb, kc)])
    P.flush()
    es.close()


import ml_dtypes
from concourse.bass_utils import run_bass_kernel_spmd

BF16_NP = ml_dtypes.bfloat16
NCORES = 8
OFF = dict(aq=0, ak=2048, av=4096, az=6144, aa=8192, ab=8208, fq=8224, fk=10272, fv=12320, ff=14368)


def core_cols(c):
    cols = []
    for h in (2 * c, 2 * c + 1):
        for nm in ("aq", "ak", "av", "az"):
            cols += list(range(OFF[nm] + h * 128, OFF[nm] + (h + 1) * 128))
    for h in (2 * c, 2 * c + 1):
        for nm in ("fq", "fk", "fv"):
            cols += list(range(OFF[nm] + h * 128, OFF[nm] + (h + 1) * 128))
    cols += [OFF["aa"] + 2 * c, OFF["aa"] + 2 * c + 1, OFF["ab"] + 2 * c, OFF["ab"] + 2 * c + 1, OFF["ff"] + 2 * c, OFF["ff"] + 2 * c + 1]
    return np.array(cols)


def host_consts_A():
    cf = np.zeros((128, 8, 128), np.float32)
    p = np.arange(128)[:, None]
    c = np.arange(128)[None, :]
    same = (p // 64) == (c // 64)
    cf[:, 0, :] = np.eye(128)
    cf[:, 1, :] = (p < c)
    cf[:, 2, :] = np.where(p <= c, 0.0, -30000.0)
    cf[:, 3, :] = 1.0
    cf[:, 4, :] = (c < p) & same
    cf[:, 5, :] = (c >= p) & same
    cf[:, 6, :] = (c > p) & same
    r = np.ones((128, 128), np.float32)
    r[:, 0] = 0
    r[:, 64] = 0
    cf[:, 7, :] = r
    return cf, np.eye(128, dtype=np.float32).astype(BF16_NP)


def load_consts_A(P, nc, cpf, cpb):
    cf = nc.alloc_sbuf_tensor("c_f", [128, 8, 128], F32)
    cb = nc.alloc_sbuf_tensor("c_b", [128, 128], BF16)
    P.dma("sp", cf[:], cpf, writes=["cf"])
    P.dma("sp", cb[:], cpb, writes=["cb"])
    return dict(ident_f=cf[:, 0, :], lt64=cf[0:64, 1, 0:64], negmask=cf[:, 2, :], ones_f=cf[:, 3, :], maskL=cf[:, 4, :], maskUi=cf[:, 5, :],
                maskUs=cf[:, 6, :], rst=cf[0:64, 7, :], ident_bf=cb[:, :])


def gdn_par(inp, c):
    gp = np.zeros((128, 2, 16), np.float32)
    cw = inp["gdn_conv_w"][0]
    for hg in range(2):
        h = 2 * c + hg
        for i in range(3):
            gp[:, hg, 4 * i:4 * i + 4] = cw[:, i * 2048 + h * 128: i * 2048 + (h + 1) * 128].T
        gp[:, hg, 12] = inp["gdn_o_norm"][0]
        gp[:, hg, 13] = inp["gdn_A_log"][0][h]
        gp[:, hg, 14] = inp["gdn_dt_bias"][0][h]
    return gp


def s5_par(inp, c):
    G = slice(32 * c, 32 * c + 32)
    are = inp["s5_A_re"][0][G]
    aim = inp["s5_A_im"][0][G]
    ls = inp["s5_log_step"][0][G]
    apar = np.zeros((128, 16, 3), np.float32)
    bpk = np.zeros((128, 2, 16, 16), np.float32)
    cpk = np.zeros((128, 2, 16, 16), np.float32)
    Bre = inp["s5_B_re"][0][G]
    Bim = inp["s5_B_im"][0][G]
    Cre = inp["s5_C_re"][0][G]
    Cim = inp["s5_C_im"][0][G]
    for k in range(16):
        for gl in range(2):
            g = 2 * k + gl
            rows = slice(gl * 64, gl * 64 + 64)
            apar[rows, k, 0] = are[g]
            apar[rows, k, 1] = aim[g]
            apar[rows, k, 2] = ls[g]
            bpk[rows, 0, k, :] = Bre[g]
            bpk[rows, 1, k, :] = Bim[g]
            cpk[rows, 0, k, :] = Cre[g].T
            cpk[rows, 1, k, :] = Cim[g].T
    dcol = np.ascontiguousarray(inp["s5_D"][0][512 * c:512 * c + 512].reshape(4, 128).T)
    return apar, bpk, cpk, dcol


def _ein(nc, name, shape, dt=F32):
    return nc.dram_tensor(name, shape, dt, kind="ExternalInput").ap()


def _eout(nc, name, shape, dt=F32):
    return nc.dram_tensor(name, shape, dt, kind="ExternalOutput").ap()


def build_A():
    nc = bass.Bass("TRN2", target_bir_lowering=False)
    xT = _ein(nc, "xT", [4096, 8192])
    wsl = _ein(nc, "wsl", [4096, NCOL])
    g0col = _ein(nc, "g0col", [128, 32])
    gpar_d = _ein(nc, "gpar", [128, 2, 16])
    nfb_d = _ein(nc, "nfb", [128, 2])
    cpf = _ein(nc, "cpf", [128, 8, 128])
    cpb = _ein(nc, "cpb", [128, 128], BF16)
    oT = _eout(nc, "oT", [4, 128, 8192], BF16)
    gq = nc.dram_tensor("gq", [8, 128, 8192], F32).ap()
    fq = nc.dram_tensor("fq", [6, 128, 8192], BF16).ap()
    rows = nc.dram_tensor("rows", [6, 8192], F32).ap()
    P = Prog(nc)
    ps = [nc.alloc_psum_tensor("ps%d" % i, [128, 512], F32) for i in range(8)]
    cst = load_consts_A(P, nc, cpf, cpb)
    gpt = nc.alloc_sbuf_tensor("gpt", [128, 2, 16], F32)
    nfbt = nc.alloc_sbuf_tensor("nfbt", [128, 2], F32)
    P.dma("sp", gpt[:], gpar_d, writes=["gpar"])
    P.dma("sp", nfbt[:], nfb_d, writes=["nfb"])
    P.flush()
    emit_A1(P, nc, ps, xT=xT, wsl=wsl, g0col=g0col, gq=gq, fq=fq, rows=rows)
    for hg in range(2):
        emit_gdn_head(P, nc, ps, cst, gq_d=gq[4 * hg:4 * hg + 4], a_d=rows[hg:hg + 1, :], b_d=rows[2 + hg:3 + hg, :], gpar=gpt[:, hg, :],
                      oT_d=oT[hg], pfx="gd%d_" % hg)
    for hf in range(2):
        emit_fox_head(P, nc, ps, cst, qT_d=fq[3 * hf + 0], kT_d=fq[3 * hf + 1], vT_d=fq[3 * hf + 2], f_d=rows[4 + hf:5 + hf, :],
                      nfb_col=nfbt[0:64, hf:hf + 1], oT_d=oT[2 + hf], pfx="fx%d_" % hf)
    return nc


def build_tok(glu, with_hn):
    nc = bass.Bass("TRN2", target_bir_lowering=False)
    inT = _ein(nc, "inT", [D, TOK], BF16)
    xres = _ein(nc, "xres", [TOK, D])
    wm = [_ein(nc, "wm%d" % i, [D, D]) for i in range(2 if glu else 1)]
    gains = _ein(nc, "gains", [4, D])
    wg = _ein(nc, "wg", [D, HID])
    wu = _ein(nc, "wu", [D, HID])
    wd = _ein(nc, "wd", [HID, D])
    ident_d = _ein(nc, "ident_in", [128, 128], BF16)
    xout = _eout(nc, "xout", [TOK, D])
    hnT = _eout(nc, "hnT", [D, TOK], BF16) if with_hn else None
    scr = {k: nc.dram_tensor("scr_" + k, [TOK, D], F32).ap() for k in ("m", "x1", "f")}
    P = Prog(nc)
    ps = [nc.alloc_psum_tensor("ps%d" % i, [128, 512], F32) for i in range(8)]
    actT = nc.alloc_sbuf_tensor("actT", [128, 32, TOK], BF16)
    ident = nc.alloc_sbuf_tensor("ident", [128, 128], BF16)
    P.dma("sp", ident[:], ident_d, writes=["ident"])
    P.flush()
    emit_tok_block(P, nc, ps, actT, ident, inT=inT, xres=xres, wmix=wm,
                   gains=(gains[0:1, :], gains[1:2, :], gains[2:3, :], gains[3:4, :] if with_hn else None),
                   wg=wg, wu=wu, wd=wd, xout=xout, hnT_out=hnT, scr=scr, pfx="t_")
    return nc


def build_C():
    nc = bass.Bass("TRN2", target_bir_lowering=False)
    uT = _ein(nc, "uT", [512, 8192], BF16)
    apar = _ein(nc, "apar", [128, 16, 3])
    bpk = _ein(nc, "bpk", [128, 2, 16, 16])
    cpk = _ein(nc, "cpk", [128, 2, 16, 16])
    dcol = _ein(nc, "dcol", [128, 4])
    cpf = _ein(nc, "cpf", [128, 8, 128])
    cpb = _ein(nc, "cpb", [128, 128], BF16)
    yT = _eout(nc, "yT", [512, 8192], BF16)
    P = Prog(nc)
    ps = [nc.alloc_psum_tensor("ps%d" % i, [128, 512], F32) for i in range(8)]
    cst = load_consts_A(P, nc, cpf, cpb)
    P.flush()
    emit_s5(P, nc, ps, cst, uT_d=uT, apar_d=apar, b_d=bpk, c_d=cpk, dcol_d=dcol, yT_d=yT)
    return nc


def kernel(**inputs):
    inp = {k: np.asarray(v) for k, v in inputs.items()}
    x = inp["x"][0]
    g = inp["norm_g"]
    cf, cb = host_consts_A()
    cores = list(range(NCORES))
    xT = np.ascontiguousarray(x.T)
    w_in = inp["w_in"][0]
    g0col = np.ascontiguousarray(g[0, 0].reshape(32, 128).T)
    fb = inp["fox_f_bias"][0]
    maps = []
    for c in cores:
        maps.append(dict(xT=xT, wsl=np.ascontiguousarray(w_in[:, core_cols(c)]), g0col=g0col, gpar=gdn_par(inp, c),
                         nfb=np.tile(-fb[2 * c:2 * c + 2][None, :], (128, 1)).astype(np.float32), cpf=cf, cpb=cb))
    resA = run_bass_kernel_spmd(build_A(), maps, core_ids=cores)
    oT_full = np.zeros((4096, 8192), dtype=BF16_NP)
    for c in cores:
        o = resA.results[c]["oT"]
        for hg in range(2):
            h = 2 * c + hg
            oT_full[h * 128:(h + 1) * 128] = o[hg]
            oT_full[2048 + h * 128:2048 + (h + 1) * 128] = o[2 + hg]
    ident = cb
    gainsB = np.stack([g[0, 1], g[0, 2], g[0, 3], g[1, 0]]).astype(np.float32)
    maps = []
    for c in cores:
        sl = slice(c * TOK, (c + 1) * TOK)
        maps.append(dict(inT=np.ascontiguousarray(oT_full[:, sl]), xres=np.ascontiguousarray(x[sl]), wm0=inp["w_out"][0], gains=gainsB,
                         wg=inp["ffn_w_gate"][0], wu=inp["ffn_w_up"][0], wd=inp["ffn_w_down"][0], ident_in=ident))
    resB = run_bass_kernel_spmd(build_tok(False, True), maps, core_ids=cores)
    x2 = [resB.results[c]["xout"] for c in cores]
    hnT_full = np.concatenate([resB.results[c]["hnT"] for c in cores], axis=1)
    maps = []
    for c in cores:
        apar, bpk, cpk, dcol = s5_par(inp, c)
        maps.append(dict(uT=np.ascontiguousarray(hnT_full[512 * c:512 * c + 512]), apar=apar, bpk=bpk, cpk=cpk, dcol=dcol, cpf=cf, cpb=cb))
    resC = run_bass_kernel_spmd(build_C(), maps, core_ids=cores)
    yT_full = np.concatenate([resC.results[c]["yT"] for c in cores], axis=0)
    gainsD = np.stack([g[1, 1], g[1, 2], g[1, 3], g[1, 3]]).astype(np.float32)
    maps = []
    for c in cores:
        sl = slice(c * TOK, (c + 1) * TOK)
        maps.append(dict(inT=np.ascontiguousarray(yT_full[:, sl]), xres=x2[c], wm0=inp["s5_w_glu_a"][0], wm1=inp["s5_w_glu_b"][0], gains=gainsD,
                         wg=inp["ffn_w_gate"][1], wu=inp["ffn_w_up"][1], wd=inp["ffn_w_down"][1], ident_in=ident))
    resD = run_bass_kernel_spmd(build_tok(True, False), maps, core_ids=cores)
    out = np.concatenate([resD.results[c]["xout"] for c in cores], axis=0)
    return out[None].astype(np.float32)
```

```python
import contextlib
import math


import numpy as np
import concourse.bass as bass
import concourse.mybir as mybir

F32 = mybir.dt.float32
BF16 = mybir.dt.bfloat16
I32 = mybir.dt.int32
AF = mybir.ActivationFunctionType
ALU = mybir.AluOpType
AX = mybir.AxisListType

ENGS = ["pe", "dve", "act", "pool", "sp"]
ATTR = {"pe": "tensor", "dve": "vector", "act": "scalar", "pool": "gpsimd", "sp": "sync"}


class Prog:
    def __init__(self, nc, nslots=6):
        self.nc = nc
        self.sem = {e: nc.alloc_semaphore("s_" + e) for e in ENGS}
        self.base = {e: 0 for e in ENGS}
        self.queues = {}
        for qn, issuer in (("sp", "sp"), ("pool", "pool"), ("act", "act")):
            self.queues[qn] = dict(
                issuer=issuer,
                sems=[nc.alloc_semaphore("q_%s_%d" % (qn, i)) for i in range(nslots)],
                cnt=[0] * nslots,
                n=0,
            )
        self.nslots = nslots
        self.maxwait = {e: {} for e in ENGS}
        self._reset()
        allsems = list(self.sem.values()) + [s for Q in self.queues.values() for s in Q["sems"]]
        with nc.Block() as blk:
            def _clr(h):
                for s in allsems:
                    h.sem_clear(s)
            blk.gpsimd(_clr)

    def _reset(self):
        self.ops = []
        self.res = {}
        self.stream = {e: [] for e in ENGS}

    def _deps(self, reads, writes):
        deps = set()
        for r in reads:
            st = self.res.get(r)
            if st and st["w"] is not None:
                deps.add(st["w"])
        for w in writes:
            st = self.res.get(w)
            if st:
                if st["w"] is not None:
                    deps.add(st["w"])
                deps.update(st["r"])
        return deps

    def _commit(self, oid, reads, writes):
        for r in reads:
            st = self.res.setdefault(r, {"w": None, "r": []})
            st["r"].append(oid)
        for w in writes:
            self.res[w] = {"w": oid, "r": []}

    def op(self, eng, fn, reads=(), writes=()):
        deps = self._deps(reads, writes)
        oid = len(self.ops)
        self.ops.append(dict(eng=eng, fn=fn, deps=deps, kind="c", marked=False))
        self.stream[eng].append(oid)
        self._commit(oid, reads, writes)
        return oid

    def dma(self, q, out, in_, reads=(), writes=(), **kw):
        Q = self.queues[q]
        slot = Q["n"] % self.nslots
        Q["n"] += 1
        prev = Q["cnt"][slot]
        Q["cnt"][slot] += 1
        deps = self._deps(reads, writes)
        oid = len(self.ops)
        self.ops.append(dict(eng=Q["issuer"], kind="dma", q=q, slot=slot, val=Q["cnt"][slot] * 16,
                             prev=prev * 16, deps=deps, out=out, in_=in_, kw=kw))
        self.stream[Q["issuer"]].append(oid)
        self._commit(oid, reads, writes)
        return oid

    def flush(self):
        ops = self.ops
        for o in ops:
            for d in o["deps"]:
                p = ops[d]
                if p["kind"] == "c":
                    if p["eng"] == o["eng"] and p["eng"] == "pe":
                        continue
                    p["marked"] = True
        cum = {e: self.base[e] for e in ENGS}
        for e in ENGS:
            for oid in self.stream[e]:
                o = ops[oid]
                if o["kind"] == "c" and o["marked"]:
                    cum[e] += 1
                    o["val"] = cum[e]
        prog = self
        nc = self.nc
        with nc.Block() as block:
            for e in ENGS:
                def body(h, e=e):
                    mw = prog.maxwait[e]
                    def wait(sem, key, val):
                        if mw.get(key, 0) >= val:
                            return
                        mw[key] = val
                        h.wait_ge(sem, val)
                    for oid in prog.stream[e]:
                        o = ops[oid]
                        for d in sorted(o["deps"]):
                            p = ops[d]
                            if p["kind"] == "dma":
                                Q = prog.queues[p["q"]]
                                wait(Q["sems"][p["slot"]], ("q", p["q"], p["slot"]), p["val"])
                            else:
                                if not p["marked"]:
                                    continue
                                if p["eng"] == e and e == "pe":
                                    continue
                                wait(prog.sem[p["eng"]], p["eng"], p["val"])
                        if o["kind"] == "dma":
                            Q = prog.queues[o["q"]]
                            if o["prev"] > 0:
                                wait(Q["sems"][o["slot"]], ("q", o["q"], o["slot"]), o["prev"])
                            ins = h.dma_start(out=o["out"], in_=o["in_"], **o["kw"])
                            ins.then_inc(Q["sems"][o["slot"]], 16)
                        else:
                            ins = o["fn"](h)
                            if o["marked"]:
                                ins.then_inc(prog.sem[e], 1)
                    for qn, Q in prog.queues.items():
                        if Q["issuer"] == e:
                            for s in range(prog.nslots):
                                if Q["cnt"][s] > 0:
                                    wait(Q["sems"][s], ("q", qn, s), Q["cnt"][s] * 16)
                getattr(block, ATTR[e])(body)
        for e in ENGS:
            self.base[e] = cum[e]
        self._reset()


D = 4096
HID = 11008
NHC = HID // 128
TOK = 1024
NTQ = TOK // 128
EPS = 1e-6


def bcast_rows(ap_row, nparts):
    return bass.AP(ap_row.tensor, ap_row.offset, [[0, nparts]] + [list(x) for x in ap_row.ap[1:]])


_UID = [0]


def emit_tok_block(P, nc, ps, actT, ident, *, inT, xres, wmix, gains, wg, wu, wd, xout, hnT_out, scr, pfx):
    glu = len(wmix) == 2
    pfx0 = pfx
    ps_bf = [p.bitcast(BF16) for p in ps]
    NPART = 3
    parts = [(0, 29), (29, 58), (58, 86)]

    NB = 4
    CW = 256
    with nc.sbuf_tensor(pfx + "wbuf", [128, NB, 32, CW], BF16) as wbuf, \
         nc.sbuf_tensor(pfx + "stg", [128, 4, CW], F32) as stg, \
         nc.sbuf_tensor(pfx + "sig", [128, 4, CW], F32) as sig:
        inT_v = inT.rearrange("(k p) t -> p k t", p=128)
        for k4 in range(4):
            P.dma("sp", actT[:, k4 * 8:(k4 + 1) * 8, :], inT_v[:, k4 * 8:(k4 + 1) * 8, :], writes=[("actT", k4)])
        wv = [w.rearrange("(k p) n -> p k n", p=128) for w in wmix]
        li = 0
        gi = 0
        for db in range(D // CW):
            bufs = []
            for mi in range(len(wmix)):
                b = li % NB
                li += 1
                for hf in range(2):
                    P.dma("pool", wbuf[:, b, hf * 16:(hf + 1) * 16, :], wv[mi][:, hf * 16:(hf + 1) * 16, db * CW:(db + 1) * CW],
                          writes=[("wbuf", b, hf)])
                bufs.append(b)
            for tq in range(NTQ):
                banks = []
                for mi in range(len(wmix)):
                    bank = (gi * len(wmix) + mi) % 8
                    banks.append(bank)
                    b = bufs[mi]
                    for kc in range(32):
                        P.op("pe", lambda h, bank=bank, b=b, kc=kc, tq=tq: h.matmul(
                            ps[bank][:, 0:CW], actT[:, kc, tq * 128:(tq + 1) * 128], wbuf[:, b, kc, :],
                            start=(kc == 0), stop=(kc == 31)),
                            reads=[("actT", kc // 8), ("wbuf", b, kc // 16)], writes=[("ps", bank)])
                s = gi % 4
                if glu:
                    P.op("act", lambda h, s=s, bk=banks[1]: h.activation(out=sig[:, s, :], in_=ps[bk][:, 0:CW], func=AF.Sigmoid),
                         reads=[("ps", banks[1])], writes=[("sig", s)])
                    P.op("dve", lambda h, s=s, bk=banks[0]: h.tensor_tensor(out=stg[:, s, :], in0=ps[bk][:, 0:CW], in1=sig[:, s, :], op=ALU.mult),
                         reads=[("ps", banks[0]), ("sig", s)], writes=[("stg", s)])
                else:
                    if gi % 2 == 0:
                        P.op("act", lambda h, s=s, bk=banks[0]: h.activation(out=stg[:, s, :], in_=ps[bk][:, 0:CW], func=AF.Copy),
                             reads=[("ps", banks[0])], writes=[("stg", s)])
                    else:
                        P.op("dve", lambda h, s=s, bk=banks[0]: h.tensor_copy(out=stg[:, s, :], in_=ps[bk][:, 0:CW]),
                             reads=[("ps", banks[0])], writes=[("stg", s)])
                P.dma("sp", scr["m"][tq * 128:(tq + 1) * 128, db * CW:(db + 1) * CW], stg[:, s, :], reads=[("stg", s)],
                      writes=[("m", tq)])
                gi += 1
        P.flush()

    def norm_pass(srcs, res_ap, g_post, g_next, x_store, to_actT, hn_dst):
        _UID[0] += 1
        pfx = pfx0 + "n%d_" % _UID[0]
        with nc.sbuf_tensor(pfx + "mt", [128, 2, D], F32) as mt, \
             nc.sbuf_tensor(pfx + "xt", [128, 2, D], F32) as xt, \
             nc.sbuf_tensor(pfx + "gb", [128, 2, D], F32) as gb, \
             nc.sbuf_tensor(pfx + "junk", [128, D], BF16) as junk, \
             nc.sbuf_tensor(pfx + "hn", [128, D], BF16) as hn, \
             nc.sbuf_tensor(pfx + "hst", [128, 2, 4, 128], BF16) as hst, \
             nc.sbuf_tensor(pfx + "st", [128, 8], F32) as st, \
             nc.sbuf_tensor(pfx + "cst", [128, 2], F32) as cst:
            P.dma("sp", gb[:, 0, :], bcast_rows(g_post, 128), writes=[("gb", 0)])
            if g_next is not None:
                P.dma("sp", gb[:, 1, :], bcast_rows(g_next, 128), writes=[("gb", 1)])
            P.op("dve", lambda h: h.memset(cst[:, 0:1], -0.5), writes=["cst"])
            ev = 0
            for tq in range(NTQ):
                b = tq % 2
                rows = slice(tq * 128, (tq + 1) * 128)
                P.dma("sp", mt[:, b, :], srcs[0][rows, :], writes=[("mt", b)])
                P.dma("sp", xt[:, b, :], res_ap[rows, :], writes=[("xt", b)])
                for extra in srcs[1:]:
                    pass
                if len(srcs) > 1:
                    raise NotImplementedError
                P.op("act", lambda h, b=b: h.activation(out=junk[:], in_=mt[:, b, :], func=AF.Square, accum_out=st[:, 0:1]),
                     reads=[("mt", b)], writes=["junk", ("st", 0)])
                P.op("dve", lambda h: h.tensor_scalar(out=st[:, 1:2], in0=st[:, 0:1], scalar1=1.0 / D, scalar2=EPS, op0=ALU.mult, op1=ALU.add),
                     reads=[("st", 0)], writes=[("st", 1)])
                P.op("pool", lambda h: h.tensor_tensor(out=st[:, 2:3], in0=st[:, 1:2], in1=cst[:, 0:1], op=ALU.pow),
                     reads=[("st", 1), "cst"], writes=[("st", 2)])
                P.op("dve", lambda h, b=b: h.scalar_tensor_tensor(out=mt[:, b, :], in0=mt[:, b, :], scalar=st[:, 2:3], in1=gb[:, 0, :],
                                                                 op0=ALU.mult, op1=ALU.mult),
                     reads=[("mt", b), ("st", 2), ("gb", 0)], writes=[("mt", b)])
                P.op("pool", lambda h, b=b: h.tensor_tensor(out=mt[:, b, :], in0=mt[:, b, :], in1=xt[:, b, :], op=ALU.add),
                     reads=[("mt", b), ("xt", b)], writes=[("mt", b)])
                P.dma("sp", x_store[rows, :], mt[:, b, :], reads=[("mt", b)], writes=[("xs", tq)])
                if g_next is None:
                    continue
                P.op("act", lambda h, b=b: h.activation(out=junk[:], in_=mt[:, b, :], func=AF.Square, accum_out=st[:, 3:4]),
                     reads=[("mt", b)], writes=["junk", ("st", 3)])
                P.op("dve", lambda h: h.tensor_scalar(out=st[:, 4:5], in0=st[:, 3:4], scalar1=1.0 / D, scalar2=EPS, op0=ALU.mult, op1=ALU.add),
                     reads=[("st", 3)], writes=[("st", 4)])
                P.op("pool", lambda h: h.tensor_tensor(out=st[:, 5:6], in0=st[:, 4:5], in1=cst[:, 0:1], op=ALU.pow),
                     reads=[("st", 4), "cst"], writes=[("st", 5)])
                P.op("dve", lambda h, b=b: h.scalar_tensor_tensor(out=hn[:], in0=mt[:, b, :], scalar=st[:, 5:6], in1=gb[:, 1, :],
                                                                 op0=ALU.mult, op1=ALU.mult),
                     reads=[("mt", b), ("st", 5), ("gb", 1)], writes=["hn"])
                for k4 in range(8):
                    bank = ev % 8
                    for j in range(4):
                        kc = k4 * 4 + j
                        P.op("pe", lambda h, bank=bank, j=j, kc=kc: h.transpose(
                            out=ps_bf[bank][:, j * 128:(j + 1) * 128], in_=hn[:, kc * 128:(kc + 1) * 128], identity=ident[:]),
                            reads=["hn", "ident"], writes=[("ps", bank)])
                    src = ps_bf[bank][:, 0:512].rearrange("p (a b) -> p a b", a=4)
                    if to_actT:
                        dst = actT[:, k4 * 4:(k4 + 1) * 4, tq * 128:(tq + 1) * 128]
                        wr = [("actT", k4 // 2, tq)]
                    else:
                        hb = ev % 2
                        dst = hst[:, hb, :, :]
                        wr = [("hst", hb)]
                    if ev % 2 == 0:
                        P.op("act", lambda h, dst=dst, src=src: h.activation(out=dst, in_=src, func=AF.Copy),
                             reads=[("ps", bank)], writes=wr)
                    else:
                        P.op("dve", lambda h, dst=dst, src=src: h.tensor_copy(out=dst, in_=src),
                             reads=[("ps", bank)], writes=wr)
                    if not to_actT:
                        dv = hn_dst.rearrange("(k p) t -> p k t", p=128)[:, k4 * 4:(k4 + 1) * 4, tq * 128:(tq + 1) * 128]
                        P.dma("sp", dv, hst[:, hb, :, :], reads=[("hst", hb)], writes=[("hno", k4, tq)])
                    ev += 1
            P.flush()

    norm_pass([scr["m"]], xres, gains[0], gains[1], scr["x1"], True, None)

    wgv = wg.rearrange("(k p) n -> p k n", p=128)
    wuv = wu.rearrange("(k p) n -> p k n", p=128)
    wdv = wd.rearrange("(c p) n -> p c n", p=128)
    with nc.sbuf_tensor(pfx + "hT", [128, 29, TOK], BF16) as hT, \
         nc.sbuf_tensor(pfx + "wgb", [128, 2, 32, 128], BF16) as wgb, \
         nc.sbuf_tensor(pfx + "wub", [128, 2, 32, 128], BF16) as wub, \
         nc.sbuf_tensor(pfx + "wdb", [128, 3, 4, 512], BF16) as wdb, \
         nc.sbuf_tensor(pfx + "sg", [128, 2, 512], BF16) as sg, \
         nc.sbuf_tensor(pfx + "fst", [128, 4, 512], F32) as fst:
        hci = 0
        wdi = 0
        evi = 0
        for pi, (h0, h1) in enumerate(parts):
            nh = h1 - h0
            for hl in range(nh):
                hc = h0 + hl
                b = hci % 2
                hci += 1
                for hf in range(2):
                    P.dma("pool", wgb[:, b, hf * 16:(hf + 1) * 16, :], wgv[:, hf * 16:(hf + 1) * 16, hc * 128:(hc + 1) * 128],
                          writes=[("wgb", b, hf)])
                    P.dma("pool", wub[:, b, hf * 16:(hf + 1) * 16, :], wuv[:, hf * 16:(hf + 1) * 16, hc * 128:(hc + 1) * 128],
                          writes=[("wub", b, hf)])
                for tt in range(2):
                    gbank = tt * 2
                    ubank = tt * 2 + 1
                    for (bank, wb, nm) in ((gbank, wgb, "wgb"), (ubank, wub, "wub")):
                        for kc in range(32):
                            P.op("pe", lambda h, bank=bank, wb=wb, b=b, kc=kc, tt=tt: h.matmul(
                                ps[bank][:, :], wb[:, b, kc, :], actT[:, kc, tt * 512:(tt + 1) * 512],
                                start=(kc == 0), stop=(kc == 31)),
                                reads=[(nm, b, kc // 16)] + [("actT", kc // 8, tt * 4 + q) for q in range(4)], writes=[("ps", bank)])
                    P.op("act", lambda h, tt=tt, gbank=gbank: h.activation(out=sg[:, tt, :], in_=ps[gbank][:, :], func=AF.Silu),
                         reads=[("ps", gbank)], writes=[("sg", tt)])
                    P.op("dve", lambda h, tt=tt, ubank=ubank, hl=hl: h.tensor_tensor(
                        out=hT[:, hl, tt * 512:(tt + 1) * 512], in0=ps[ubank][:, :], in1=sg[:, tt, :], op=ALU.mult),
                        reads=[("ps", ubank), ("sg", tt)], writes=[("hT", hl, tt)])
            groups = [(a, min(a + 4, nh)) for a in range(0, nh, 4)]
            for db in range(8):
                for (a0, a1) in groups:
                    wb_i = wdi % 3
                    wdi += 1
                    P.dma("pool", wdb[:, wb_i, 0:a1 - a0, :], wdv[:, h0 + a0:h0 + a1, db * 512:(db + 1) * 512], writes=[("wdb", wb_i)])
                    for hl in range(a0, a1):
                        for tq in range(NTQ):
                            P.op("pe", lambda h, tq=tq, hl=hl, wb_i=wb_i, a0=a0: h.matmul(
                                ps[tq][:, :], hT[:, hl, tq * 128:(tq + 1) * 128], wdb[:, wb_i, hl - a0, :],
                                start=(hl == 0), stop=(hl == nh - 1)),
                                reads=[("hT", hl, tq // 4), ("wdb", wb_i)], writes=[("ps", tq)])
                for tq in range(NTQ):
                    s = evi % 4
                    if evi % 2 == 0:
                        P.op("act", lambda h, s=s, tq=tq: h.activation(out=fst[:, s, :], in_=ps[tq][:, :], func=AF.Copy),
                             reads=[("ps", tq)], writes=[("fst", s)])
                    else:
                        P.op("dve", lambda h, s=s, tq=tq: h.tensor_copy(out=fst[:, s, :], in_=ps[tq][:, :]),
                             reads=[("ps", tq)], writes=[("fst", s)])
                    evi += 1
                    dst = scr["f"][tq * 128:(tq + 1) * 128, db * 512:(db + 1) * 512]
                    if pi == 0:
                        P.dma("sp", dst, fst[:, s, :], reads=[("fst", s)], writes=[("f", tq, db)])
                    else:
                        P.dma("pool", dst, fst[:, s, :], reads=[("fst", s)], writes=[("f", tq, db)], accum_op=ALU.add)
        P.flush()

    norm_pass([scr["f"]], scr["x1"], gains[2], gains[3], xout, False, hnT_out)


T = 8192
NCOL = 1798


def emit_A1(P, nc, ps, *, xT, wsl, g0col, gq, fq, rows, pfx="a1_"):
    TW = 256
    NT = T // TW
    with nc.sbuf_tensor(pfx + "W", [128, 32, NCOL], BF16) as Wbf, \
         nc.sbuf_tensor(pfx + "wst", [128, 2, NCOL], F32) as wst, \
         nc.sbuf_tensor(pfx + "gc", [128, 32], F32) as gcol, \
         nc.sbuf_tensor(pfx + "xt", [128, 2, 32, TW], BF16) as xt, \
         nc.sbuf_tensor(pfx + "sq", [128, 4, TW], BF16) as sq, \
         nc.sbuf_tensor(pfx + "ones", [128, 128], BF16) as ones, \
         nc.sbuf_tensor(pfx + "rb", [128, 2, TW], F32) as rb, \
         nc.sbuf_tensor(pfx + "stf", [128, 4, TW], F32) as stf, \
         nc.sbuf_tensor(pfx + "stb", [128, 4, TW], BF16) as stb:
        P.dma("sp", gcol[:], g0col, writes=["gcol"])
        P.op("pool", lambda h: h.memset(ones[:], 1.0), writes=["ones"])
        wv = wsl.rearrange("(k p) n -> p k n", p=128)
        for kc in range(32):
            b = kc % 2
            P.dma("sp", wst[:, b, :], wv[:, kc, :], writes=[("wst", b)])
            P.op("dve", lambda h, b=b, kc=kc: h.tensor_scalar(out=Wbf[:, kc, :], in0=wst[:, b, :], scalar1=gcol[:, kc:kc + 1], scalar2=None,
                                                             op0=ALU.mult),
                 reads=[("wst", b), "gcol"], writes=[("W", kc)])
        xv = xT.rearrange("(k p) t -> p k t", p=128)
        bankc = 0
        sti = 0
        for it in range(NT):
            xb = it % 2
            cols = slice(it * TW, (it + 1) * TW)
            for k4 in range(4):
                P.dma("pool", xt[:, xb, k4 * 8:(k4 + 1) * 8, :], xv[:, k4 * 8:(k4 + 1) * 8, cols], writes=[("xt", xb, k4)])
            for kc in range(32):
                s = kc % 4
                P.op("act", lambda h, s=s, kc=kc, xb=xb: h.activation(out=sq[:, s, :], in_=xt[:, xb, kc, :], func=AF.Square),
                     reads=[("xt", xb, kc // 8)], writes=[("sq", s)])
                P.op("pe", lambda h, s=s, kc=kc: h.matmul(ps[7][:, 0:TW], ones[:], sq[:, s, :], start=(kc == 0), stop=(kc == 31)),
                     reads=[("sq", s), "ones"], writes=[("ps", 7)])
            r = it % 2
            P.op("dve", lambda h, r=r: h.tensor_scalar(out=rb[:, r, :], in0=ps[7][:, 0:TW], scalar1=1.0 / D, scalar2=EPS, op0=ALU.mult, op1=ALU.add),
                 reads=[("ps", 7)], writes=[("rb", r)])
            P.op("act", lambda h, r=r: h.activation(out=rb[:, r, :], in_=rb[:, r, :], func=AF.Sqrt), reads=[("rb", r)], writes=[("rb", r)])
            P.op("dve", lambda h, r=r: h.reciprocal(out=rb[:, r, :], in_=rb[:, r, :]), reads=[("rb", r)], writes=[("rb", r)])
            for cg in range(15):
                bank = bankc % 7
                bankc += 1
                M = 128 if cg < 14 else 6
                for kc in range(32):
                    P.op("pe", lambda h, bank=bank, kc=kc, cg=cg, M=M, xb=xb: h.matmul(
                        ps[bank][0:M, 0:TW], Wbf[:, kc, cg * 128:cg * 128 + M], xt[:, xb, kc, :], start=(kc == 0), stop=(kc == 31)),
                        reads=[("W", kc), ("xt", xb, kc // 8)], writes=[("ps", bank)])
                s = sti % 4
                sti += 1
                if 8 <= cg < 14:
                    P.op("dve", lambda h, s=s, bank=bank, r=r: h.tensor_tensor(out=stb[:, s, :], in0=ps[bank][:, 0:TW], in1=rb[:, r, :], op=ALU.mult),
                         reads=[("ps", bank), ("rb", r)], writes=[("stb", s)])
                    P.dma("sp", fq[cg - 8, :, cols], stb[:, s, :], reads=[("stb", s)], writes=[("fq", cg, it)])
                else:
                    P.op("dve", lambda h, s=s, bank=bank, r=r, M=M: h.tensor_tensor(out=stf[0:M, s, :], in0=ps[bank][0:M, 0:TW], in1=rb[0:M, r, :],
                                                                                    op=ALU.mult),
                         reads=[("ps", bank), ("rb", r)], writes=[("stf", s)])
                    if cg < 8:
                        P.dma("sp", gq[cg, :, cols], stf[:, s, :], reads=[("stf", s)], writes=[("gq", cg, it)])
                    else:
                        P.dma("sp", rows[:, cols], stf[0:6, s, :], reads=[("stf", s)], writes=[("rows", it)])
        P.flush()


NKB = T // 128


def emit_fox_head(P, nc, ps, cst, *, qT_d, kT_d, vT_d, f_d, nfb_col, oT_d, pfx):
    scale = 128 ** -0.5
    ps_bf = [p.bitcast(BF16) for p in ps]
    with nc.sbuf_tensor(pfx + "qT", [128, T], BF16) as qT, \
         nc.sbuf_tensor(pfx + "kT", [128, T], BF16) as kT, \
         nc.sbuf_tensor(pfx + "va", [128, NKB, 130], BF16) as va, \
         nc.sbuf_tensor(pfx + "cq", [128, T], F32) as cq, \
         nc.sbuf_tensor(pfx + "ck", [128, NKB], F32) as ck, \
         nc.sbuf_tensor(pfx + "c1", [64, 128], F32) as c1, \
         nc.sbuf_tensor(pfx + "c2", [64, 128], F32) as c2, \
         nc.sbuf_tensor(pfx + "car", [64, 1], F32) as car, \
         nc.sbuf_tensor(pfx + "tmp", [128, 3, 512], F32) as tmp, \
         nc.sbuf_tensor(pfx + "pT", [128, 3, 512], BF16) as pT, \
         nc.sbuf_tensor(pfx + "on", [128, 2, 128], BF16) as on, \
         nc.sbuf_tensor(pfx + "rs", [128, 4], F32) as rs, \
         nc.sbuf_tensor(pfx + "ost", [128, 2, 512], BF16) as ost:
        for hf in range(2):
            sl = slice(hf * 4096, (hf + 1) * 4096)
            P.dma("sp", qT[:, sl], qT_d[:, sl], writes=[("qT", hf)])
            P.dma("sp", kT[:, sl], kT_d[:, sl], writes=[("kT", hf)])
        vT = cq.bitcast(BF16)
        P.dma("sp", vT[:, 0:T], vT_d, writes=["cq"])
        P.op("pool", lambda h: h.memset(va[:, :, 128:130], 1.0), writes=[("va1",)])
        for g in range(NKB // 4):
            bank = 4 + g % 4
            for j in range(4):
                kb = g * 4 + j
                P.op("pe", lambda h, bank=bank, j=j, kb=kb: h.transpose(out=ps_bf[bank][:, j * 128:(j + 1) * 128],
                                                                       in_=vT[:, kb * 128:(kb + 1) * 128], identity=cst["ident_bf"][:]),
                     reads=["cq", "ident_bf"], writes=[("ps", bank)])
            src = ps_bf[bank][:, 0:512].rearrange("p (a b) -> p a b", a=4)
            eng = "act" if g % 2 == 0 else "dve"
            if eng == "act":
                P.op("act", lambda h, g=g, src=src: h.activation(out=va[:, g * 4:(g + 1) * 4, 0:128], in_=src, func=AF.Copy),
                     reads=[("ps", bank)], writes=[("va", g)])
            else:
                P.op("dve", lambda h, g=g, src=src: h.tensor_copy(out=va[:, g * 4:(g + 1) * 4, 0:128], in_=src),
                     reads=[("ps", bank)], writes=[("va", g)])
        P.dma("sp", c1[:], f_d.rearrange("o (k p) -> (o k) p", p=128), writes=["c1"])
        P.op("act", lambda h: h.activation(out=c1[:], in_=c1[:], func=AF.Exp, scale=-1.0, bias=nfb_col), reads=["c1", "nfb"], writes=["c1"])
        P.op("act", lambda h: h.activation(out=c1[:], in_=c1[:], func=AF.Ln, scale=1.0, bias=1.0), reads=["c1"], writes=["c1"])
        P.op("dve", lambda h: h.tensor_tensor_scan(out=c2[:], data0=cst["ones_f"][0:64, :], data1=c1[:], initial=0.0, op0=ALU.mult, op1=ALU.add),
             reads=["c1", "ones_f"], writes=["c2"])
        P.op("pe", lambda h: h.matmul(ps[0][0:64, 0:1], cst["lt64"][:], c2[:, 127:128], start=True, stop=True),
             reads=["c2", "lt64"], writes=[("ps", 0)])
        P.op("dve", lambda h: h.tensor_copy(out=car[:], in_=ps[0][0:64, 0:1]), reads=[("ps", 0)], writes=["car"])
        P.op("dve", lambda h: h.tensor_scalar(out=c2[:], in0=c2[:], scalar1=car[:, 0:1], scalar2=None, op0=ALU.add), reads=["c2", "car"], writes=["c2"])
        P.op("pe", lambda h: h.matmul(ps[1][:, 0:64], c2[:], cst["ident_f"][0:64, 0:64], start=True, stop=True),
             reads=["c2", "ident_f"], writes=[("ps", 1)])
        P.op("dve", lambda h: h.tensor_copy(out=ck[:], in_=ps[1][:, 0:64]), reads=[("ps", 1)], writes=["ck"])
        for g in range(NKB // 4):
            bank = 4 + g % 4
            for j in range(4):
                kb = g * 4 + j
                P.op("pe", lambda h, bank=bank, j=j, kb=kb: h.matmul(ps[bank][:, j * 128:(j + 1) * 128],
                                                                    cst["ident_f"][0:64, kb:kb + 1].to_broadcast([64, 128]), c2[:],
                                                                    start=True, stop=True),
                     reads=["c2", "ident_f"], writes=[("ps", bank)])
            P.op("act", lambda h, g=g, bank=bank: h.activation(out=cq[:, g * 512:(g + 1) * 512], in_=ps[bank][:, :], func=AF.Copy, scale=-1.0),
                 reads=[("ps", bank)], writes=["cq"])
        it = 0
        for qt in range(T // 512):
            nkb = 4 * qt + 4
            for kb in range(nkb):
                j = kb - 4 * qt
                c0 = max(j, 0) * 128
                bank = 4 + it % 4
                s = it % 3
                it += 1
                q0 = qt * 512
                P.op("pe", lambda h, bank=bank, kb=kb, c0=c0, q0=q0: h.matmul(ps[bank][:, c0:512], kT[:, kb * 128:(kb + 1) * 128],
                                                                             qT[:, q0 + c0:q0 + 512], start=True, stop=True),
                     reads=[("kT", kb // 32), ("qT", qt // 8)], writes=[("ps", bank)])
                P.op("dve", lambda h, bank=bank, s=s, c0=c0, q0=q0: h.scalar_tensor_tensor(
                    out=tmp[:, s, c0:512], in0=ps[bank][:, c0:512], scalar=scale, in1=cq[:, q0 + c0:q0 + 512], op0=ALU.mult, op1=ALU.add),
                    reads=[("ps", bank), "cq"], writes=[("tmp", s)])
                if j >= 0:
                    P.op("pool", lambda h, s=s, c0=c0: h.tensor_tensor(out=tmp[:, s, c0:c0 + 128], in0=tmp[:, s, c0:c0 + 128],
                                                                       in1=cst["negmask"][:], op=ALU.add),
                         reads=[("tmp", s), "negmask"], writes=[("tmp", s)])
                P.op("act", lambda h, s=s, c0=c0, kb=kb: h.activation(out=pT[:, s, c0:512], in_=tmp[:, s, c0:512], func=AF.Exp,
                                                                       bias=ck[:, kb:kb + 1], scale=1.0),
                     reads=[("tmp", s), "ck"], writes=[("pT", s)])
                for i in range(max(j, 0), 4):
                    P.op("pe", lambda h, i=i, s=s, kb=kb, qt=qt: h.matmul(ps[i][:, 0:129], pT[:, s, i * 128:(i + 1) * 128], va[:, kb, 0:129],
                                                                          start=(kb == 0), stop=(kb == 4 * qt + i)),
                         reads=[("pT", s), ("va", kb // 4), ("va1",)], writes=[("ps", i)])
            ob = qt % 2
            for i in range(4):
                P.op("dve", lambda h, i=i: h.reciprocal(out=rs[:, i:i + 1], in_=ps[i][:, 128:129]), reads=[("ps", i)], writes=[("rs", i)])
                o2 = i % 2
                P.op("act", lambda h, i=i, o2=o2: h.activation(out=on[:, o2, :], in_=ps[i][:, 0:128], func=AF.Copy, scale=rs[:, i:i + 1]),
                     reads=[("ps", i), ("rs", i)], writes=[("on", o2)])
                P.op("pe", lambda h, i=i, o2=o2: h.transpose(out=ps_bf[i][:, 512:640], in_=on[:, o2, :], identity=cst["ident_bf"][:]),
                     reads=[("on", o2), "ident_bf"], writes=[("ps", i)])
                P.op("dve", lambda h, i=i, ob=ob: h.tensor_copy(out=ost[:, ob, i * 128:(i + 1) * 128], in_=ps_bf[i][:, 512:640]),
                     reads=[("ps", i)], writes=[("ost", ob)])
            P.dma("sp", oT_d[:, qt * 512:(qt + 1) * 512], ost[:, ob, :], reads=[("ost", ob)], writes=[("oT", qt)])
        P.flush()


SEG = 1024
NSEG = T // SEG
NM = SEG // 128
L2_EPS = 1e-6
RMS_EPS = 1e-6


def OP(P, eng, method, reads, writes, *args, **kw):
    return P.op(eng, lambda h: getattr(h, method)(*args, **kw), reads, writes)


def emit_gdn_head(P, nc, ps, cst, *, gq_d, a_d, b_d, gpar, oT_d, pfx):
    ps_bf = [p.bitcast(BF16) for p in ps]
    bankc = [0]

    def nb():
        b = bankc[0] % 8
        bankc[0] += 1
        return b

    es = contextlib.ExitStack()

    def sb(name, shape, dt):
        return es.enter_context(nc.sbuf_tensor(pfx + name, shape, dt))

    a_sc = sb("a_sc", [64, 128], F32); b_sc = sb("b_sc", [64, 128], F32)
    g_sc = sb("g_sc", [64, 128], F32); gc_sc = sb("gc_sc", [64, 128], F32)
    t_sc = sb("t_sc", [64, 128], F32); t2_sc = sb("t2_sc", [64, 128], F32)
    bg_sc = sb("bg_sc", [64, 128], F32); ekd_sc = sb("ekd_sc", [64, 128], F32)
    egl_sc = sb("egl_sc", [64, 2], F32); nea = sb("nea", [128, 1], F32)
    gccol = sb("gccol", [128, 64], F32); nbcol = sb("nbcol", [128, 64], F32); bcol = sb("bcol", [128, 64], F32)
    bgcol = sb("bgcol", [128, 64], F32); ekdcol = sb("ekdcol", [128, 64], F32)
    eglb = sb("eglb", [128, 128], F32)
    S32 = sb("S32", [128, 128], F32); Sbf = sb("Sbf", [128, 128], BF16)
    xr = [sb("xr%d" % i, [128, SEG + 3], F32) for i in range(3)]
    cv = [sb("cv%d" % i, [128, SEG], F32) for i in range(3)]
    sqt = sb("sqt", [128, SEG], F32); rinv = sb("rinv", [128, SEG], F32)
    qnT = sb("qnT", [128, SEG], BF16); knT = sb("knT", [128, SEG], BF16); vT = sb("vT", [128, SEG], BF16); qgT = sb("qgT", [128, SEG], BF16)
    gcb = sb("gcb", [128, SEG], F32); nbb = sb("nbb", [128, SEG], F32); egb = sb("egb", [128, SEG], F32)
    dif = sb("dif", [128, SEG], F32); e1 = sb("e1", [128, SEG], F32); e2 = sb("e2", [128, SEG], F32)
    dmti = sb("dmti", [128, SEG], F32); dmtb = sb("dmtb", [128, SEG], F32)
    Y = [sb("Y%d" % i, [128, SEG], BF16) for i in range(2)]
    YT = [sb("YT%d" % i, [128, SEG], BF16) for i in range(2)]
    P32 = sb("P32", [128, SEG], F32); Pbf = [sb("Pbf%d" % i, [128, SEG], BF16) for i in range(2)]
    attnT = sb("attnT", [128, SEG], BF16)
    kbg = sb("kbg", [128, NM, 128], BF16); kd = sb("kd", [128, NM, 128], BF16); vb = sb("vb", [128, NM, 128], BF16)
    u_sb = sb("u_sb", [128, NM, 128], F32); wT = sb("wT", [128, SEG], BF16)
    vn = sb("vn", [128, 2, 128], BF16)
    oT_sb = sb("oT_sb", [128, SEG], F32); zt = sb("zt", [128, SEG], F32); gout = sb("gout", [128, SEG], BF16)

    ident_f = cst["ident_f"]; ident_bf = cst["ident_bf"]; ones_f = cst["ones_f"]

    def v3(ap):
        return ap.rearrange("p (m j) -> p m j", m=NM)

    def colb(col, m0):
        return col[:, m0:m0 + NM].unsqueeze(2).to_broadcast([128, NM, 128])

    def maskb(mk):
        return mk.unsqueeze(1).to_broadcast([128, NM, 128])

    P.dma("sp", a_sc[:], a_d.rearrange("o (k p) -> (o k) p", p=128), writes=["a_sc"])
    P.dma("sp", b_sc[:], b_d.rearrange("o (k p) -> (o k) p", p=128), writes=["b_sc"])
    OP(P, "act", "activation", ["gpar"], ["nea"], out=nea[:], in_=gpar[:, 13:14], func=AF.Exp)
    OP(P, "dve", "tensor_scalar", ["nea"], ["nea"], out=nea[:], in0=nea[:], scalar1=-1.0, scalar2=None, op0=ALU.mult)
    OP(P, "act", "activation", ["a_sc", "gpar"], ["t_sc"], out=t_sc[:], in_=a_sc[:], func=AF.Exp, bias=gpar[0:64, 14:15], scale=1.0)
    OP(P, "act", "activation", ["t_sc"], ["t_sc"], out=t_sc[:], in_=t_sc[:], func=AF.Ln, bias=1.0, scale=1.0)
    OP(P, "dve", "tensor_scalar", ["t_sc", "nea"], ["g_sc"], out=g_sc[:], in0=t_sc[:], scalar1=nea[0:64, 0:1], scalar2=None, op0=ALU.mult)
    OP(P, "dve", "tensor_tensor_scan", ["g_sc", "rst"], ["gc_sc"], out=gc_sc[:], data0=cst["rst"], data1=g_sc[:], initial=0.0,
       op0=ALU.mult, op1=ALU.add)
    OP(P, "act", "activation", ["b_sc"], ["b_sc"], out=b_sc[:], in_=b_sc[:], func=AF.Sigmoid)
    OP(P, "act", "activation", ["gc_sc"], ["t_sc"], out=t_sc[:], in_=gc_sc[:], func=AF.Exp)
    OP(P, "dve", "tensor_tensor", ["t_sc", "b_sc"], ["bg_sc"], out=bg_sc[:], in0=t_sc[:], in1=b_sc[:], op=ALU.mult)
    gc3 = gc_sc[:].rearrange("p (a b) -> p a b", a=2)
    OP(P, "dve", "tensor_tensor", ["gc_sc"], ["t2_sc"], out=t2_sc[:].rearrange("p (a b) -> p a b", a=2),
       in0=gc3[:, :, 63:64].to_broadcast([64, 2, 64]), in1=gc3, op=ALU.subtract)
    OP(P, "act", "activation", ["t2_sc"], ["ekd_sc"], out=ekd_sc[:], in_=t2_sc[:], func=AF.Exp)
    OP(P, "act", "activation", ["gc_sc"], ["egl_sc"], out=egl_sc[:].unsqueeze(2), in_=gc3[:, :, 63:64], func=AF.Exp)
    for (src, srcn, dst, nm, sc) in ((gc_sc, "gc_sc", gccol, "gccol", 1.0), (b_sc, "b_sc", nbcol, "nbcol", -1.0), (b_sc, "b_sc", bcol, "bcol", 1.0),
                                     (bg_sc, "bg_sc", bgcol, "bgcol", 1.0), (ekd_sc, "ekd_sc", ekdcol, "ekdcol", 1.0)):
        bk = nb()
        OP(P, "pe", "matmul", [srcn, "ident_f"], [("ps", bk)], ps[bk][:, 0:64], src[:], ident_f[0:64, 0:64],
           start=True, stop=True)
        OP(P, "act", "activation", [("ps", bk)], [nm], out=dst[:], in_=ps[bk][:, 0:64], func=AF.Copy, scale=sc)
    bk = nb()
    for m in range(64):
        OP(P, "pe", "matmul", ["egl_sc", "ident_f"], [("ps", bk)], ps[bk][:, 2 * m:2 * m + 2], ident_f[0:64, m:m + 1].to_broadcast([64, 128]),
           egl_sc[:], start=True, stop=True)
    OP(P, "dve", "tensor_copy", [("ps", bk)], ["eglb"], out=eglb[:], in_=ps[bk][:, 0:128])
    OP(P, "dve", "memset", [], ["S32"], S32[:], 0.0)
    OP(P, "pool", "memset", [], ["Sbf"], Sbf[:], 0.0)

    names3 = ["q", "k", "v"]
    for sg in range(NSEG):
        t0 = sg * SEG
        m0 = sg * NM
        for i in range(3):
            if sg == 0:
                OP(P, "pool", "memset", [], [("xr", i)], xr[i][:, 0:3], 0.0)
                P.dma("sp", xr[i][:, 3:SEG + 3], gq_d[i, :, 0:SEG], writes=[("xr", i)])
            else:
                P.dma("sp", xr[i][:, :], gq_d[i, :, t0 - 3:t0 + SEG], writes=[("xr", i)])
            OP(P, "act", "activation", [("xr", i), "gpar"], [("cv", i)], out=cv[i][:], in_=xr[i][:, 3:SEG + 3], func=AF.Copy,
               scale=gpar[:, 4 * i + 3:4 * i + 4])
            for j in (2, 1, 0):
                OP(P, "dve", "scalar_tensor_tensor", [("xr", i), ("cv", i), "gpar"], [("cv", i)], out=cv[i][:], in0=xr[i][:, j:j + SEG],
                   scalar=gpar[:, 4 * i + j:4 * i + j + 1], in1=cv[i][:], op0=ALU.mult, op1=ALU.add)
            OP(P, "act", "activation", [("cv", i)], [("cv", i)], out=cv[i][:], in_=cv[i][:], func=AF.Silu)
        P.dma("sp", zt[:], gq_d[3, :, t0:t0 + SEG], writes=["zt"])
        OP(P, "pool", "tensor_copy", [("cv", 2)], ["vT"], out=vT[:], in_=cv[2][:])
        for (src, srcn, dst, dstn, sc) in ((gc_sc, "gc_sc", gcb, "gcb", 1.0), (b_sc, "b_sc", nbb, "nbb", -1.0)):
            for hf in range(NM // 4):
                bk = nb()
                for j in range(4):
                    m = m0 + hf * 4 + j
                    OP(P, "pe", "matmul", [srcn, "ident_f"], [("ps", bk)], ps[bk][:, j * 128:(j + 1) * 128],
                       ident_f[0:64, m:m + 1].to_broadcast([64, 128]), src[:], start=True, stop=True)
                OP(P, "act", "activation", [("ps", bk)], [dstn], out=dst[:, hf * 512:(hf + 1) * 512], in_=ps[bk][:, :], func=AF.Copy, scale=sc)
        OP(P, "act", "activation", ["gcb"], ["egb"], out=egb[:], in_=gcb[:], func=AF.Exp)
        for i in range(2):
            OP(P, "act", "activation", [("cv", i)], ["sqt"], out=sqt[:], in_=cv[i][:], func=AF.Square)
            for hf in range(SEG // 512):
                bk = nb()
                OP(P, "pe", "matmul", ["sqt", "ones_f"], [("ps", bk)], ps[bk][:, :], ones_f, sqt[:, hf * 512:(hf + 1) * 512], start=True, stop=True)
                OP(P, "dve", "tensor_scalar", [("ps", bk)], ["rinv"], out=rinv[:, hf * 512:(hf + 1) * 512], in0=ps[bk][:, :], scalar1=L2_EPS,
                   scalar2=None, op0=ALU.add)
            OP(P, "act", "activation", ["rinv"], ["rinv"], out=rinv[:], in_=rinv[:], func=AF.Sqrt)
            OP(P, "dve", "reciprocal", ["rinv"], ["rinv"], out=rinv[:], in_=rinv[:])
            if i == 0:
                OP(P, "dve", "scalar_tensor_tensor", [("cv", 0), "rinv"], ["qnT"], out=qnT[:], in0=cv[0][:], scalar=128 ** -0.5, in1=rinv[:],
                   op0=ALU.mult, op1=ALU.mult)
                OP(P, "dve", "tensor_tensor", ["qnT", "egb"], ["qgT"], out=qgT[:], in0=qnT[:], in1=egb[:], op=ALU.mult)
            else:
                OP(P, "dve", "tensor_tensor", [("cv", 1), "rinv"], ["knT"], out=knT[:], in0=cv[1][:], in1=rinv[:], op=ALU.mult)
        OP(P, "dve", "tensor_tensor", ["gccol", "gcb"], ["dif"], out=v3(dif[:]), in0=colb(gccol, m0), in1=v3(gcb[:]), op=ALU.subtract)
        OP(P, "dve", "tensor_scalar", ["dif"], ["e1"], out=e1[:], in0=dif[:], scalar1=0.0, scalar2=None, op0=ALU.min)
        OP(P, "act", "activation", ["e1"], ["e1"], out=e1[:], in_=e1[:], func=AF.Exp)
        OP(P, "pool", "tensor_tensor", ["e1", "maskL"], ["e1"], out=v3(e1[:]), in0=v3(e1[:]), in1=maskb(cst["maskL"]), op=ALU.mult)
        OP(P, "dve", "tensor_tensor", ["e1", "nbcol"], ["e1"], out=v3(e1[:]), in0=v3(e1[:]), in1=colb(nbcol, m0), op=ALU.mult)
        OP(P, "dve", "tensor_scalar", ["dif"], ["e2"], out=e2[:], in0=dif[:], scalar1=0.0, scalar2=None, op0=ALU.max)
        OP(P, "act", "activation", ["e2"], ["e2"], out=e2[:], in_=e2[:], func=AF.Exp, scale=-1.0)
        OP(P, "pool", "tensor_tensor", ["e2", "maskUi"], ["dmti"], out=v3(dmti[:]), in0=v3(e2[:]), in1=maskb(cst["maskUi"]), op=ALU.mult)
        OP(P, "pool", "tensor_tensor", ["nbb", "maskUs"], ["dmtb"], out=v3(dmtb[:]), in0=v3(nbb[:]), in1=maskb(cst["maskUs"]), op=ALU.mult)
        OP(P, "dve", "tensor_tensor", ["dmtb", "e2"], ["dmtb"], out=dmtb[:], in0=dmtb[:], in1=e2[:], op=ALU.mult)
        for hf in range(NM // 4):
            bkk = nb(); bqk = nb()
            for j in range(4):
                m = hf * 4 + j
                sl = slice(m * 128, (m + 1) * 128)
                OP(P, "pe", "matmul", ["knT"], [("ps", bkk)], ps[bkk][:, j * 128:(j + 1) * 128], knT[:, sl], knT[:, sl], start=True, stop=True)
                OP(P, "pe", "matmul", ["knT", "qnT"], [("ps", bqk)], ps[bqk][:, j * 128:(j + 1) * 128], knT[:, sl], qnT[:, sl], start=True, stop=True)
            hs = slice(hf * 512, (hf + 1) * 512)
            OP(P, "dve", "tensor_tensor", [("ps", bkk), "e1"], [("Y", 0)], out=Y[0][:, hs], in0=ps[bkk][:, :], in1=e1[:, hs], op=ALU.mult)
            OP(P, "dve", "tensor_tensor", [("ps", bkk), "dmtb"], [("YT", 0)], out=YT[0][:, hs], in0=ps[bkk][:, :], in1=dmtb[:, hs], op=ALU.mult)
            OP(P, "dve", "tensor_tensor", [("ps", bqk), "dmti"], ["attnT"], out=attnT[:, hs], in0=ps[bqk][:, :], in1=dmti[:, hs], op=ALU.mult)
        for hf in range(NM // 4):
            bk_k = nb(); bk_v = nb()
            for j in range(4):
                m = hf * 4 + j
                sl = slice(m * 128, (m + 1) * 128)
                OP(P, "pe", "transpose", ["knT", "ident_bf"], [("ps", bk_k)], out=ps_bf[bk_k][:, j * 128:(j + 1) * 128], in_=knT[:, sl], identity=ident_bf)
                OP(P, "pe", "transpose", ["vT", "ident_bf"], [("ps", bk_v)], out=ps_bf[bk_v][:, j * 128:(j + 1) * 128], in_=vT[:, sl], identity=ident_bf)
            ms = slice(hf * 4, hf * 4 + 4)
            srck = ps_bf[bk_k][:, 0:512].rearrange("p (a b) -> p a b", a=4)
            srcv = ps_bf[bk_v][:, 0:512].rearrange("p (a b) -> p a b", a=4)

            def cb4(col):
                return col[:, m0 + hf * 4:m0 + hf * 4 + 4].unsqueeze(2).to_broadcast([128, 4, 128])
            OP(P, "dve", "tensor_tensor", [("ps", bk_k), "bgcol"], ["kbg"], out=kbg[:, ms, :], in0=srck, in1=cb4(bgcol), op=ALU.mult)
            OP(P, "dve", "tensor_tensor", [("ps", bk_k), "ekdcol"], ["kd"], out=kd[:, ms, :], in0=srck, in1=cb4(ekdcol), op=ALU.mult)
            OP(P, "dve", "tensor_tensor", [("ps", bk_v), "bcol"], ["vb"], out=vb[:, ms, :], in0=srcv, in1=cb4(bcol), op=ALU.mult)
        OP(P, "dve", "tensor_tensor", [("YT", 0), "ident_f"], ["P32"], out=v3(P32[:]), in0=v3(YT[0][:]), in1=maskb(ident_f), op=ALU.add)
        OP(P, "act", "activation", ["P32"], [("Pbf", 0)], out=Pbf[0][:], in_=P32[:], func=AF.Copy)
        pcur = 0
        for lvl in range(1, 6):
            cu = (lvl - 1) % 2
            nx = lvl % 2
            for hf in range(NM // 4):
                hs = slice(hf * 512, (hf + 1) * 512)
                by = nb()
                for j in range(4):
                    sl = slice((hf * 4 + j) * 128, (hf * 4 + j + 1) * 128)
                    OP(P, "pe", "matmul", [("YT", cu), ("Y", cu)], [("ps", by)], ps[by][:, j * 128:(j + 1) * 128], YT[cu][:, sl], Y[cu][:, sl],
                       start=True, stop=True)
                OP(P, "act", "activation", [("ps", by)], [("Y", nx)], out=Y[nx][:, hs], in_=ps[by][:, :], func=AF.Copy)
                if lvl < 5:
                    byt = nb()
                    for j in range(4):
                        sl = slice((hf * 4 + j) * 128, (hf * 4 + j + 1) * 128)
                        OP(P, "pe", "matmul", [("YT", cu), ("Y", cu)], [("ps", byt)], ps[byt][:, j * 128:(j + 1) * 128], Y[cu][:, sl], YT[cu][:, sl],
                           start=True, stop=True)
                    OP(P, "dve", "tensor_copy", [("ps", byt)], [("YT", nx)], out=YT[nx][:, hs], in_=ps[byt][:, :])
            for hf in range(NM // 4):
                hs = slice(hf * 512, (hf + 1) * 512)
                bp = nb()
                for j in range(4):
                    sl = slice((hf * 4 + j) * 128, (hf * 4 + j + 1) * 128)
                    OP(P, "pe", "matmul", [("Y", nx), ("Pbf", pcur)], [("ps", bp)], ps[bp][:, j * 128:(j + 1) * 128], Y[nx][:, sl], Pbf[pcur][:, sl],
                       start=True, stop=True)
                OP(P, "dve", "tensor_tensor", [("ps", bp), "P32"], ["P32"], out=P32[:, hs], in0=ps[bp][:, :], in1=P32[:, hs], op=ALU.add)
            pn = 1 - pcur
            OP(P, "act", "activation", ["P32"], [("Pbf", pn)], out=Pbf[pn][:], in_=P32[:], func=AF.Copy)
            pcur = pn
        TT = Pbf[pcur]
        ttn = ("Pbf", pcur)
        for hf in range(NM // 4):
            bu = nb(); bw = nb()
            for j in range(4):
                m = hf * 4 + j
                sl = slice(m * 128, (m + 1) * 128)
                OP(P, "pe", "matmul", [ttn, "vb"], [("ps", bu)], ps[bu][:, j * 128:(j + 1) * 128], TT[:, sl], vb[:, m, :], start=True, stop=True)
                OP(P, "pe", "matmul", [ttn, "kbg"], [("ps", bw)], ps[bw][:, j * 128:(j + 1) * 128], kbg[:, m, :], TT[:, sl], start=True, stop=True)
            OP(P, "act", "activation", [("ps", bu)], ["u_sb"], out=u_sb[:, hf * 4:hf * 4 + 4, :],
               in_=ps[bu][:, :].rearrange("p (a b) -> p a b", a=4), func=AF.Copy)
            OP(P, "dve", "tensor_copy", [("ps", bw)], ["wT"], out=wT[:, hf * 512:(hf + 1) * 512], in_=ps[bw][:, :])
        for half in range(2):
            bo = nb()
            for cl in range(8):
                cc = half * 8 + cl
                m = cc // 2
                hh = cc % 2
                R = slice(64 * hh, 64 * hh + 64)
                C = slice(m * 128 + 64 * hh, m * 128 + 64 * hh + 64)
                cg = sg * 16 + cc
                bws = nb()
                while bws == bo:
                    bws = nb()
                OP(P, "pe", "matmul", ["wT", "Sbf"], [("ps", bws)], ps[bws][:, 0:128], wT[:, m * 128:(m + 1) * 128], Sbf[:], start=True, stop=True)
                vi = cc % 2
                OP(P, "dve", "tensor_tensor", ["u_sb", ("ps", bws)], [("vn", vi)], out=vn[R, vi, :], in0=u_sb[R, m, :], in1=ps[bws][R, 0:128],
                   op=ALU.subtract)
                OP(P, "pe", "matmul", ["Sbf", "qgT"], [("ps", bo)], ps[bo][:, cl * 64:(cl + 1) * 64], Sbf[:], qgT[:, C], start=True, stop=False)
                OP(P, "pe", "matmul", [("vn", vi), "attnT"], [("ps", bo)], ps[bo][:, cl * 64:(cl + 1) * 64], vn[R, vi, :], attnT[R, C],
                   start=False, stop=True)
                bsd = nb()
                while bsd == bo:
                    bsd = nb()
                OP(P, "pe", "matmul", ["kd", ("vn", vi)], [("ps", bsd)], ps[bsd][:, 0:128], kd[R, m, :], vn[R, vi, :], start=True, stop=True)
                OP(P, "dve", "scalar_tensor_tensor", ["S32", "eglb", ("ps", bsd)], ["S32"], out=S32[:], in0=S32[:], scalar=eglb[:, cg:cg + 1],
                   in1=ps[bsd][:, 0:128], op0=ALU.mult, op1=ALU.add)
                OP(P, "act", "activation", ["S32"], ["Sbf"], out=Sbf[:], in_=S32[:], func=AF.Copy)
            OP(P, "act", "activation", [("ps", bo)], ["oT_sb"], out=oT_sb[:, half * 512:(half + 1) * 512], in_=ps[bo][:, :], func=AF.Copy)
        OP(P, "act", "activation", ["oT_sb"], ["sqt"], out=sqt[:], in_=oT_sb[:], func=AF.Square)
        for hf in range(SEG // 512):
            bk = nb()
            OP(P, "pe", "matmul", ["sqt", "ones_f"], [("ps", bk)], ps[bk][:, :], ones_f, sqt[:, hf * 512:(hf + 1) * 512], start=True, stop=True)
            OP(P, "dve", "tensor_scalar", [("ps", bk)], ["rinv"], out=rinv[:, hf * 512:(hf + 1) * 512], in0=ps[bk][:, :], scalar1=1.0 / 128,
               scalar2=RMS_EPS, op0=ALU.mult, op1=ALU.add)
        OP(P, "act", "activation", ["rinv"], ["rinv"], out=rinv[:], in_=rinv[:], func=AF.Sqrt)
        OP(P, "dve", "reciprocal", ["rinv"], ["rinv"], out=rinv[:], in_=rinv[:])
        OP(P, "dve", "scalar_tensor_tensor", ["oT_sb", "rinv", "gpar"], ["oT_sb"], out=oT_sb[:], in0=oT_sb[:], scalar=gpar[:, 12:13], in1=rinv[:],
           op0=ALU.mult, op1=ALU.mult)
        OP(P, "act", "activation", ["zt"], ["zt"], out=zt[:], in_=zt[:], func=AF.Silu)
        OP(P, "dve", "tensor_tensor", ["oT_sb", "zt"], ["gout"], out=gout[:], in0=oT_sb[:], in1=zt[:], op=ALU.mult)
        P.dma("sp", oT_d[:, t0:t0 + SEG], gout[:], reads=["gout"], writes=[("oTd", sg)])
    P.flush()
    es.close()


LB = 512
NTB = T // LB
S5_MAX_RE = -1e-4
USE_GELU_TANH_AF = False


def OP(P, eng, method, reads, writes, *args, **kw):
    return P.op(eng, lambda h: getattr(h, method)(*args, **kw), reads, writes)


def emit_s5(P, nc, ps, cst, *, uT_d, apar_d, b_d, c_d, dcol_d, yT_d, pfx="s5_"):
    es = contextlib.ExitStack()
    uid = [0]

    def sb(name, shape, dt):
        return es.enter_context(nc.sbuf_tensor(pfx + name, shape, dt))

    def small(name):
        return sb(name, [128, 16], F32)

    bankc = [0]

    def nb():
        b = bankc[0] % 8
        bankc[0] += 1
        return b

    ident_f = cst["ident_f"]
    apar = sb("apar", [128, 16, 3], F32)
    Bt = sb("Bt", [128, 2, 16, 16], F32)
    Ct = sb("Ct", [128, 2, 16, 16], F32)
    dcol = sb("dcol", [128, 4], F32)
    P.dma("sp", apar[:], apar_d, writes=["apar"])
    P.dma("sp", Bt[:], b_d, writes=["Bt"])
    P.dma("sp", Ct[:], c_d, writes=["Ct"])
    P.dma("sp", dcol[:], dcol_d, writes=["dcol"])

    lre = small("lre"); lim = small("lim"); dt = small("dt"); rho = small("rho"); th = small("th")
    c1 = small("c1"); s1 = small("s1"); t1 = small("t1"); t2 = small("t2"); t3 = small("t3")
    fre = small("fre"); fim = small("fim"); cc = small("cc"); ss = small("ss")
    halfpi = sb("halfpi", [128, 1], F32)
    OP(P, "dve", "memset", [], ["halfpi"], halfpi[:], math.pi / 2)
    OP(P, "dve", "tensor_scalar", ["apar"], ["lre"], out=lre[:], in0=apar[:, :, 0], scalar1=S5_MAX_RE, scalar2=None, op0=ALU.min)
    OP(P, "dve", "tensor_copy", ["apar"], ["lim"], out=lim[:], in_=apar[:, :, 1])
    OP(P, "act", "activation", ["apar"], ["dt"], out=dt[:], in_=apar[:, :, 2], func=AF.Exp)
    OP(P, "dve", "tensor_tensor", ["lre", "dt"], ["t1"], out=t1[:], in0=lre[:], in1=dt[:], op=ALU.mult)
    OP(P, "act", "activation", ["t1"], ["rho"], out=rho[:], in_=t1[:], func=AF.Exp)
    OP(P, "dve", "tensor_tensor", ["lim", "dt"], ["th"], out=th[:], in0=lim[:], in1=dt[:], op=ALU.mult)
    OP(P, "dve", "tensor_scalar", ["th"], ["t1"], out=t1[:], in0=th[:], scalar1=1.0 / (2 * math.pi), scalar2=12582912.0, op0=ALU.mult, op1=ALU.add)
    OP(P, "dve", "tensor_scalar", ["t1"], ["t1"], out=t1[:], in0=t1[:], scalar1=-12582912.0, scalar2=None, op0=ALU.add)
    C1 = 6.28125
    C2 = 2 * math.pi - C1
    OP(P, "dve", "scalar_tensor_tensor", ["t1", "th"], ["t2"], out=t2[:], in0=t1[:], scalar=-C1, in1=th[:], op0=ALU.mult, op1=ALU.add)
    OP(P, "dve", "scalar_tensor_tensor", ["t1", "t2"], ["t2"], out=t2[:], in0=t1[:], scalar=-C2, in1=t2[:], op0=ALU.mult, op1=ALU.add)
    OP(P, "dve", "tensor_scalar", ["t2"], ["t2"], out=t2[:], in0=t2[:], scalar1=math.pi, scalar2=-math.pi, op0=ALU.min, op1=ALU.max)
    OP(P, "act", "activation", ["t2"], ["s1"], out=s1[:], in_=t2[:], func=AF.Sin)
    OP(P, "dve", "tensor_scalar", ["t2"], ["t3"], out=t3[:], in0=t2[:], scalar1=-1.0, scalar2=None, op0=ALU.mult)
    OP(P, "dve", "tensor_tensor", ["t2", "t3"], ["t3"], out=t3[:], in0=t3[:], in1=t2[:], op=ALU.max)
    OP(P, "act", "activation", ["t3", "halfpi"], ["c1"], out=c1[:], in_=t3[:], func=AF.Sin, scale=-1.0, bias=halfpi[:, 0:1])
    OP(P, "dve", "tensor_tensor", ["rho", "c1"], ["t1"], out=t1[:], in0=rho[:], in1=c1[:], op=ALU.mult)
    OP(P, "dve", "tensor_scalar", ["t1"], ["t1"], out=t1[:], in0=t1[:], scalar1=-1.0, scalar2=None, op0=ALU.add)
    OP(P, "dve", "tensor_tensor", ["rho", "s1"], ["t2"], out=t2[:], in0=rho[:], in1=s1[:], op=ALU.mult)
    OP(P, "dve", "tensor_tensor", ["lre"], ["t3"], out=t3[:], in0=lre[:], in1=lre[:], op=ALU.mult)
    OP(P, "dve", "tensor_tensor", ["lim"], ["cc"], out=cc[:], in0=lim[:], in1=lim[:], op=ALU.mult)
    OP(P, "dve", "tensor_tensor", ["t3", "cc"], ["t3"], out=t3[:], in0=t3[:], in1=cc[:], op=ALU.add)
    OP(P, "dve", "reciprocal", ["t3"], ["t3"], out=t3[:], in_=t3[:])
    OP(P, "dve", "tensor_tensor", ["t1", "lre"], ["fre"], out=fre[:], in0=t1[:], in1=lre[:], op=ALU.mult)
    OP(P, "dve", "tensor_tensor", ["t2", "lim"], ["cc"], out=cc[:], in0=t2[:], in1=lim[:], op=ALU.mult)
    OP(P, "dve", "tensor_tensor", ["fre", "cc"], ["fre"], out=fre[:], in0=fre[:], in1=cc[:], op=ALU.add)
    OP(P, "dve", "tensor_tensor", ["fre", "t3"], ["fre"], out=fre[:], in0=fre[:], in1=t3[:], op=ALU.mult)
    OP(P, "dve", "tensor_tensor", ["t2", "lre"], ["fim"], out=fim[:], in0=t2[:], in1=lre[:], op=ALU.mult)
    OP(P, "dve", "tensor_tensor", ["t1", "lim"], ["cc"], out=cc[:], in0=t1[:], in1=lim[:], op=ALU.mult)
    OP(P, "dve", "tensor_tensor", ["fim", "cc"], ["fim"], out=fim[:], in0=fim[:], in1=cc[:], op=ALU.subtract)
    OP(P, "dve", "tensor_tensor", ["fim", "t3"], ["fim"], out=fim[:], in0=fim[:], in1=t3[:], op=ALU.mult)

    def bc16(x):
        return x[:].unsqueeze(2).to_broadcast([128, 16, 16])

    bbr = sb("bbr", [128, 16, 16], F32); bbi = sb("bbi", [128, 16, 16], F32); tq = sb("tq", [128, 16, 16], F32)
    OP(P, "dve", "tensor_tensor", ["Bt", "fre"], ["bbr"], out=bbr[:], in0=Bt[:, 0], in1=bc16(fre), op=ALU.mult)
    OP(P, "dve", "tensor_tensor", ["Bt", "fim"], ["tq"], out=tq[:], in0=Bt[:, 1], in1=bc16(fim), op=ALU.mult)
    OP(P, "dve", "tensor_tensor", ["bbr", "tq"], ["bbr"], out=bbr[:], in0=bbr[:], in1=tq[:], op=ALU.subtract)
    OP(P, "dve", "tensor_tensor", ["Bt", "fre"], ["bbi"], out=bbi[:], in0=Bt[:, 1], in1=bc16(fre), op=ALU.mult)
    OP(P, "dve", "tensor_tensor", ["Bt", "fim"], ["tq"], out=tq[:], in0=Bt[:, 0], in1=bc16(fim), op=ALU.mult)
    OP(P, "dve", "tensor_tensor", ["bbi", "tq"], ["bbi"], out=bbi[:], in0=bbi[:], in1=tq[:], op=ALU.add)
    ccr = sb("ccr", [128, 16, 16], F32); cci = sb("cci", [128, 16, 16], F32)
    OP(P, "dve", "tensor_tensor", ["Ct", "c1"], ["ccr"], out=ccr[:], in0=Ct[:, 0], in1=bc16(c1), op=ALU.mult)
    OP(P, "dve", "tensor_tensor", ["Ct", "s1"], ["tq"], out=tq[:], in0=Ct[:, 1], in1=bc16(s1), op=ALU.mult)
    OP(P, "dve", "tensor_tensor", ["ccr", "tq"], ["ccr"], out=ccr[:], in0=ccr[:], in1=tq[:], op=ALU.add)
    OP(P, "dve", "tensor_tensor", ["Ct", "s1"], ["cci"], out=cci[:], in0=Ct[:, 0], in1=bc16(s1), op=ALU.mult)
    OP(P, "dve", "tensor_tensor", ["Ct", "c1"], ["tq"], out=tq[:], in0=Ct[:, 1], in1=bc16(c1), op=ALU.mult)
    OP(P, "dve", "tensor_tensor", ["cci", "tq"], ["cci"], out=cci[:], in0=cci[:], in1=tq[:], op=ALU.subtract)

    Mre = sb("Mre", [128, 16, 128], F32); Mim = sb("Mim", [128, 16, 128], F32)
    CTr = sb("CTr", [128, 16, 128], BF16); CTi = sb("CTi", [128, 16, 128], BF16)
    BTr = sb("BTr", [128, 16, 128], BF16); BTi = sb("BTi", [128, 16, 128], BF16)
    for (tl, nm) in ((Mre, "Mre"), (Mim, "Mim")):
        OP(P, "pool", "memset", [], [nm], tl[:], 0.0)
    for (tl, nm) in ((CTr, "CTr"), (CTi, "CTi")):
        OP(P, "pool", "memset", [], [nm], tl[:], 0.0)

    def place(dst, gl):
        full = dst[gl * 64:(gl + 1) * 64, :, :]
        t = full.tensor
        a = full.ap
        pstep = a[0][0]
        return bass.AP(t, full.offset + gl * 16, [[pstep, 64], [512, 4], [160, 4], [1, 16]])

    def src4(x, gl):
        return x[gl * 64:(gl + 1) * 64, :, :].rearrange("p (a b) q -> p a b q", a=4)

    for gl in range(2):
        OP(P, "dve", "tensor_copy", ["bbr", "Mre"], ["Mre"], out=place(Mre, gl), in_=src4(bbr, gl))
        OP(P, "dve", "tensor_copy", ["bbi", "Mim"], ["Mim"], out=place(Mim, gl), in_=src4(bbi, gl))
        OP(P, "dve", "tensor_copy", ["ccr", "CTr"], ["CTr"], out=place(CTr, gl), in_=src4(ccr, gl))
        OP(P, "dve", "tensor_copy", ["cci", "CTi"], ["CTi"], out=place(CTi, gl), in_=src4(cci, gl))
    for (M, mn, BT, bn) in ((Mre, "Mre", BTr, "BTr"), (Mim, "Mim", BTi, "BTi")):
        for g4 in range(4):
            bk = nb()
            for j in range(4):
                k = g4 * 4 + j
                OP(P, "pe", "matmul", [mn, "ident_f"], [("ps", bk)], ps[bk][:, j * 128:(j + 1) * 128], M[:, k, :], ident_f, start=True, stop=True)
            OP(P, "act", "activation", [("ps", bk)], [bn], out=BT[:, g4 * 4:g4 * 4 + 4, :],
               in_=ps[bk][:, :].rearrange("p (a b) -> p a b", a=4), func=AF.Copy)

    NTAB = LB + 1
    tabC = sb("tabC", [128, 16, NTAB], F32); tabS = sb("tabS", [128, 16, NTAB], F32)
    ta = sb("ta", [128, 16, 256], F32); tb_ = sb("tb", [128, 16, 256], F32)
    OP(P, "dve", "memset", [], ["tabC"], tabC[:, :, 0:1], 1.0)
    OP(P, "dve", "memset", [], ["tabS"], tabS[:, :, 0:1], 0.0)
    OP(P, "dve", "tensor_copy", ["c1"], ["cc"], out=cc[:], in_=c1[:])
    OP(P, "dve", "tensor_copy", ["s1"], ["ss"], out=ss[:], in_=s1[:])
    n = 1
    while n < NTAB:
        w = min(n, NTAB - n)

        def bcw(x, w=w):
            return x[:].unsqueeze(2).to_broadcast([128, 16, w])
        src_c = tabC[:, :, 0:w]
        src_s = tabS[:, :, 0:w]
        OP(P, "dve", "tensor_tensor", ["tabC", "cc"], ["ta"], out=ta[:, :, 0:w], in0=src_c, in1=bcw(cc), op=ALU.mult)
        OP(P, "dve", "tensor_tensor", ["tabS", "ss"], ["tb"], out=tb_[:, :, 0:w], in0=src_s, in1=bcw(ss), op=ALU.mult)
        OP(P, "dve", "tensor_tensor", ["ta", "tb"], ["tabC"], out=tabC[:, :, n:n + w], in0=ta[:, :, 0:w], in1=tb_[:, :, 0:w], op=ALU.subtract)
        OP(P, "dve", "tensor_tensor", ["tabC", "ss"], ["ta"], out=ta[:, :, 0:w], in0=src_c, in1=bcw(ss), op=ALU.mult)
        OP(P, "dve", "tensor_tensor", ["tabS", "cc"], ["tb"], out=tb_[:, :, 0:w], in0=src_s, in1=bcw(cc), op=ALU.mult)
        OP(P, "dve", "tensor_tensor", ["ta", "tb"], ["tabS"], out=tabS[:, :, n:n + w], in0=ta[:, :, 0:w], in1=tb_[:, :, 0:w], op=ALU.add)
        n += w
        if n < NTAB:
            OP(P, "dve", "tensor_tensor", ["cc", "ss"], ["t1"], out=t1[:], in0=cc[:], in1=ss[:], op=ALU.mult)
            OP(P, "dve", "tensor_tensor", ["cc"], ["t2"], out=t2[:], in0=cc[:], in1=cc[:], op=ALU.mult)
            OP(P, "dve", "tensor_tensor", ["ss"], ["t3"], out=t3[:], in0=ss[:], in1=ss[:], op=ALU.mult)
            OP(P, "dve", "tensor_tensor", ["t2", "t3"], ["cc"], out=cc[:], in0=t2[:], in1=t3[:], op=ALU.subtract)
            OP(P, "dve", "tensor_scalar", ["t1"], ["ss"], out=ss[:], in0=t1[:], scalar1=2.0, scalar2=None, op0=ALU.mult)

    ut = sb("ut", [128, 2, 4, LB], BF16)
    wk = [sb("wk%d" % i, [128, LB], F32) for i in range(4)]
    cre = sb("cre", [128, 2, LB], F32); cim = sb("cim", [128, 2, LB], F32)
    vre = sb("vre", [128, 2, LB], F32); vim = sb("vim", [128, 2, LB], F32)
    sre = sb("sre", [128, 2, LB], F32); sim = sb("sim", [128, 2, LB], F32)
    sreb = sb("sreb", [128, 2, LB], BF16); simb = sb("simb", [128, 2, LB], BF16)
    car = sb("car", [128, 16, 2], F32)
    yw = [sb("yw%d" % i, [128, LB], F32) for i in range(3)]
    yo = sb("yo", [128, 2, LB], BF16)
    OP(P, "dve", "memset", [], [("car", k, j) for k in range(16) for j in range(2)], car[:], 0.0)
    uv = uT_d.rearrange("(c p) t -> p c t", p=128)
    yv = yT_d.rearrange("(c p) t -> p c t", p=128)
    it = 0
    for tb in range(NTB):
        ub = tb % 2
        tsl = slice(tb * LB, (tb + 1) * LB)
        P.dma("sp", ut[:, ub, :, :], uv[:, :, tsl], writes=[("ut", ub)])
        for kc in range(4):
            by = nb()
            for kq in range(4):
                k = kc * 4 + kq
                s = it % 2
                it += 1
                br = nb()
                while br == by:
                    br = nb()
                bi = nb()
                while bi == by:
                    bi = nb()
                OP(P, "pe", "matmul", ["BTr", ("ut", ub)], [("ps", br)], ps[br][:, :], BTr[:, k, :], ut[:, ub, kc, :], start=True, stop=True)
                OP(P, "pe", "matmul", ["BTi", ("ut", ub)], [("ps", bi)], ps[bi][:, :], BTi[:, k, :], ut[:, ub, kc, :], start=True, stop=True)
                tc0 = tabC[:, k, 0:LB]; ts0 = tabS[:, k, 0:LB]
                tc1 = tabC[:, k, 1:LB + 1]; ts1 = tabS[:, k, 1:LB + 1]
                OP(P, "dve", "tensor_tensor", [("ps", br), "tabC"], [("wk", 0)], out=wk[0][:], in0=ps[br][:, :], in1=tc0, op=ALU.mult)
                OP(P, "dve", "tensor_tensor", [("ps", bi), "tabS"], [("wk", 1)], out=wk[1][:], in0=ps[bi][:, :], in1=ts0, op=ALU.mult)
                OP(P, "pool", "tensor_tensor", [("wk", 0), ("wk", 1)], [("cre", s)], out=cre[:, s, :], in0=wk[0][:], in1=wk[1][:], op=ALU.add)
                OP(P, "dve", "tensor_tensor", [("ps", bi), "tabC"], [("wk", 2)], out=wk[2][:], in0=ps[bi][:, :], in1=tc0, op=ALU.mult)
                OP(P, "dve", "tensor_tensor", [("ps", br), "tabS"], [("wk", 3)], out=wk[3][:], in0=ps[br][:, :], in1=ts0, op=ALU.mult)
                OP(P, "pool", "tensor_tensor", [("wk", 2), ("wk", 3)], [("cim", s)], out=cim[:, s, :], in0=wk[2][:], in1=wk[3][:], op=ALU.subtract)
                rb = rho[:, k:k + 1].to_broadcast([128, LB])
                OP(P, "dve", "tensor_tensor_scan", [("cre", s), "rho", ("car", k, 0)], [("vre", s)], out=vre[:, s, :], data0=rb, data1=cre[:, s, :],
                   initial=car[:, k, 0:1], op0=ALU.mult, op1=ALU.add)
                OP(P, "dve", "tensor_tensor_scan", [("cim", s), "rho", ("car", k, 1)], [("vim", s)], out=vim[:, s, :], data0=rb, data1=cim[:, s, :],
                   initial=car[:, k, 1:2], op0=ALU.mult, op1=ALU.add)
                OP(P, "pool", "tensor_tensor", [("vre", s), "tabC"], [("wk", 0)], out=wk[0][:], in0=vre[:, s, :], in1=tc1, op=ALU.mult)
                OP(P, "pool", "tensor_tensor", [("vim", s), "tabS"], [("wk", 1)], out=wk[1][:], in0=vim[:, s, :], in1=ts1, op=ALU.mult)
                OP(P, "pool", "tensor_tensor", [("wk", 0), ("wk", 1)], [("sre", s)], out=sre[:, s, :], in0=wk[0][:], in1=wk[1][:], op=ALU.subtract)
                OP(P, "pool", "tensor_tensor", [("vre", s), "tabS"], [("wk", 2)], out=wk[2][:], in0=vre[:, s, :], in1=ts1, op=ALU.mult)
                OP(P, "pool", "tensor_tensor", [("vim", s), "tabC"], [("wk", 3)], out=wk[3][:], in0=vim[:, s, :], in1=tc1, op=ALU.mult)
                OP(P, "pool", "tensor_tensor", [("wk", 2), ("wk", 3)], [("sim", s)], out=sim[:, s, :], in0=wk[2][:], in1=wk[3][:], op=ALU.add)
                OP(P, "act", "activation", [("sre", s)], [("car", k, 0)], out=car[:, k, 0:1], in_=sre[:, s, LB - 1:LB], func=AF.Copy)
                OP(P, "act", "activation", [("sim", s)], [("car", k, 1)], out=car[:, k, 1:2], in_=sim[:, s, LB - 1:LB], func=AF.Copy)
                OP(P, "act", "activation", [("sre", s)], [("sreb", s)], out=sreb[:, s, :], in_=sre[:, s, :], func=AF.Copy)
                OP(P, "act", "activation", [("sim", s)], [("simb", s)], out=simb[:, s, :], in_=sim[:, s, :], func=AF.Copy)
                OP(P, "pe", "matmul", ["CTr", ("sreb", s)], [("ps", by)], ps[by][:, :], CTr[:, k, :], sreb[:, s, :], start=(kq == 0), stop=False)
                OP(P, "pe", "matmul", ["CTi", ("simb", s)], [("ps", by)], ps[by][:, :], CTi[:, k, :], simb[:, s, :], start=False, stop=(kq == 3))
            yb = (tb * 4 + kc) % 2
            OP(P, "dve", "scalar_tensor_tensor", [("ps", by), ("ut", ub), "dcol"], [("yw", 0)], out=yw[0][:], in0=ut[:, ub, kc, :],
               scalar=dcol[:, kc:kc + 1], in1=ps[by][:, :], op0=ALU.mult, op1=ALU.add)
            if USE_GELU_TANH_AF:
                OP(P, "act", "activation", [("yw", 0)], [("yo", yb)], out=yo[:, yb, :], in_=yw[0][:], func=AF.Gelu_apprx_tanh)
            else:
                OP(P, "dve", "tensor_tensor", [("yw", 0)], [("yw", 1)], out=yw[1][:], in0=yw[0][:], in1=yw[0][:], op=ALU.mult)
                OP(P, "dve", "tensor_scalar", [("yw", 1)], [("yw", 1)], out=yw[1][:], in0=yw[1][:], scalar1=0.044715 * math.sqrt(2 / math.pi),
                   scalar2=math.sqrt(2 / math.pi), op0=ALU.mult, op1=ALU.add)
                OP(P, "dve", "tensor_tensor", [("yw", 0), ("yw", 1)], [("yw", 1)], out=yw[1][:], in0=yw[1][:], in1=yw[0][:], op=ALU.mult)
                OP(P, "act", "activation", [("yw", 1)], [("yw", 2)], out=yw[2][:], in_=yw[1][:], func=AF.Tanh)
                OP(P, "dve", "tensor_scalar", [("yw", 2)], [("yw", 2)], out=yw[2][:], in0=yw[2][:], scalar1=1.0, scalar2=0.5, op0=ALU.add, op1=ALU.mult)
                OP(P, "dve", "tensor_tensor", [("yw", 2), ("yw", 0)], [("yo", yb)], out=yo[:, yb, :], in0=yw[2][:], in1=yw[0][:], op=ALU.mult)
            P.dma("sp", yv[:, kc, tsl], yo[:, yb, :], reads=[("yo", yb)], writes=[("yT", tb, kc)])
    P.flush()
    es.close()


import ml_dtypes
from concourse.bass_utils import run_bass_kernel_spmd

BF16_NP = ml_dtypes.bfloat16
NCORES = 8
OFF = dict(aq=0, ak=2048, av=4096, az=6144, aa=8192, ab=8208, fq=8224, fk=10272, fv=12320, ff=14368)


def core_cols(c):
    cols = []
    for h in (2 * c, 2 * c + 1):
        for nm in ("aq", "ak", "av", "az"):
            cols += list(range(OFF[nm] + h * 128, OFF[nm] + (h + 1) * 128))
    for h in (2 * c, 2 * c + 1):
        for nm in ("fq", "fk", "fv"):
            cols += list(range(OFF[nm] + h * 128, OFF[nm] + (h + 1) * 128))
    cols += [OFF["aa"] + 2 * c, OFF["aa"] + 2 * c + 1, OFF["ab"] + 2 * c, OFF["ab"] + 2 * c + 1, OFF["ff"] + 2 * c, OFF["ff"] + 2 * c + 1]
    return np.array(cols)


def host_consts_A():
    cf = np.zeros((128, 8, 128), np.float32)
    p = np.arange(128)[:, None]
    c = np.arange(128)[None, :]
    same = (p // 64) == (c // 64)
    cf[:, 0, :] = np.eye(128)
    cf[:, 1, :] = (p < c)
    cf[:, 2, :] = np.where(p <= c, 0.0, -30000.0)
    cf[:, 3, :] = 1.0
    cf[:, 4, :] = (c < p) & same
    cf[:, 5, :] = (c >= p) & same
    cf[:, 6, :] = (c > p) & same
    r = np.ones((128, 128), np.float32)
    r[:, 0] = 0
    r[:, 64] = 0
    cf[:, 7, :] = r
    return cf, np.eye(128, dtype=np.float32).astype(BF16_NP)


def load_consts_A(P, nc, cpf, cpb):
    cf = nc.alloc_sbuf_tensor("c_f", [128, 8, 128], F32)
    cb = nc.alloc_sbuf_tensor("c_b", [128, 128], BF16)
    P.dma("sp", cf[:], cpf, writes=["cf"])
    P.dma("sp", cb[:], cpb, writes=["cb"])
    return dict(ident_f=cf[:, 0, :], lt64=cf[0:64, 1, 0:64], negmask=cf[:, 2, :], ones_f=cf[:, 3, :], maskL=cf[:, 4, :], maskUi=cf[:, 5, :],
                maskUs=cf[:, 6, :], rst=cf[0:64, 7, :], ident_bf=cb[:, :])


def gdn_par(inp, c):
    gp = np.zeros((128, 2, 16), np.float32)
    cw = inp["gdn_conv_w"][0]
    for hg in range(2):
        h = 2 * c + hg
        for i in range(3):
            gp[:, hg, 4 * i:4 * i + 4] = cw[:, i * 2048 + h * 128: i * 2048 + (h + 1) * 128].T
        gp[:, hg, 12] = inp["gdn_o_norm"][0]
        gp[:, hg, 13] = inp["gdn_A_log"][0][h]
        gp[:, hg, 14] = inp["gdn_dt_bias"][0][h]
    return gp


def s5_par(inp, c):
    G = slice(32 * c, 32 * c + 32)
    are = inp["s5_A_re"][0][G]
    aim = inp["s5_A_im"][0][G]
    ls = inp["s5_log_step"][0][G]
    apar = np.zeros((128, 16, 3), np.float32)
    bpk = np.zeros((128, 2, 16, 16), np.float32)
    cpk = np.zeros((128, 2, 16, 16), np.float32)
    Bre = inp["s5_B_re"][0][G]
    Bim = inp["s5_B_im"][0][G]
    Cre = inp["s5_C_re"][0][G]
    Cim = inp["s5_C_im"][0][G]
    for k in range(16):
        for gl in range(2):
            g = 2 * k + gl
            rows = slice(gl * 64, gl * 64 + 64)
            apar[rows, k, 0] = are[g]
            apar[rows, k, 1] = aim[g]
            apar[rows, k, 2] = ls[g]
            bpk[rows, 0, k, :] = Bre[g]
            bpk[rows, 1, k, :] = Bim[g]
            cpk[rows, 0, k, :] = Cre[g].T
            cpk[rows, 1, k, :] = Cim[g].T
    dcol = np.ascontiguousarray(inp["s5_D"][0][512 * c:512 * c + 512].reshape(4, 128).T)
    return apar, bpk, cpk, dcol


def _ein(nc, name, shape, dt=F32):
    return nc.dram_tensor(name, shape, dt, kind="ExternalInput").ap()


def _eout(nc, name, shape, dt=F32):
    return nc.dram_tensor(name, shape, dt, kind="ExternalOutput").ap()


def build_A():
    nc = bass.Bass("TRN2", target_bir_lowering=False)
    xT = _ein(nc, "xT", [4096, 8192])
    wsl = _ein(nc, "wsl", [4096, NCOL])
    g0col = _ein(nc, "g0col", [128, 32])
    gpar_d = _ein(nc, "gpar", [128, 2, 16])
    nfb_d = _ein(nc, "nfb", [128, 2])
    cpf = _ein(nc, "cpf", [128, 8, 128])
    cpb = _ein(nc, "cpb", [128, 128], BF16)
    oT = _eout(nc, "oT", [4, 128, 8192], BF16)
    gq = nc.dram_tensor("gq", [8, 128, 8192], F32).ap()
    fq = nc.dram_tensor("fq", [6, 128, 8192], BF16).ap()
    rows = nc.dram_tensor("rows", [6, 8192], F32).ap()
    P = Prog(nc)
    ps = [nc.alloc_psum_tensor("ps%d" % i, [128, 512], F32) for i in range(8)]
    cst = load_consts_A(P, nc, cpf, cpb)
    gpt = nc.alloc_sbuf_tensor("gpt", [128, 2, 16], F32)
    nfbt = nc.alloc_sbuf_tensor("nfbt", [128, 2], F32)
    P.dma("sp", gpt[:], gpar_d, writes=["gpar"])
    P.dma("sp", nfbt[:], nfb_d, writes=["nfb"])
    P.flush()
    emit_A1(P, nc, ps, xT=xT, wsl=wsl, g0col=g0col, gq=gq, fq=fq, rows=rows)
    for hg in range(2):
        emit_gdn_head(P, nc, ps, cst, gq_d=gq[4 * hg:4 * hg + 4], a_d=rows[hg:hg + 1, :], b_d=rows[2 + hg:3 + hg, :], gpar=gpt[:, hg, :],
                      oT_d=oT[hg], pfx="gd%d_" % hg)
    for hf in range(2):
        emit_fox_head(P, nc, ps, cst, qT_d=fq[3 * hf + 0], kT_d=fq[3 * hf + 1], vT_d=fq[3 * hf + 2], f_d=rows[4 + hf:5 + hf, :],
                      nfb_col=nfbt[0:64, hf:hf + 1], oT_d=oT[2 + hf], pfx="fx%d_" % hf)
    return nc


def build_tok(glu, with_hn):
    nc = bass.Bass("TRN2", target_bir_lowering=False)
    inT = _ein(nc, "inT", [D, TOK], BF16)
    xres = _ein(nc, "xres", [TOK, D])
    wm = [_ein(nc, "wm%d" % i, [D, D]) for i in range(2 if glu else 1)]
    gains = _ein(nc, "gains", [4, D])
    wg = _ein(nc, "wg", [D, HID])
    wu = _ein(nc, "wu", [D, HID])
    wd = _ein(nc, "wd", [HID, D])
    ident_d = _ein(nc, "ident_in", [128, 128], BF16)
    xout = _eout(nc, "xout", [TOK, D])
    hnT = _eout(nc, "hnT", [D, TOK], BF16) if with_hn else None
    scr = {k: nc.dram_tensor("scr_" + k, [TOK, D], F32).ap() for k in ("m", "x1", "f")}
    P = Prog(nc)
    ps = [nc.alloc_psum_tensor("ps%d" % i, [128, 512], F32) for i in range(8)]
    actT = nc.alloc_sbuf_tensor("actT", [128, 32, TOK], BF16)
    ident = nc.alloc_sbuf_tensor("ident", [128, 128], BF16)
    P.dma("sp", ident[:], ident_d, writes=["ident"])
    P.flush()
    emit_tok_block(P, nc, ps, actT, ident, inT=inT, xres=xres, wmix=wm,
                   gains=(gains[0:1, :], gains[1:2, :], gains[2:3, :], gains[3:4, :] if with_hn else None),
                   wg=wg, wu=wu, wd=wd, xout=xout, hnT_out=hnT, scr=scr, pfx="t_")
    return nc


def build_C():
    nc = bass.Bass("TRN2", target_bir_lowering=False)
    uT = _ein(nc, "uT", [512, 8192], BF16)
    apar = _ein(nc, "apar", [128, 16, 3])
    bpk = _ein(nc, "bpk", [128, 2, 16, 16])
    cpk = _ein(nc, "cpk", [128, 2, 16, 16])
    dcol = _ein(nc, "dcol", [128, 4])
    cpf = _ein(nc, "cpf", [128, 8, 128])
    cpb = _ein(nc, "cpb", [128, 128], BF16)
    yT = _eout(nc, "yT", [512, 8192], BF16)
    P = Prog(nc)
    ps = [nc.alloc_psum_tensor("ps%d" % i, [128, 512], F32) for i in range(8)]
    cst = load_consts_A(P, nc, cpf, cpb)
    P.flush()
    emit_s5(P, nc, ps, cst, uT_d=uT, apar_d=apar, b_d=bpk, c_d=cpk, dcol_d=dcol, yT_d=yT)
    return nc


def kernel(**inputs):
    inp = {k: np.asarray(v) for k, v in inputs.items()}
    x = inp["x"][0]
    g = inp["norm_g"]
    cf, cb = host_consts_A()
    cores = list(range(NCORES))
    xT = np.ascontiguousarray(x.T)
    w_in = inp["w_in"][0]
    g0col = np.ascontiguousarray(g[0, 0].reshape(32, 128).T)
    fb = inp["fox_f_bias"][0]
    maps = []
    for c in cores:
        maps.append(dict(xT=xT, wsl=np.ascontiguousarray(w_in[:, core_cols(c)]), g0col=g0col, gpar=gdn_par(inp, c),
                         nfb=np.tile(-fb[2 * c:2 * c + 2][None, :], (128, 1)).astype(np.float32), cpf=cf, cpb=cb))
    resA = run_bass_kernel_spmd(build_A(), maps, core_ids=cores)
    oT_full = np.zeros((4096, 8192), dtype=BF16_NP)
    for c in cores:
        o = resA.results[c]["oT"]
        for hg in range(2):
            h = 2 * c + hg
            oT_full[h * 128:(h + 1) * 128] = o[hg]
            oT_full[2048 + h * 128:2048 + (h + 1) * 128] = o[2 + hg]
    ident = cb
    gainsB = np.stack([g[0, 1], g[0, 2], g[0, 3], g[1, 0]]).astype(np.float32)
    maps = []
    for c in cores:
        sl = slice(c * TOK, (c + 1) * TOK)
        maps.append(dict(inT=np.ascontiguousarray(oT_full[:, sl]), xres=np.ascontiguousarray(x[sl]), wm0=inp["w_out"][0], gains=gainsB,
                         wg=inp["ffn_w_gate"][0], wu=inp["ffn_w_up"][0], wd=inp["ffn_w_down"][0], ident_in=ident))
    resB = run_bass_kernel_spmd(build_tok(False, True), maps, core_ids=cores)
    x2 = [resB.results[c]["xout"] for c in cores]
    hnT_full = np.concatenate([resB.results[c]["hnT"] for c in cores], axis=1)
    maps = []
    for c in cores:
        apar, bpk, cpk, dcol = s5_par(inp, c)
        maps.append(dict(uT=np.ascontiguousarray(hnT_full[512 * c:512 * c + 512]), apar=apar, bpk=bpk, cpk=cpk, dcol=dcol, cpf=cf, cpb=cb))
    resC = run_bass_kernel_spmd(build_C(), maps, core_ids=cores)
    yT_full = np.concatenate([resC.results[c]["yT"] for c in cores], axis=0)
    gainsD = np.stack([g[1, 1], g[1, 2], g[1, 3], g[1, 3]]).astype(np.float32)
    maps = []
    for c in cores:
        sl = slice(c * TOK, (c + 1) * TOK)
        maps.append(dict(inT=np.ascontiguousarray(yT_full[:, sl]), xres=x2[c], wm0=inp["s5_w_glu_a"][0], wm1=inp["s5_w_glu_b"][0], gains=gainsD,
                         wg=inp["ffn_w_gate"][1], wu=inp["ffn_w_up"][1], wd=inp["ffn_w_down"][1], ident_in=ident))
    resD = run_bass_kernel_spmd(build_tok(True, False), maps, core_ids=cores)
    out = np.concatenate([resD.results[c]["xout"] for c in cores], axis=0)
    return out[None].astype(np.float32)
```

```python
import contextlib
import math


import numpy as np
import concourse.bass as bass
import concourse.mybir as mybir

F32 = mybir.dt.float32
BF16 = mybir.dt.bfloat16
I32 = mybir.dt.int32
AF = mybir.ActivationFunctionType
ALU = mybir.AluOpType
AX = mybir.AxisListType

ENGS = ["pe", "dve", "act", "pool", "sp"]
ATTR = {"pe": "tensor", "dve": "vector", "act": "scalar", "pool": "gpsimd", "sp": "sync"}


class Prog:
    def __init__(self, nc, nslots=6):
        self.nc = nc
        self.sem = {e: nc.alloc_semaphore("s_" + e) for e in ENGS}
        self.base = {e: 0 for e in ENGS}
        self.queues = {}
        for qn, issuer in (("sp", "sp"), ("pool", "pool"), ("act", "act")):
            self.queues[qn] = dict(
                issuer=issuer,
                sems=[nc.alloc_semaphore("q_%s_%d" % (qn, i)) for i in range(nslots)],
                cnt=[0] * nslots,
                n=0,
            )
        self.nslots = nslots
        self.ccsems = []
        self.maxwait = {e: {} for e in ENGS}
        self._reset()
        allsems = list(self.sem.values()) + [s for Q in self.queues.values() for s in Q["sems"]]
        with nc.Block() as blk:
            def _clr(h):
                for s in allsems:
                    h.sem_clear(s)
            blk.gpsimd(_clr)

    def _reset(self):
        self.ops = []
        self.res = {}
        self.stream = {e: [] for e in ENGS}

    def _deps(self, reads, writes):
        deps = set()
        for r in reads:
            st = self.res.get(r)
            if st and st["w"] is not None:
                deps.add(st["w"])
        for w in writes:
            st = self.res.get(w)
            if st:
                if st["w"] is not None:
                    deps.add(st["w"])
                deps.update(st["r"])
        return deps

    def _commit(self, oid, reads, writes):
        for r in reads:
            st = self.res.setdefault(r, {"w": None, "r": []})
            st["r"].append(oid)
        for w in writes:
            self.res[w] = {"w": oid, "r": []}

    def op(self, eng, fn, reads=(), writes=()):
        deps = self._deps(reads, writes)
        oid = len(self.ops)
        self.ops.append(dict(eng=eng, fn=fn, deps=deps, kind="c", marked=False))
        self.stream[eng].append(oid)
        self._commit(oid, reads, writes)
        return oid

    def cc(self, kind, ins, outs, reads=(), writes=()):
        sem = self.nc.alloc_semaphore("cc_%d" % len(self.ccsems))
        self.ccsems.append(sem)
        deps = self._deps(reads, writes)
        oid = len(self.ops)
        self.ops.append(dict(eng="pool", kind="cc", cckind=kind, ins=ins, outs=outs, sem=sem, deps=deps))
        self.stream["pool"].append(oid)
        self._commit(oid, reads, writes)
        return oid

    def dma(self, q, out, in_, reads=(), writes=(), **kw):
        Q = self.queues[q]
        slot = Q["n"] % self.nslots
        Q["n"] += 1
        prev = Q["cnt"][slot]
        Q["cnt"][slot] += 1
        deps = self._deps(reads, writes)
        oid = len(self.ops)
        self.ops.append(dict(eng=Q["issuer"], kind="dma", q=q, slot=slot, val=Q["cnt"][slot] * 16,
                             prev=prev * 16, deps=deps, out=out, in_=in_, kw=kw))
        self.stream[Q["issuer"]].append(oid)
        self._commit(oid, reads, writes)
        return oid

    def flush(self):
        ops = self.ops
        for o in ops:
            for d in o["deps"]:
                p = ops[d]
                if p["kind"] == "c":
                    if o["kind"] == "cc" and False:
                        pass
                    if p["eng"] == o["eng"] and p["eng"] == "pe":
                        continue
                    p["marked"] = True
        cum = {e: self.base[e] for e in ENGS}
        for e in ENGS:
            for oid in self.stream[e]:
                o = ops[oid]
                if o["kind"] == "c" and o["marked"]:
                    cum[e] += 1
                    o["val"] = cum[e]
        prog = self
        nc = self.nc
        with nc.Block() as block:
            for e in ENGS:
                def body(h, e=e):
                    mw = prog.maxwait[e]
                    def wait(sem, key, val):
                        if mw.get(key, 0) >= val:
                            return
                        mw[key] = val
                        h.wait_ge(sem, val)
                    for oid in prog.stream[e]:
                        o = ops[oid]
                        for d in sorted(o["deps"]):
                            p = ops[d]
                            if p["kind"] == "cc":
                                assert e == "pool", "consume collectives only after a flush()"
                                continue
                            if p["kind"] == "dma":
                                Q = prog.queues[p["q"]]
                                wait(Q["sems"][p["slot"]], ("q", p["q"], p["slot"]), p["val"])
                            else:
                                if not p["marked"]:
                                    continue
                                if p["eng"] == e and e == "pe":
                                    continue
                                wait(prog.sem[p["eng"]], p["eng"], p["val"])
                        if o["kind"] == "dma":
                            Q = prog.queues[o["q"]]
                            if o["prev"] > 0:
                                wait(Q["sems"][o["slot"]], ("q", o["q"], o["slot"]), o["prev"])
                            kw = dict(o["kw"])
                            idx = kw.pop("idx", None)
                            if idx is not None:
                                ins = h.indirect_dma_start(out=o["out"], out_offset=None, in_=o["in_"],
                                                           in_offset=bass.IndirectOffsetOnAxis(ap=idx, axis=0))
                            else:
                                ins = h.dma_start(out=o["out"], in_=o["in_"], **kw)
                            ins.then_inc(Q["sems"][o["slot"]], 16)
                        elif o["kind"] == "cc":
                            h.sem_clear(o["sem"])
                            h.collective_compute(o["cckind"], ALU.bypass, replica_groups=[list(range(8))], ins=o["ins"],
                                                 outs=o["outs"]).then_inc(o["sem"])
                            h.wait_ge(o["sem"], 1)
                        else:
                            ins = o["fn"](h)
                            if o["marked"]:
                                ins.then_inc(prog.sem[e], 1)
                    for qn, Q in prog.queues.items():
                        if Q["issuer"] == e:
                            for s in range(prog.nslots):
                                if Q["cnt"][s] > 0:
                                    wait(Q["sems"][s], ("q", qn, s), Q["cnt"][s] * 16)
                getattr(block, ATTR[e])(body)
        for e in ENGS:
            self.base[e] = cum[e]
        self._reset()


D = 4096
HID = 11008
NHC = HID // 128
TOK = 1024
NTQ = TOK // 128
EPS = 1e-6


def bcast_rows(ap_row, nparts):
    return bass.AP(ap_row.tensor, ap_row.offset, [[0, nparts]] + [list(x) for x in ap_row.ap[1:]])


_UID = [0]


def emit_tok_block(P, nc, ps, actT, ident, *, inT, xres, wmix, gains, wg, wu, wd, xout, hnT_out, scr, pfx, gather=None):
    glu = len(wmix) == 2
    pfx0 = pfx
    ps_bf = [p.bitcast(BF16) for p in ps]
    NPART = 3
    parts = [(0, 29), (29, 58), (58, 86)]

    NB = 4
    CW = 256
    with nc.sbuf_tensor(pfx + "wbuf", [128, NB, 32, CW], BF16) as wbuf, \
         nc.sbuf_tensor(pfx + "stg", [128, 4, CW], F32) as stg, \
         nc.sbuf_tensor(pfx + "sig", [128, 4, CW], F32) as sig:
        if gather is None:
            inT_v = inT.rearrange("(k p) t -> p k t", p=128)
            for k4 in range(4):
                P.dma("sp", actT[:, k4 * 8:(k4 + 1) * 8, :], inT_v[:, k4 * 8:(k4 + 1) * 8, :], writes=[("actT", k4)])
        else:
            Gv, idx = gather
            for k in range(32):
                P.dma("pool", actT[:, k, :], Gv, writes=[("actT", k // 8)], idx=idx[:, k:k + 1])
        wv = [w.rearrange("(k p) n -> p k n", p=128) for w in wmix]
        li = 0
        gi = 0
        for db in range(D // CW):
            bufs = []
            for mi in range(len(wmix)):
                b = li % NB
                li += 1
                for hf in range(2):
                    P.dma("pool", wbuf[:, b, hf * 16:(hf + 1) * 16, :], wv[mi][:, hf * 16:(hf + 1) * 16, db * CW:(db + 1) * CW],
                          writes=[("wbuf", b, hf)])
                bufs.append(b)
            for tq in range(NTQ):
                banks = []
                for mi in range(len(wmix)):
                    bank = (gi * len(wmix) + mi) % 8
                    banks.append(bank)
                    b = bufs[mi]
                    for kc in range(32):
                        P.op("pe", lambda h, bank=bank, b=b, kc=kc, tq=tq: h.matmul(
                            ps[bank][:, 0:CW], actT[:, kc, tq * 128:(tq + 1) * 128], wbuf[:, b, kc, :],
                            start=(kc == 0), stop=(kc == 31)),
                            reads=[("actT", kc // 8), ("wbuf", b, kc // 16)], writes=[("ps", bank)])
                s = gi % 4
                if glu:
                    P.op("act", lambda h, s=s, bk=banks[1]: h.activation(out=sig[:, s, :], in_=ps[bk][:, 0:CW], func=AF.Sigmoid),
                         reads=[("ps", banks[1])], writes=[("sig", s)])
                    P.op("dve", lambda h, s=s, bk=banks[0]: h.tensor_tensor(out=stg[:, s, :], in0=ps[bk][:, 0:CW], in1=sig[:, s, :], op=ALU.mult),
                         reads=[("ps", banks[0]), ("sig", s)], writes=[("stg", s)])
                else:
                    if gi % 2 == 0:
                        P.op("act", lambda h, s=s, bk=banks[0]: h.activation(out=stg[:, s, :], in_=ps[bk][:, 0:CW], func=AF.Copy),
                             reads=[("ps", banks[0])], writes=[("stg", s)])
                    else:
                        P.op("dve", lambda h, s=s, bk=banks[0]: h.tensor_copy(out=stg[:, s, :], in_=ps[bk][:, 0:CW]),
                             reads=[("ps", banks[0])], writes=[("stg", s)])
                P.dma("sp", scr["m"][tq * 128:(tq + 1) * 128, db * CW:(db + 1) * CW], stg[:, s, :], reads=[("stg", s)],
                      writes=[("m", tq)])
                gi += 1
        P.flush()

    def norm_pass(srcs, res_ap, g_post, g_next, x_store, to_actT, hn_dst):
        _UID[0] += 1
        pfx = pfx0 + "n%d_" % _UID[0]
        with nc.sbuf_tensor(pfx + "mt", [128, 2, D], F32) as mt, \
             nc.sbuf_tensor(pfx + "xt", [128, 2, D], F32) as xt, \
             nc.sbuf_tensor(pfx + "gb", [128, 2, D], F32) as gb, \
             nc.sbuf_tensor(pfx + "junk", [128, D], BF16) as junk, \
             nc.sbuf_tensor(pfx + "hn", [128, D], BF16) as hn, \
             nc.sbuf_tensor(pfx + "hst", [128, 2, 4, 128], BF16) as hst, \
             nc.sbuf_tensor(pfx + "st", [128, 8], F32) as st, \
             nc.sbuf_tensor(pfx + "cst", [128, 2], F32) as cst:
            P.dma("sp", gb[:, 0, :], bcast_rows(g_post, 128), writes=[("gb", 0)])
            if g_next is not None:
                P.dma("sp", gb[:, 1, :], bcast_rows(g_next, 128), writes=[("gb", 1)])
            P.op("dve", lambda h: h.memset(cst[:, 0:1], -0.5), writes=["cst"])
            ev = 0
            for tq in range(NTQ):
                b = tq % 2
                rows = slice(tq * 128, (tq + 1) * 128)
                P.dma("sp", mt[:, b, :], srcs[0][rows, :], writes=[("mt", b)])
                P.dma("sp", xt[:, b, :], res_ap[rows, :], writes=[("xt", b)])
                for extra in srcs[1:]:
                    pass
                if len(srcs) > 1:
                    raise NotImplementedError
                P.op("act", lambda h, b=b: h.activation(out=junk[:], in_=mt[:, b, :], func=AF.Square, accum_out=st[:, 0:1]),
                     reads=[("mt", b)], writes=["junk", ("st", 0)])
                P.op("dve", lambda h: h.tensor_scalar(out=st[:, 1:2], in0=st[:, 0:1], scalar1=1.0 / D, scalar2=EPS, op0=ALU.mult, op1=ALU.add),
                     reads=[("st", 0)], writes=[("st", 1)])
                P.op("pool", lambda h: h.tensor_tensor(out=st[:, 2:3], in0=st[:, 1:2], in1=cst[:, 0:1], op=ALU.pow),
                     reads=[("st", 1), "cst"], writes=[("st", 2)])
                P.op("dve", lambda h, b=b: h.scalar_tensor_tensor(out=mt[:, b, :], in0=mt[:, b, :], scalar=st[:, 2:3], in1=gb[:, 0, :],
                                                                 op0=ALU.mult, op1=ALU.mult),
                     reads=[("mt", b), ("st", 2), ("gb", 0)], writes=[("mt", b)])
                P.op("pool", lambda h, b=b: h.tensor_tensor(out=mt[:, b, :], in0=mt[:, b, :], in1=xt[:, b, :], op=ALU.add),
                     reads=[("mt", b), ("xt", b)], writes=[("mt", b)])
                P.dma("sp", x_store[rows, :], mt[:, b, :], reads=[("mt", b)], writes=[("xs", tq)])
                if g_next is None:
                    continue
                P.op("act", lambda h, b=b: h.activation(out=junk[:], in_=mt[:, b, :], func=AF.Square, accum_out=st[:, 3:4]),
                     reads=[("mt", b)], writes=["junk", ("st", 3)])
                P.op("dve", lambda h: h.tensor_scalar(out=st[:, 4:5], in0=st[:, 3:4], scalar1=1.0 / D, scalar2=EPS, op0=ALU.mult, op1=ALU.add),
                     reads=[("st", 3)], writes=[("st", 4)])
                P.op("pool", lambda h: h.tensor_tensor(out=st[:, 5:6], in0=st[:, 4:5], in1=cst[:, 0:1], op=ALU.pow),
                     reads=[("st", 4), "cst"], writes=[("st", 5)])
                P.op("dve", lambda h, b=b: h.scalar_tensor_tensor(out=hn[:], in0=mt[:, b, :], scalar=st[:, 5:6], in1=gb[:, 1, :],
                                                                 op0=ALU.mult, op1=ALU.mult),
                     reads=[("mt", b), ("st", 5), ("gb", 1)], writes=["hn"])
                for k4 in range(8):
                    bank = ev % 8
                    for j in range(4):
                        kc = k4 * 4 + j
                        P.op("pe", lambda h, bank=bank, j=j, kc=kc: h.transpose(
                            out=ps_bf[bank][:, j * 128:(j + 1) * 128], in_=hn[:, kc * 128:(kc + 1) * 128], identity=ident[:]),
                            reads=["hn", "ident"], writes=[("ps", bank)])
                    src = ps_bf[bank][:, 0:512].rearrange("p (a b) -> p a b", a=4)
                    if to_actT:
                        dst = actT[:, k4 * 4:(k4 + 1) * 4, tq * 128:(tq + 1) * 128]
                        wr = [("actT", k4 // 2, tq)]
                    else:
                        hb = ev % 2
                        dst = hst[:, hb, :, :]
                        wr = [("hst", hb)]
                    if ev % 2 == 0:
                        P.op("act", lambda h, dst=dst, src=src: h.activation(out=dst, in_=src, func=AF.Copy),
                             reads=[("ps", bank)], writes=wr)
                    else:
                        P.op("dve", lambda h, dst=dst, src=src: h.tensor_copy(out=dst, in_=src),
                             reads=[("ps", bank)], writes=wr)
                    if not to_actT:
                        dv = hn_dst.rearrange("(k p) t -> p k t", p=128)[:, k4 * 4:(k4 + 1) * 4, tq * 128:(tq + 1) * 128]
                        P.dma("sp", dv, hst[:, hb, :, :], reads=[("hst", hb)], writes=[("hno", k4, tq)])
                    ev += 1
            P.flush()

    norm_pass([scr["m"]], xres, gains[0], gains[1], scr["x1"], True, None)

    wgv = wg.rearrange("(k p) n -> p k n", p=128)
    wuv = wu.rearrange("(k p) n -> p k n", p=128)
    wdv = wd.rearrange("(c p) n -> p c n", p=128)
    with nc.sbuf_tensor(pfx + "hT", [128, 29, TOK], BF16) as hT, \
         nc.sbuf_tensor(pfx + "wgb", [128, 2, 32, 128], BF16) as wgb, \
         nc.sbuf_tensor(pfx + "wub", [128, 2, 32, 128], BF16) as wub, \
         nc.sbuf_tensor(pfx + "wdb", [128, 3, 4, 512], BF16) as wdb, \
         nc.sbuf_tensor(pfx + "sg", [128, 2, 512], BF16) as sg, \
         nc.sbuf_tensor(pfx + "fst", [128, 4, 512], F32) as fst:
        hci = 0
        wdi = 0
        evi = 0
        for pi, (h0, h1) in enumerate(parts):
            nh = h1 - h0
            for hl in range(nh):
                hc = h0 + hl
                b = hci % 2
                hci += 1
                for hf in range(2):
                    P.dma("pool", wgb[:, b, hf * 16:(hf + 1) * 16, :], wgv[:, hf * 16:(hf + 1) * 16, hc * 128:(hc + 1) * 128],
                          writes=[("wgb", b, hf)])
                    P.dma("pool", wub[:, b, hf * 16:(hf + 1) * 16, :], wuv[:, hf * 16:(hf + 1) * 16, hc * 128:(hc + 1) * 128],
                          writes=[("wub", b, hf)])
                for tt in range(2):
                    gbank = tt * 2
                    ubank = tt * 2 + 1
                    for (bank, wb, nm) in ((gbank, wgb, "wgb"), (ubank, wub, "wub")):
                        for kc in range(32):
                            P.op("pe", lambda h, bank=bank, wb=wb, b=b, kc=kc, tt=tt: h.matmul(
                                ps[bank][:, :], wb[:, b, kc, :], actT[:, kc, tt * 512:(tt + 1) * 512],
                                start=(kc == 0), stop=(kc == 31)),
                                reads=[(nm, b, kc // 16)] + [("actT", kc // 8, tt * 4 + q) for q in range(4)], writes=[("ps", bank)])
                    P.op("act", lambda h, tt=tt, gbank=gbank: h.activation(out=sg[:, tt, :], in_=ps[gbank][:, :], func=AF.Silu),
                         reads=[("ps", gbank)], writes=[("sg", tt)])
                    P.op("dve", lambda h, tt=tt, ubank=ubank, hl=hl: h.tensor_tensor(
                        out=hT[:, hl, tt * 512:(tt + 1) * 512], in0=ps[ubank][:, :], in1=sg[:, tt, :], op=ALU.mult),
                        reads=[("ps", ubank), ("sg", tt)], writes=[("hT", hl, tt)])
            groups = [(a, min(a + 4, nh)) for a in range(0, nh, 4)]
            for db in range(8):
                for (a0, a1) in groups:
                    wb_i = wdi % 3
                    wdi += 1
                    P.dma("pool", wdb[:, wb_i, 0:a1 - a0, :], wdv[:, h0 + a0:h0 + a1, db * 512:(db + 1) * 512], writes=[("wdb", wb_i)])
                    for hl in range(a0, a1):
                        for tq in range(NTQ):
                            P.op("pe", lambda h, tq=tq, hl=hl, wb_i=wb_i, a0=a0: h.matmul(
                                ps[tq][:, :], hT[:, hl, tq * 128:(tq + 1) * 128], wdb[:, wb_i, hl - a0, :],
                                start=(hl == 0), stop=(hl == nh - 1)),
                                reads=[("hT", hl, tq // 4), ("wdb", wb_i)], writes=[("ps", tq)])
                for tq in range(NTQ):
                    s = evi % 4
                    if evi % 2 == 0:
                        P.op("act", lambda h, s=s, tq=tq: h.activation(out=fst[:, s, :], in_=ps[tq][:, :], func=AF.Copy),
                             reads=[("ps", tq)], writes=[("fst", s)])
                    else:
                        P.op("dve", lambda h, s=s, tq=tq: h.tensor_copy(out=fst[:, s, :], in_=ps[tq][:, :]),
                             reads=[("ps", tq)], writes=[("fst", s)])
                    evi += 1
                    dst = scr["f"][tq * 128:(tq + 1) * 128, db * 512:(db + 1) * 512]
                    if pi == 0:
                        P.dma("sp", dst, fst[:, s, :], reads=[("fst", s)], writes=[("f", tq, db)])
                    else:
                        P.dma("pool", dst, fst[:, s, :], reads=[("fst", s)], writes=[("f", tq, db)], accum_op=ALU.add)
        P.flush()

    norm_pass([scr["f"]], scr["x1"], gains[2], gains[3], xout, False, hnT_out)


T = 8192
NCOL = 1798


def emit_A1(P, nc, ps, *, xT, wsl, g0col, gq, fq, rows, pfx="a1_"):
    TW = 256
    NT = T // TW
    with nc.sbuf_tensor(pfx + "W", [128, 32, NCOL], BF16) as Wbf, \
         nc.sbuf_tensor(pfx + "wst", [128, 2, NCOL], F32) as wst, \
         nc.sbuf_tensor(pfx + "gc", [128, 32], F32) as gcol, \
         nc.sbuf_tensor(pfx + "xt", [128, 2, 32, TW], BF16) as xt, \
         nc.sbuf_tensor(pfx + "sq", [128, 4, TW], BF16) as sq, \
         nc.sbuf_tensor(pfx + "ones", [128, 128], BF16) as ones, \
         nc.sbuf_tensor(pfx + "rb", [128, 2, TW], F32) as rb, \
         nc.sbuf_tensor(pfx + "stf", [128, 4, TW], F32) as stf, \
         nc.sbuf_tensor(pfx + "stb", [128, 4, TW], BF16) as stb:
        P.dma("sp", gcol[:], g0col, writes=["gcol"])
        P.op("pool", lambda h: h.memset(ones[:], 1.0), writes=["ones"])
        wv = wsl.rearrange("(k p) n -> p k n", p=128)
        for kc in range(32):
            b = kc % 2
            P.dma("sp", wst[:, b, :], wv[:, kc, :], writes=[("wst", b)])
            P.op("dve", lambda h, b=b, kc=kc: h.tensor_scalar(out=Wbf[:, kc, :], in0=wst[:, b, :], scalar1=gcol[:, kc:kc + 1], scalar2=None,
                                                             op0=ALU.mult),
                 reads=[("wst", b), "gcol"], writes=[("W", kc)])
        xv = xT.rearrange("(k p) t -> p k t", p=128)
        bankc = 0
        sti = 0
        for it in range(NT):
            xb = it % 2
            cols = slice(it * TW, (it + 1) * TW)
            for k4 in range(4):
                P.dma("pool", xt[:, xb, k4 * 8:(k4 + 1) * 8, :], xv[:, k4 * 8:(k4 + 1) * 8, cols], writes=[("xt", xb, k4)])
            for kc in range(32):
                s = kc % 4
                P.op("act", lambda h, s=s, kc=kc, xb=xb: h.activation(out=sq[:, s, :], in_=xt[:, xb, kc, :], func=AF.Square),
                     reads=[("xt", xb, kc // 8)], writes=[("sq", s)])
                P.op("pe", lambda h, s=s, kc=kc: h.matmul(ps[7][:, 0:TW], ones[:], sq[:, s, :], start=(kc == 0), stop=(kc == 31)),
                     reads=[("sq", s), "ones"], writes=[("ps", 7)])
            r = it % 2
            P.op("dve", lambda h, r=r: h.tensor_scalar(out=rb[:, r, :], in0=ps[7][:, 0:TW], scalar1=1.0 / D, scalar2=EPS, op0=ALU.mult, op1=ALU.add),
                 reads=[("ps", 7)], writes=[("rb", r)])
            P.op("act", lambda h, r=r: h.activation(out=rb[:, r, :], in_=rb[:, r, :], func=AF.Sqrt), reads=[("rb", r)], writes=[("rb", r)])
            P.op("dve", lambda h, r=r: h.reciprocal(out=rb[:, r, :], in_=rb[:, r, :]), reads=[("rb", r)], writes=[("rb", r)])
            for cg in range(15):
                bank = bankc % 7
                bankc += 1
                M = 128 if cg < 14 else 6
                for kc in range(32):
                    P.op("pe", lambda h, bank=bank, kc=kc, cg=cg, M=M, xb=xb: h.matmul(
                        ps[bank][0:M, 0:TW], Wbf[:, kc, cg * 128:cg * 128 + M], xt[:, xb, kc, :], start=(kc == 0), stop=(kc == 31)),
                        reads=[("W", kc), ("xt", xb, kc // 8)], writes=[("ps", bank)])
                s = sti % 4
                sti += 1
                if 8 <= cg < 14:
                    P.op("dve", lambda h, s=s, bank=bank, r=r: h.tensor_tensor(out=stb[:, s, :], in0=ps[bank][:, 0:TW], in1=rb[:, r, :], op=ALU.mult),
                         reads=[("ps", bank), ("rb", r)], writes=[("stb", s)])
                    P.dma("sp", fq[cg - 8, :, cols], stb[:, s, :], reads=[("stb", s)], writes=[("fq", cg, it)])
                else:
                    P.op("dve", lambda h, s=s, bank=bank, r=r, M=M: h.tensor_tensor(out=stf[0:M, s, :], in0=ps[bank][0:M, 0:TW], in1=rb[0:M, r, :],
                                                                                    op=ALU.mult),
                         reads=[("ps", bank), ("rb", r)], writes=[("stf", s)])
                    if cg < 8:
                        P.dma("sp", gq[cg, :, cols], stf[:, s, :], reads=[("stf", s)], writes=[("gq", cg, it)])
                    else:
                        P.dma("sp", rows[:, cols], stf[0:6, s, :], reads=[("stf", s)], writes=[("rows", it)])
        P.flush()


NKB = T // 128


def emit_fox_head(P, nc, ps, cst, *, qT_d, kT_d, vT_d, f_d, nfb_col, oT_d, pfx):
    scale = 128 ** -0.5
    ps_bf = [p.bitcast(BF16) for p in ps]
    with nc.sbuf_tensor(pfx + "qT", [128, T], BF16) as qT, \
         nc.sbuf_tensor(pfx + "kT", [128, T], BF16) as kT, \
         nc.sbuf_tensor(pfx + "va", [128, NKB, 130], BF16) as va, \
         nc.sbuf_tensor(pfx + "cq", [128, T], F32) as cq, \
         nc.sbuf_tensor(pfx + "ck", [128, NKB], F32) as ck, \
         nc.sbuf_tensor(pfx + "c1", [64, 128], F32) as c1, \
         nc.sbuf_tensor(pfx + "c2", [64, 128], F32) as c2, \
         nc.sbuf_tensor(pfx + "car", [64, 1], F32) as car, \
         nc.sbuf_tensor(pfx + "tmp", [128, 4, 512], F32) as tmp, \
         nc.sbuf_tensor(pfx + "pT", [128, 4, 512], BF16) as pT, \
         nc.sbuf_tensor(pfx + "on", [128, 2, 128], BF16) as on, \
         nc.sbuf_tensor(pfx + "rs", [128, 4], F32) as rs, \
         nc.sbuf_tensor(pfx + "ost", [128, 2, 512], BF16) as ost:
        for hf in range(2):
            sl = slice(hf * 4096, (hf + 1) * 4096)
            P.dma("sp", qT[:, sl], qT_d[:, sl], writes=[("qT", hf)])
            P.dma("sp", kT[:, sl], kT_d[:, sl], writes=[("kT", hf)])
        vT = cq.bitcast(BF16)
        P.dma("sp", vT[:, 0:T], vT_d, writes=["cq"])
        P.op("pool", lambda h: h.memset(va[:, :, 128:130], 1.0), writes=[("va1",)])
        for g in range(NKB // 4):
            bank = 4 + g % 4
            for j in range(4):
                kb = g * 4 + j
                P.op("pe", lambda h, bank=bank, j=j, kb=kb: h.transpose(out=ps_bf[bank][:, j * 128:(j + 1) * 128],
                                                                       in_=vT[:, kb * 128:(kb + 1) * 128], identity=cst["ident_bf"][:]),
                     reads=["cq", "ident_bf"], writes=[("ps", bank)])
            src = ps_bf[bank][:, 0:512].rearrange("p (a b) -> p a b", a=4)
            eng = "act" if g % 2 == 0 else "dve"
            if eng == "act":
                P.op("act", lambda h, g=g, src=src: h.activation(out=va[:, g * 4:(g + 1) * 4, 0:128], in_=src, func=AF.Copy),
                     reads=[("ps", bank)], writes=[("va", g)])
            else:
                P.op("dve", lambda h, g=g, src=src: h.tensor_copy(out=va[:, g * 4:(g + 1) * 4, 0:128], in_=src),
                     reads=[("ps", bank)], writes=[("va", g)])
        P.dma("sp", c1[:], f_d.rearrange("o (k p) -> (o k) p", p=128), writes=["c1"])
        P.op("act", lambda h: h.activation(out=c1[:], in_=c1[:], func=AF.Exp, scale=-1.0, bias=nfb_col), reads=["c1", "nfb"], writes=["c1"])
        P.op("act", lambda h: h.activation(out=c1[:], in_=c1[:], func=AF.Ln, scale=1.0, bias=1.0), reads=["c1"], writes=["c1"])
        P.op("dve", lambda h: h.tensor_tensor_scan(out=c2[:], data0=cst["ones_f"][0:64, :], data1=c1[:], initial=0.0, op0=ALU.mult, op1=ALU.add),
             reads=["c1", "ones_f"], writes=["c2"])
        P.op("pe", lambda h: h.matmul(ps[0][0:64, 0:1], cst["lt64"][:], c2[:, 127:128], start=True, stop=True),
             reads=["c2", "lt64"], writes=[("ps", 0)])
        P.op("dve", lambda h: h.tensor_copy(out=car[:], in_=ps[0][0:64, 0:1]), reads=[("ps", 0)], writes=["car"])
        P.op("dve", lambda h: h.tensor_scalar(out=c2[:], in0=c2[:], scalar1=car[:, 0:1], scalar2=None, op0=ALU.add), reads=["c2", "car"], writes=["c2"])
        P.op("pe", lambda h: h.matmul(ps[1][:, 0:64], c2[:], cst["ident_f"][0:64, 0:64], start=True, stop=True),
             reads=["c2", "ident_f"], writes=[("ps", 1)])
        P.op("dve", lambda h: h.tensor_copy(out=ck[:], in_=ps[1][:, 0:64]), reads=[("ps", 1)], writes=["ck"])
        for g in range(NKB // 4):
            bank = 4 + g % 4
            for j in range(4):
                kb = g * 4 + j
                P.op("pe", lambda h, bank=bank, j=j, kb=kb: h.matmul(ps[bank][:, j * 128:(j + 1) * 128],
                                                                    cst["ident_f"][0:64, kb:kb + 1].to_broadcast([64, 128]), c2[:],
                                                                    start=True, stop=True),
                     reads=["c2", "ident_f"], writes=[("ps", bank)])
            P.op("act", lambda h, g=g, bank=bank: h.activation(out=cq[:, g * 512:(g + 1) * 512], in_=ps[bank][:, :], func=AF.Copy, scale=-1.0),
                 reads=[("ps", bank)], writes=["cq"])
        LA = 2
        blocks = []
        for qt in range(T // 512):
            for kb in range(4 * qt + 4):
                blocks.append((qt, kb))

        def emit_scores(it, qt, kb):
            j = kb - 4 * qt
            c0 = max(j, 0) * 128
            bank = 4 + it % 4
            s = it % 4
            q0 = qt * 512
            P.op("pe", lambda h: h.matmul(ps[bank][:, c0:512], kT[:, kb * 128:(kb + 1) * 128], qT[:, q0 + c0:q0 + 512], start=True, stop=True),
                 reads=[("kT", kb // 32), ("qT", qt // 8)], writes=[("ps", bank)])
            P.op("dve", lambda h: h.scalar_tensor_tensor(out=tmp[:, s, c0:512], in0=ps[bank][:, c0:512], scalar=scale, in1=cq[:, q0 + c0:q0 + 512],
                                                         op0=ALU.mult, op1=ALU.add),
                 reads=[("ps", bank), "cq"], writes=[("tmp", s)])
            if j >= 0:
                P.op("pool", lambda h: h.tensor_tensor(out=tmp[:, s, c0:c0 + 128], in0=tmp[:, s, c0:c0 + 128], in1=cst["negmask"][:], op=ALU.add),
                     reads=[("tmp", s), "negmask"], writes=[("tmp", s)])
            P.op("act", lambda h: h.activation(out=pT[:, s, c0:512], in_=tmp[:, s, c0:512], func=AF.Exp, bias=ck[:, kb:kb + 1], scale=1.0),
                 reads=[("tmp", s), "ck"], writes=[("pT", s)])

        def emit_pv(it, qt, kb):
            j = kb - 4 * qt
            s = it % 4
            for i in range(max(j, 0), 4):
                P.op("pe", lambda h, i=i: h.matmul(ps[i][:, 0:129], pT[:, s, i * 128:(i + 1) * 128], va[:, kb, 0:129],
                                                   start=(kb == 0), stop=(kb == 4 * qt + i)),
                     reads=[("pT", s), ("va", kb // 4), ("va1",)], writes=[("ps", i)])
            if kb == 4 * qt + 3:
                ob = qt % 2
                for i in range(4):
                    o2 = i % 2
                    P.op("dve", lambda h, i=i: h.reciprocal(out=rs[:, i:i + 1], in_=ps[i][:, 128:129]), reads=[("ps", i)], writes=[("rs", i)])
                    P.op("act", lambda h, i=i, o2=o2: h.activation(out=on[:, o2, :], in_=ps[i][:, 0:128], func=AF.Copy, scale=rs[:, i:i + 1]),
                         reads=[("ps", i), ("rs", i)], writes=[("on", o2)])
                    P.op("pe", lambda h, i=i, o2=o2: h.transpose(out=ps_bf[i][:, 512:640], in_=on[:, o2, :], identity=cst["ident_bf"][:]),
                         reads=[("on", o2), "ident_bf"], writes=[("ps", i)])
                    P.op("dve", lambda h, i=i: h.tensor_copy(out=ost[:, ob, i * 128:(i + 1) * 128], in_=ps_bf[i][:, 512:640]),
                         reads=[("ps", i)], writes=[("ost", ob)])
                P.dma("sp", oT_d[:, qt * 512:(qt + 1) * 512], ost[:, ob, :], reads=[("ost", ob)], writes=[("oT", qt)])

        for it, (qt, kb) in enumerate(blocks):
            emit_scores(it, qt, kb)
            if it >= LA:
                emit_pv(it - LA, *blocks[it - LA])
        for it in range(max(len(blocks) - LA, 0), len(blocks)):
            emit_pv(it, *blocks[it])
        P.flush()


SEG = 1024
NSEG = T // SEG
NM = SEG // 128
L2_EPS = 1e-6
RMS_EPS = 1e-6


def OP(P, eng, method, reads, writes, *args, **kw):
    return P.op(eng, lambda h: getattr(h, method)(*args, **kw), reads, writes)


def emit_gdn_head(P, nc, ps, cst, *, gq_d, a_d, b_d, gpar, oT_d, pfx):
    ps_bf = [p.bitcast(BF16) for p in ps]
    bankc = [0]

    def nb():
        b = bankc[0] % 8
        bankc[0] += 1
        return b

    es = contextlib.ExitStack()

    def sb(name, shape, dt):
        return es.enter_context(nc.sbuf_tensor(pfx + name, shape, dt))

    a_sc = sb("a_sc", [64, 128], F32); b_sc = sb("b_sc", [64, 128], F32)
    g_sc = sb("g_sc", [64, 128], F32); gc_sc = sb("gc_sc", [64, 128], F32)
    t_sc = sb("t_sc", [64, 128], F32); t2_sc = sb("t2_sc", [64, 128], F32)
    bg_sc = sb("bg_sc", [64, 128], F32); ekd_sc = sb("ekd_sc", [64, 128], F32)
    egl_sc = sb("egl_sc", [64, 2], F32); nea = sb("nea", [128, 1], F32)
    gccol = sb("gccol", [128, 64], F32); nbcol = sb("nbcol", [128, 64], F32); bcol = sb("bcol", [128, 64], F32)
    bgcol = sb("bgcol", [128, 64], F32); ekdcol = sb("ekdcol", [128, 64], F32)
    eglb = sb("eglb", [128, 128], F32)
    S32 = sb("S32", [128, 128], F32); Sbf = sb("Sbf", [128, 128], BF16)
    xr = [sb("xr%d" % i, [128, SEG + 3], F32) for i in range(3)]
    cv = [sb("cv%d" % i, [128, SEG], F32) for i in range(3)]
    sqt = sb("sqt", [128, SEG], F32); rinv = sb("rinv", [128, SEG], F32)
    qnT = sb("qnT", [128, SEG], BF16); knT = sb("knT", [128, SEG], BF16); vT = sb("vT", [128, SEG], BF16); qgT = sb("qgT", [128, SEG], BF16)
    gcb = sb("gcb", [128, SEG], F32); nbb = sb("nbb", [128, SEG], F32); egb = sb("egb", [128, SEG], F32)
    dif = sb("dif", [128, SEG], F32); e1 = sb("e1", [128, SEG], F32); e2 = sb("e2", [128, SEG], F32)
    dmti = sb("dmti", [128, SEG], F32); dmtb = sb("dmtb", [128, SEG], F32)
    Y = [sb("Y%d" % i, [128, SEG], BF16) for i in range(2)]
    YT = [sb("YT%d" % i, [128, SEG], BF16) for i in range(2)]
    P32 = sb("P32", [128, SEG], F32); Pbf = [sb("Pbf%d" % i, [128, SEG], BF16) for i in range(2)]
    attnT = sb("attnT", [128, SEG], BF16)
    kbg = sb("kbg", [128, NM, 128], BF16); kd = sb("kd", [128, NM, 128], BF16); vb = sb("vb", [128, NM, 128], BF16)
    u_sb = sb("u_sb", [128, NM, 128], F32); wT = sb("wT", [128, SEG], BF16)
    vn = sb("vn", [128, 2, 128], BF16)
    oT_sb = sb("oT_sb", [128, SEG], F32); zt = sb("zt", [128, SEG], F32); gout = sb("gout", [128, SEG], BF16)

    ident_f = cst["ident_f"]; ident_bf = cst["ident_bf"]; ones_f = cst["ones_f"]

    def v3(ap):
        return ap.rearrange("p (m j) -> p m j", m=NM)

    def colb(col, m0):
        return col[:, m0:m0 + NM].unsqueeze(2).to_broadcast([128, NM, 128])

    def maskb(mk):
        return mk.unsqueeze(1).to_broadcast([128, NM, 128])

    P.dma("sp", a_sc[:], a_d.rearrange("o (k p) -> (o k) p", p=128), writes=["a_sc"])
    P.dma("sp", b_sc[:], b_d.rearrange("o (k p) -> (o k) p", p=128), writes=["b_sc"])
    OP(P, "act", "activation", ["gpar"], ["nea"], out=nea[:], in_=gpar[:, 13:14], func=AF.Exp)
    OP(P, "dve", "tensor_scalar", ["nea"], ["nea"], out=nea[:], in0=nea[:], scalar1=-1.0, scalar2=None, op0=ALU.mult)
    OP(P, "act", "activation", ["a_sc", "gpar"], ["t_sc"], out=t_sc[:], in_=a_sc[:], func=AF.Exp, bias=gpar[0:64, 14:15], scale=1.0)
    OP(P, "act", "activation", ["t_sc"], ["t_sc"], out=t_sc[:], in_=t_sc[:], func=AF.Ln, bias=1.0, scale=1.0)
    OP(P, "dve", "tensor_scalar", ["t_sc", "nea"], ["g_sc"], out=g_sc[:], in0=t_sc[:], scalar1=nea[0:64, 0:1], scalar2=None, op0=ALU.mult)
    OP(P, "dve", "tensor_tensor_scan", ["g_sc", "rst"], ["gc_sc"], out=gc_sc[:], data0=cst["rst"], data1=g_sc[:], initial=0.0,
       op0=ALU.mult, op1=ALU.add)
    OP(P, "act", "activation", ["b_sc"], ["b_sc"], out=b_sc[:], in_=b_sc[:], func=AF.Sigmoid)
    OP(P, "act", "activation", ["gc_sc"], ["t_sc"], out=t_sc[:], in_=gc_sc[:], func=AF.Exp)
    OP(P, "dve", "tensor_tensor", ["t_sc", "b_sc"], ["bg_sc"], out=bg_sc[:], in0=t_sc[:], in1=b_sc[:], op=ALU.mult)
    gc3 = gc_sc[:].rearrange("p (a b) -> p a b", a=2)
    OP(P, "dve", "tensor_tensor", ["gc_sc"], ["t2_sc"], out=t2_sc[:].rearrange("p (a b) -> p a b", a=2),
       in0=gc3[:, :, 63:64].to_broadcast([64, 2, 64]), in1=gc3, op=ALU.subtract)
    OP(P, "act", "activation", ["t2_sc"], ["ekd_sc"], out=ekd_sc[:], in_=t2_sc[:], func=AF.Exp)
    OP(P, "act", "activation", ["gc_sc"], ["egl_sc"], out=egl_sc[:].unsqueeze(2), in_=gc3[:, :, 63:64], func=AF.Exp)
    for (src, srcn, dst, nm, sc) in ((gc_sc, "gc_sc", gccol, "gccol", 1.0), (b_sc, "b_sc", nbcol, "nbcol", -1.0), (b_sc, "b_sc", bcol, "bcol", 1.0),
                                     (bg_sc, "bg_sc", bgcol, "bgcol", 1.0), (ekd_sc, "ekd_sc", ekdcol, "ekdcol", 1.0)):
        bk = nb()
        OP(P, "pe", "matmul", [srcn, "ident_f"], [("ps", bk)], ps[bk][:, 0:64], src[:], ident_f[0:64, 0:64],
           start=True, stop=True)
        OP(P, "act", "activation", [("ps", bk)], [nm], out=dst[:], in_=ps[bk][:, 0:64], func=AF.Copy, scale=sc)
    bk = nb()
    for m in range(64):
        OP(P, "pe", "matmul", ["egl_sc", "ident_f"], [("ps", bk)], ps[bk][:, 2 * m:2 * m + 2], ident_f[0:64, m:m + 1].to_broadcast([64, 128]),
           egl_sc[:], start=True, stop=True)
    OP(P, "dve", "tensor_copy", [("ps", bk)], ["eglb"], out=eglb[:], in_=ps[bk][:, 0:128])
    OP(P, "dve", "memset", [], ["S32"], S32[:], 0.0)
    OP(P, "pool", "memset", [], ["Sbf"], Sbf[:], 0.0)

    names3 = ["q", "k", "v"]
    for sg in range(NSEG):
        t0 = sg * SEG
        m0 = sg * NM
        for i in range(3):
            if sg == 0:
                OP(P, "pool", "memset", [], [("xr", i)], xr[i][:, 0:3], 0.0)
                P.dma("sp", xr[i][:, 3:SEG + 3], gq_d[i, :, 0:SEG], writes=[("xr", i)])
            else:
                P.dma("sp", xr[i][:, :], gq_d[i, :, t0 - 3:t0 + SEG], writes=[("xr", i)])
            OP(P, "act", "activation", [("xr", i), "gpar"], [("cv", i)], out=cv[i][:], in_=xr[i][:, 3:SEG + 3], func=AF.Copy,
               scale=gpar[:, 4 * i + 3:4 * i + 4])
            for j in (2, 1, 0):
                OP(P, "dve", "scalar_tensor_tensor", [("xr", i), ("cv", i), "gpar"], [("cv", i)], out=cv[i][:], in0=xr[i][:, j:j + SEG],
                   scalar=gpar[:, 4 * i + j:4 * i + j + 1], in1=cv[i][:], op0=ALU.mult, op1=ALU.add)
            OP(P, "act", "activation", [("cv", i)], [("cv", i)], out=cv[i][:], in_=cv[i][:], func=AF.Silu)
        P.dma("sp", zt[:], gq_d[3, :, t0:t0 + SEG], writes=["zt"])
        OP(P, "pool", "tensor_copy", [("cv", 2)], ["vT"], out=vT[:], in_=cv[2][:])
        for (src, srcn, dst, dstn, sc) in ((gc_sc, "gc_sc", gcb, "gcb", 1.0), (b_sc, "b_sc", nbb, "nbb", -1.0)):
            for hf in range(NM // 4):
                bk = nb()
                for j in range(4):
                    m = m0 + hf * 4 + j
                    OP(P, "pe", "matmul", [srcn, "ident_f"], [("ps", bk)], ps[bk][:, j * 128:(j + 1) * 128],
                       ident_f[0:64, m:m + 1].to_broadcast([64, 128]), src[:], start=True, stop=True)
                OP(P, "act", "activation", [("ps", bk)], [dstn], out=dst[:, hf * 512:(hf + 1) * 512], in_=ps[bk][:, :], func=AF.Copy, scale=sc)
        OP(P, "act", "activation", ["gcb"], ["egb"], out=egb[:], in_=gcb[:], func=AF.Exp)
        for i in range(2):
            OP(P, "act", "activation", [("cv", i)], ["sqt"], out=sqt[:], in_=cv[i][:], func=AF.Square)
            for hf in range(SEG // 512):
                bk = nb()
                OP(P, "pe", "matmul", ["sqt", "ones_f"], [("ps", bk)], ps[bk][:, :], ones_f, sqt[:, hf * 512:(hf + 1) * 512], start=True, stop=True)
                OP(P, "dve", "tensor_scalar", [("ps", bk)], ["rinv"], out=rinv[:, hf * 512:(hf + 1) * 512], in0=ps[bk][:, :], scalar1=L2_EPS,
                   scalar2=None, op0=ALU.add)
            OP(P, "act", "activation", ["rinv"], ["rinv"], out=rinv[:], in_=rinv[:], func=AF.Sqrt)
            OP(P, "dve", "reciprocal", ["rinv"], ["rinv"], out=rinv[:], in_=rinv[:])
            if i == 0:
                OP(P, "dve", "scalar_tensor_tensor", [("cv", 0), "rinv"], ["qnT"], out=qnT[:], in0=cv[0][:], scalar=128 ** -0.5, in1=rinv[:],
                   op0=ALU.mult, op1=ALU.mult)
                OP(P, "dve", "tensor_tensor", ["qnT", "egb"], ["qgT"], out=qgT[:], in0=qnT[:], in1=egb[:], op=ALU.mult)
            else:
                OP(P, "dve", "tensor_tensor", [("cv", 1), "rinv"], ["knT"], out=knT[:], in0=cv[1][:], in1=rinv[:], op=ALU.mult)
        OP(P, "dve", "tensor_tensor", ["gccol", "gcb"], ["dif"], out=v3(dif[:]), in0=colb(gccol, m0), in1=v3(gcb[:]), op=ALU.subtract)
        OP(P, "dve", "tensor_scalar", ["dif"], ["e1"], out=e1[:], in0=dif[:], scalar1=0.0, scalar2=None, op0=ALU.min)
        OP(P, "act", "activation", ["e1"], ["e1"], out=e1[:], in_=e1[:], func=AF.Exp)
        OP(P, "pool", "tensor_tensor", ["e1", "maskL"], ["e1"], out=v3(e1[:]), in0=v3(e1[:]), in1=maskb(cst["maskL"]), op=ALU.mult)
        OP(P, "dve", "tensor_tensor", ["e1", "nbcol"], ["e1"], out=v3(e1[:]), in0=v3(e1[:]), in1=colb(nbcol, m0), op=ALU.mult)
        OP(P, "dve", "tensor_scalar", ["dif"], ["e2"], out=e2[:], in0=dif[:], scalar1=0.0, scalar2=None, op0=ALU.max)
        OP(P, "act", "activation", ["e2"], ["e2"], out=e2[:], in_=e2[:], func=AF.Exp, scale=-1.0)
        OP(P, "pool", "tensor_tensor", ["e2", "maskUi"], ["dmti"], out=v3(dmti[:]), in0=v3(e2[:]), in1=maskb(cst["maskUi"]), op=ALU.mult)
        OP(P, "pool", "tensor_tensor", ["nbb", "maskUs"], ["dmtb"], out=v3(dmtb[:]), in0=v3(nbb[:]), in1=maskb(cst["maskUs"]), op=ALU.mult)
        OP(P, "dve", "tensor_tensor", ["dmtb", "e2"], ["dmtb"], out=dmtb[:], in0=dmtb[:], in1=e2[:], op=ALU.mult)
        for hf in range(NM // 4):
            bkk = nb(); bqk = nb()
            for j in range(4):
                m = hf * 4 + j
                sl = slice(m * 128, (m + 1) * 128)
                OP(P, "pe", "matmul", ["knT"], [("ps", bkk)], ps[bkk][:, j * 128:(j + 1) * 128], knT[:, sl], knT[:, sl], start=True, stop=True)
                OP(P, "pe", "matmul", ["knT", "qnT"], [("ps", bqk)], ps[bqk][:, j * 128:(j + 1) * 128], knT[:, sl], qnT[:, sl], start=True, stop=True)
            hs = slice(hf * 512, (hf + 1) * 512)
            OP(P, "dve", "tensor_tensor", [("ps", bkk), "e1"], [("Y", 0)], out=Y[0][:, hs], in0=ps[bkk][:, :], in1=e1[:, hs], op=ALU.mult)
            OP(P, "dve", "tensor_tensor", [("ps", bkk), "dmtb"], [("YT", 0)], out=YT[0][:, hs], in0=ps[bkk][:, :], in1=dmtb[:, hs], op=ALU.mult)
            OP(P, "dve", "tensor_tensor", [("ps", bqk), "dmti"], ["attnT"], out=attnT[:, hs], in0=ps[bqk][:, :], in1=dmti[:, hs], op=ALU.mult)
        for hf in range(NM // 4):
            bk_k = nb(); bk_v = nb()
            for j in range(4):
                m = hf * 4 + j
                sl = slice(m * 128, (m + 1) * 128)
                OP(P, "pe", "transpose", ["knT", "ident_bf"], [("ps", bk_k)], out=ps_bf[bk_k][:, j * 128:(j + 1) * 128], in_=knT[:, sl], identity=ident_bf)
                OP(P, "pe", "transpose", ["vT", "ident_bf"], [("ps", bk_v)], out=ps_bf[bk_v][:, j * 128:(j + 1) * 128], in_=vT[:, sl], identity=ident_bf)
            ms = slice(hf * 4, hf * 4 + 4)
            srck = ps_bf[bk_k][:, 0:512].rearrange("p (a b) -> p a b", a=4)
            srcv = ps_bf[bk_v][:, 0:512].rearrange("p (a b) -> p a b", a=4)

            def cb4(col):
                return col[:, m0 + hf * 4:m0 + hf * 4 + 4].unsqueeze(2).to_broadcast([128, 4, 128])
            OP(P, "dve", "tensor_tensor", [("ps", bk_k), "bgcol"], ["kbg"], out=kbg[:, ms, :], in0=srck, in1=cb4(bgcol), op=ALU.mult)
            OP(P, "dve", "tensor_tensor", [("ps", bk_k), "ekdcol"], ["kd"], out=kd[:, ms, :], in0=srck, in1=cb4(ekdcol), op=ALU.mult)
            OP(P, "dve", "tensor_tensor", [("ps", bk_v), "bcol"], ["vb"], out=vb[:, ms, :], in0=srcv, in1=cb4(bcol), op=ALU.mult)
        OP(P, "dve", "tensor_tensor", [("YT", 0), "ident_f"], ["P32"], out=v3(P32[:]), in0=v3(YT[0][:]), in1=maskb(ident_f), op=ALU.add)
        OP(P, "act", "activation", ["P32"], [("Pbf", 0)], out=Pbf[0][:], in_=P32[:], func=AF.Copy)
        pcur = 0
        for lvl in range(1, 6):
            cu = (lvl - 1) % 2
            nx = lvl % 2
            for hf in range(NM // 4):
                hs = slice(hf * 512, (hf + 1) * 512)
                by = nb()
                for j in range(4):
                    sl = slice((hf * 4 + j) * 128, (hf * 4 + j + 1) * 128)
                    OP(P, "pe", "matmul", [("YT", cu), ("Y", cu)], [("ps", by)], ps[by][:, j * 128:(j + 1) * 128], YT[cu][:, sl], Y[cu][:, sl],
                       start=True, stop=True)
                OP(P, "act", "activation", [("ps", by)], [("Y", nx)], out=Y[nx][:, hs], in_=ps[by][:, :], func=AF.Copy)
                if lvl < 5:
                    byt = nb()
                    for j in range(4):
                        sl = slice((hf * 4 + j) * 128, (hf * 4 + j + 1) * 128)
                        OP(P, "pe", "matmul", [("YT", cu), ("Y", cu)], [("ps", byt)], ps[byt][:, j * 128:(j + 1) * 128], Y[cu][:, sl], YT[cu][:, sl],
                           start=True, stop=True)
                    OP(P, "dve", "tensor_copy", [("ps", byt)], [("YT", nx)], out=YT[nx][:, hs], in_=ps[byt][:, :])
            for hf in range(NM // 4):
                hs = slice(hf * 512, (hf + 1) * 512)
                bp = nb()
                for j in range(4):
                    sl = slice((hf * 4 + j) * 128, (hf * 4 + j + 1) * 128)
                    OP(P, "pe", "matmul", [("Y", nx), ("Pbf", pcur)], [("ps", bp)], ps[bp][:, j * 128:(j + 1) * 128], Y[nx][:, sl], Pbf[pcur][:, sl],
                       start=True, stop=True)
                OP(P, "dve", "tensor_tensor", [("ps", bp), "P32"], ["P32"], out=P32[:, hs], in0=ps[bp][:, :], in1=P32[:, hs], op=ALU.add)
            pn = 1 - pcur
            OP(P, "act", "activation", ["P32"], [("Pbf", pn)], out=Pbf[pn][:], in_=P32[:], func=AF.Copy)
            pcur = pn
        TT = Pbf[pcur]
        ttn = ("Pbf", pcur)
        for hf in range(NM // 4):
            bu = nb(); bw = nb()
            for j in range(4):
                m = hf * 4 + j
                sl = slice(m * 128, (m + 1) * 128)
                OP(P, "pe", "matmul", [ttn, "vb"], [("ps", bu)], ps[bu][:, j * 128:(j + 1) * 128], TT[:, sl], vb[:, m, :], start=True, stop=True)
                OP(P, "pe", "matmul", [ttn, "kbg"], [("ps", bw)], ps[bw][:, j * 128:(j + 1) * 128], kbg[:, m, :], TT[:, sl], start=True, stop=True)
            OP(P, "act", "activation", [("ps", bu)], ["u_sb"], out=u_sb[:, hf * 4:hf * 4 + 4, :],
               in_=ps[bu][:, :].rearrange("p (a b) -> p a b", a=4), func=AF.Copy)
            OP(P, "dve", "tensor_copy", [("ps", bw)], ["wT"], out=wT[:, hf * 512:(hf + 1) * 512], in_=ps[bw][:, :])
        for half in range(2):
            bo = nb()
            for cl in range(8):
                cc = half * 8 + cl
                m = cc // 2
                hh = cc % 2
                R = slice(64 * hh, 64 * hh + 64)
                C = slice(m * 128 + 64 * hh, m * 128 + 64 * hh + 64)
                cg = sg * 16 + cc
                bws = nb()
                while bws == bo:
                    bws = nb()
                OP(P, "pe", "matmul", ["wT", "Sbf"], [("ps", bws)], ps[bws][:, 0:128], wT[:, m * 128:(m + 1) * 128], Sbf[:], start=True, stop=True)
                vi = cc % 2
                OP(P, "dve", "tensor_tensor", ["u_sb", ("ps", bws)], [("vn", vi)], out=vn[R, vi, :], in0=u_sb[R, m, :], in1=ps[bws][R, 0:128],
                   op=ALU.subtract)
                OP(P, "pe", "matmul", ["Sbf", "qgT"], [("ps", bo)], ps[bo][:, cl * 64:(cl + 1) * 64], Sbf[:], qgT[:, C], start=True, stop=False)
                OP(P, "pe", "matmul", [("vn", vi), "attnT"], [("ps", bo)], ps[bo][:, cl * 64:(cl + 1) * 64], vn[R, vi, :], attnT[R, C],
                   start=False, stop=True)
                bsd = nb()
                while bsd == bo:
                    bsd = nb()
                OP(P, "pe", "matmul", ["kd", ("vn", vi)], [("ps", bsd)], ps[bsd][:, 0:128], kd[R, m, :], vn[R, vi, :], start=True, stop=True)
                OP(P, "dve", "scalar_tensor_tensor", ["S32", "eglb", ("ps", bsd)], ["S32"], out=S32[:], in0=S32[:], scalar=eglb[:, cg:cg + 1],
                   in1=ps[bsd][:, 0:128], op0=ALU.mult, op1=ALU.add)
                OP(P, "act", "activation", ["S32"], ["Sbf"], out=Sbf[:], in_=S32[:], func=AF.Copy)
            OP(P, "act", "activation", [("ps", bo)], ["oT_sb"], out=oT_sb[:, half * 512:(half + 1) * 512], in_=ps[bo][:, :], func=AF.Copy)
        OP(P, "act", "activation", ["oT_sb"], ["sqt"], out=sqt[:], in_=oT_sb[:], func=AF.Square)
        for hf in range(SEG // 512):
            bk = nb()
            OP(P, "pe", "matmul", ["sqt", "ones_f"], [("ps", bk)], ps[bk][:, :], ones_f, sqt[:, hf * 512:(hf + 1) * 512], start=True, stop=True)
            OP(P, "dve", "tensor_scalar", [("ps", bk)], ["rinv"], out=rinv[:, hf * 512:(hf + 1) * 512], in0=ps[bk][:, :], scalar1=1.0 / 128,
               scalar2=RMS_EPS, op0=ALU.mult, op1=ALU.add)
        OP(P, "act", "activation", ["rinv"], ["rinv"], out=rinv[:], in_=rinv[:], func=AF.Sqrt)
        OP(P, "dve", "reciprocal", ["rinv"], ["rinv"], out=rinv[:], in_=rinv[:])
        OP(P, "dve", "scalar_tensor_tensor", ["oT_sb", "rinv", "gpar"], ["oT_sb"], out=oT_sb[:], in0=oT_sb[:], scalar=gpar[:, 12:13], in1=rinv[:],
           op0=ALU.mult, op1=ALU.mult)
        OP(P, "act", "activation", ["zt"], ["zt"], out=zt[:], in_=zt[:], func=AF.Silu)
        OP(P, "dve", "tensor_tensor", ["oT_sb", "zt"], ["gout"], out=gout[:], in0=oT_sb[:], in1=zt[:], op=ALU.mult)
        P.dma("sp", oT_d[:, t0:t0 + SEG], gout[:], reads=["gout"], writes=[("oTd", sg)])
    P.flush()
    es.close()


LB = 512
NTB = T // LB
S5_MAX_RE = -1e-4
USE_GELU_TANH_AF = False


def OP(P, eng, method, reads, writes, *args, **kw):
    return P.op(eng, lambda h: getattr(h, method)(*args, **kw), reads, writes)


def emit_s5(P, nc, ps, cst, *, uT_d, apar_d, b_d, c_d, dcol_d, yT_d, pfx="s5_", gather=None):
    es = contextlib.ExitStack()
    uid = [0]

    def sb(name, shape, dt):
        return es.enter_context(nc.sbuf_tensor(pfx + name, shape, dt))

    def small(name):
        return sb(name, [128, 16], F32)

    bankc = [0]

    def nb():
        b = bankc[0] % 8
        bankc[0] += 1
        return b

    ident_f = cst["ident_f"]
    apar = sb("apar", [128, 16, 3], F32)
    Bt = sb("Bt", [128, 2, 16, 16], F32)
    Ct = sb("Ct", [128, 2, 16, 16], F32)
    dcol = sb("dcol", [128, 4], F32)
    P.dma("sp", apar[:], apar_d, writes=["apar"])
    P.dma("sp", Bt[:], b_d, writes=["Bt"])
    P.dma("sp", Ct[:], c_d, writes=["Ct"])
    P.dma("sp", dcol[:], dcol_d, writes=["dcol"])

    lre = small("lre"); lim = small("lim"); dt = small("dt"); rho = small("rho"); th = small("th")
    c1 = small("c1"); s1 = small("s1"); t1 = small("t1"); t2 = small("t2"); t3 = small("t3")
    fre = small("fre"); fim = small("fim"); cc = small("cc"); ss = small("ss")
    halfpi = sb("halfpi", [128, 1], F32)
    OP(P, "dve", "memset", [], ["halfpi"], halfpi[:], math.pi / 2)
    OP(P, "dve", "tensor_scalar", ["apar"], ["lre"], out=lre[:], in0=apar[:, :, 0], scalar1=S5_MAX_RE, scalar2=None, op0=ALU.min)
    OP(P, "dve", "tensor_copy", ["apar"], ["lim"], out=lim[:], in_=apar[:, :, 1])
    OP(P, "act", "activation", ["apar"], ["dt"], out=dt[:], in_=apar[:, :, 2], func=AF.Exp)
    OP(P, "dve", "tensor_tensor", ["lre", "dt"], ["t1"], out=t1[:], in0=lre[:], in1=dt[:], op=ALU.mult)
    OP(P, "act", "activation", ["t1"], ["rho"], out=rho[:], in_=t1[:], func=AF.Exp)
    OP(P, "dve", "tensor_tensor", ["lim", "dt"], ["th"], out=th[:], in0=lim[:], in1=dt[:], op=ALU.mult)
    OP(P, "dve", "tensor_scalar", ["th"], ["t1"], out=t1[:], in0=th[:], scalar1=1.0 / (2 * math.pi), scalar2=12582912.0, op0=ALU.mult, op1=ALU.add)
    OP(P, "dve", "tensor_scalar", ["t1"], ["t1"], out=t1[:], in0=t1[:], scalar1=-12582912.0, scalar2=None, op0=ALU.add)
    C1 = 6.28125
    C2 = 2 * math.pi - C1
    OP(P, "dve", "scalar_tensor_tensor", ["t1", "th"], ["t2"], out=t2[:], in0=t1[:], scalar=-C1, in1=th[:], op0=ALU.mult, op1=ALU.add)
    OP(P, "dve", "scalar_tensor_tensor", ["t1", "t2"], ["t2"], out=t2[:], in0=t1[:], scalar=-C2, in1=t2[:], op0=ALU.mult, op1=ALU.add)
    OP(P, "dve", "tensor_scalar", ["t2"], ["t2"], out=t2[:], in0=t2[:], scalar1=math.pi, scalar2=-math.pi, op0=ALU.min, op1=ALU.max)
    OP(P, "act", "activation", ["t2"], ["s1"], out=s1[:], in_=t2[:], func=AF.Sin)
    OP(P, "dve", "tensor_scalar", ["t2"], ["t3"], out=t3[:], in0=t2[:], scalar1=-1.0, scalar2=None, op0=ALU.mult)
    OP(P, "dve", "tensor_tensor", ["t2", "t3"], ["t3"], out=t3[:], in0=t3[:], in1=t2[:], op=ALU.max)
    OP(P, "act", "activation", ["t3", "halfpi"], ["c1"], out=c1[:], in_=t3[:], func=AF.Sin, scale=-1.0, bias=halfpi[:, 0:1])
    OP(P, "dve", "tensor_tensor", ["rho", "c1"], ["t1"], out=t1[:], in0=rho[:], in1=c1[:], op=ALU.mult)
    OP(P, "dve", "tensor_scalar", ["t1"], ["t1"], out=t1[:], in0=t1[:], scalar1=-1.0, scalar2=None, op0=ALU.add)
    OP(P, "dve", "tensor_tensor", ["rho", "s1"], ["t2"], out=t2[:], in0=rho[:], in1=s1[:], op=ALU.mult)
    OP(P, "dve", "tensor_tensor", ["lre"], ["t3"], out=t3[:], in0=lre[:], in1=lre[:], op=ALU.mult)
    OP(P, "dve", "tensor_tensor", ["lim"], ["cc"], out=cc[:], in0=lim[:], in1=lim[:], op=ALU.mult)
    OP(P, "dve", "tensor_tensor", ["t3", "cc"], ["t3"], out=t3[:], in0=t3[:], in1=cc[:], op=ALU.add)
    OP(P, "dve", "reciprocal", ["t3"], ["t3"], out=t3[:], in_=t3[:])
    OP(P, "dve", "tensor_tensor", ["t1", "lre"], ["fre"], out=fre[:], in0=t1[:], in1=lre[:], op=ALU.mult)
    OP(P, "dve", "tensor_tensor", ["t2", "lim"], ["cc"], out=cc[:], in0=t2[:], in1=lim[:], op=ALU.mult)
    OP(P, "dve", "tensor_tensor", ["fre", "cc"], ["fre"], out=fre[:], in0=fre[:], in1=cc[:], op=ALU.add)
    OP(P, "dve", "tensor_tensor", ["fre", "t3"], ["fre"], out=fre[:], in0=fre[:], in1=t3[:], op=ALU.mult)
    OP(P, "dve", "tensor_tensor", ["t2", "lre"], ["fim"], out=fim[:], in0=t2[:], in1=lre[:], op=ALU.mult)
    OP(P, "dve", "tensor_tensor", ["t1", "lim"], ["cc"], out=cc[:], in0=t1[:], in1=lim[:], op=ALU.mult)
    OP(P, "dve", "tensor_tensor", ["fim", "cc"], ["fim"], out=fim[:], in0=fim[:], in1=cc[:], op=ALU.subtract)
    OP(P, "dve", "tensor_tensor", ["fim", "t3"], ["fim"], out=fim[:], in0=fim[:], in1=t3[:], op=ALU.mult)

    def bc16(x):
        return x[:].unsqueeze(2).to_broadcast([128, 16, 16])

    bbr = sb("bbr", [128, 16, 16], F32); bbi = sb("bbi", [128, 16, 16], F32); tq = sb("tq", [128, 16, 16], F32)
    OP(P, "dve", "tensor_tensor", ["Bt", "fre"], ["bbr"], out=bbr[:], in0=Bt[:, 0], in1=bc16(fre), op=ALU.mult)
    OP(P, "dve", "tensor_tensor", ["Bt", "fim"], ["tq"], out=tq[:], in0=Bt[:, 1], in1=bc16(fim), op=ALU.mult)
    OP(P, "dve", "tensor_tensor", ["bbr", "tq"], ["bbr"], out=bbr[:], in0=bbr[:], in1=tq[:], op=ALU.subtract)
    OP(P, "dve", "tensor_tensor", ["Bt", "fre"], ["bbi"], out=bbi[:], in0=Bt[:, 1], in1=bc16(fre), op=ALU.mult)
    OP(P, "dve", "tensor_tensor", ["Bt", "fim"], ["tq"], out=tq[:], in0=Bt[:, 0], in1=bc16(fim), op=ALU.mult)
    OP(P, "dve", "tensor_tensor", ["bbi", "tq"], ["bbi"], out=bbi[:], in0=bbi[:], in1=tq[:], op=ALU.add)
    ccr = sb("ccr", [128, 16, 16], F32); cci = sb("cci", [128, 16, 16], F32)
    OP(P, "dve", "tensor_tensor", ["Ct", "c1"], ["ccr"], out=ccr[:], in0=Ct[:, 0], in1=bc16(c1), op=ALU.mult)
    OP(P, "dve", "tensor_tensor", ["Ct", "s1"], ["tq"], out=tq[:], in0=Ct[:, 1], in1=bc16(s1), op=ALU.mult)
    OP(P, "dve", "tensor_tensor", ["ccr", "tq"], ["ccr"], out=ccr[:], in0=ccr[:], in1=tq[:], op=ALU.add)
    OP(P, "dve", "tensor_tensor", ["Ct", "s1"], ["cci"], out=cci[:], in0=Ct[:, 0], in1=bc16(s1), op=ALU.mult)
    OP(P, "dve", "tensor_tensor", ["Ct", "c1"], ["tq"], out=tq[:], in0=Ct[:, 1], in1=bc16(c1), op=ALU.mult)
    OP(P, "dve", "tensor_tensor", ["cci", "tq"], ["cci"], out=cci[:], in0=cci[:], in1=tq[:], op=ALU.subtract)

    Mre = sb("Mre", [128, 16, 128], F32); Mim = sb("Mim", [128, 16, 128], F32)
    CTr = sb("CTr", [128, 16, 128], BF16); CTi = sb("CTi", [128, 16, 128], BF16)
    BTr = sb("BTr", [128, 16, 128], BF16); BTi = sb("BTi", [128, 16, 128], BF16)
    for (tl, nm) in ((Mre, "Mre"), (Mim, "Mim")):
        OP(P, "pool", "memset", [], [nm], tl[:], 0.0)
    for (tl, nm) in ((CTr, "CTr"), (CTi, "CTi")):
        OP(P, "pool", "memset", [], [nm], tl[:], 0.0)

    def place(dst, gl):
        full = dst[gl * 64:(gl + 1) * 64, :, :]
        t = full.tensor
        a = full.ap
        pstep = a[0][0]
        return bass.AP(t, full.offset + gl * 16, [[pstep, 64], [512, 4], [160, 4], [1, 16]])

    def src4(x, gl):
        return x[gl * 64:(gl + 1) * 64, :, :].rearrange("p (a b) q -> p a b q", a=4)

    for gl in range(2):
        OP(P, "dve", "tensor_copy", ["bbr", "Mre"], ["Mre"], out=place(Mre, gl), in_=src4(bbr, gl))
        OP(P, "dve", "tensor_copy", ["bbi", "Mim"], ["Mim"], out=place(Mim, gl), in_=src4(bbi, gl))
        OP(P, "dve", "tensor_copy", ["ccr", "CTr"], ["CTr"], out=place(CTr, gl), in_=src4(ccr, gl))
        OP(P, "dve", "tensor_copy", ["cci", "CTi"], ["CTi"], out=place(CTi, gl), in_=src4(cci, gl))
    for (M, mn, BT, bn) in ((Mre, "Mre", BTr, "BTr"), (Mim, "Mim", BTi, "BTi")):
        for g4 in range(4):
            bk = nb()
            for j in range(4):
                k = g4 * 4 + j
                OP(P, "pe", "matmul", [mn, "ident_f"], [("ps", bk)], ps[bk][:, j * 128:(j + 1) * 128], M[:, k, :], ident_f, start=True, stop=True)
            OP(P, "act", "activation", [("ps", bk)], [bn], out=BT[:, g4 * 4:g4 * 4 + 4, :],
               in_=ps[bk][:, :].rearrange("p (a b) -> p a b", a=4), func=AF.Copy)

    NTAB = LB + 1
    tabC = sb("tabC", [128, 16, NTAB], F32); tabS = sb("tabS", [128, 16, NTAB], F32)
    ta = sb("ta", [128, 16, 256], F32); tb_ = sb("tb", [128, 16, 256], F32)
    OP(P, "dve", "memset", [], ["tabC"], tabC[:, :, 0:1], 1.0)
    OP(P, "dve", "memset", [], ["tabS"], tabS[:, :, 0:1], 0.0)
    OP(P, "dve", "tensor_copy", ["c1"], ["cc"], out=cc[:], in_=c1[:])
    OP(P, "dve", "tensor_copy", ["s1"], ["ss"], out=ss[:], in_=s1[:])
    n = 1
    while n < NTAB:
        w = min(n, NTAB - n)

        def bcw(x, w=w):
            return x[:].unsqueeze(2).to_broadcast([128, 16, w])
        src_c = tabC[:, :, 0:w]
        src_s = tabS[:, :, 0:w]
        OP(P, "dve", "tensor_tensor", ["tabC", "cc"], ["ta"], out=ta[:, :, 0:w], in0=src_c, in1=bcw(cc), op=ALU.mult)
        OP(P, "dve", "tensor_tensor", ["tabS", "ss"], ["tb"], out=tb_[:, :, 0:w], in0=src_s, in1=bcw(ss), op=ALU.mult)
        OP(P, "dve", "tensor_tensor", ["ta", "tb"], ["tabC"], out=tabC[:, :, n:n + w], in0=ta[:, :, 0:w], in1=tb_[:, :, 0:w], op=ALU.subtract)
        OP(P, "dve", "tensor_tensor", ["tabC", "ss"], ["ta"], out=ta[:, :, 0:w], in0=src_c, in1=bcw(ss), op=ALU.mult)
        OP(P, "dve", "tensor_tensor", ["tabS", "cc"], ["tb"], out=tb_[:, :, 0:w], in0=src_s, in1=bcw(cc), op=ALU.mult)
        OP(P, "dve", "tensor_tensor", ["ta", "tb"], ["tabS"], out=tabS[:, :, n:n + w], in0=ta[:, :, 0:w], in1=tb_[:, :, 0:w], op=ALU.add)
        n += w
        if n < NTAB:
            OP(P, "dve", "tensor_tensor", ["cc", "ss"], ["t1"], out=t1[:], in0=cc[:], in1=ss[:], op=ALU.mult)
            OP(P, "dve", "tensor_tensor", ["cc"], ["t2"], out=t2[:], in0=cc[:], in1=cc[:], op=ALU.mult)
            OP(P, "dve", "tensor_tensor", ["ss"], ["t3"], out=t3[:], in0=ss[:], in1=ss[:], op=ALU.mult)
            OP(P, "dve", "tensor_tensor", ["t2", "t3"], ["cc"], out=cc[:], in0=t2[:], in1=t3[:], op=ALU.subtract)
            OP(P, "dve", "tensor_scalar", ["t1"], ["ss"], out=ss[:], in0=t1[:], scalar1=2.0, scalar2=None, op0=ALU.mult)

    ut = sb("ut", [128, 2, 4, 2 * LB], BF16)
    wk = [sb("wk%d" % i, [128, LB], F32) for i in range(4)]
    cre = sb("cre", [128, 2, LB], F32); cim = sb("cim", [128, 2, LB], F32)
    vre = sb("vre", [128, 2, LB], F32); vim = sb("vim", [128, 2, LB], F32)
    sre = sb("sre", [128, 2, LB], F32); sim = sb("sim", [128, 2, LB], F32)
    sreb = sb("sreb", [128, 2, LB], BF16); simb = sb("simb", [128, 2, LB], BF16)
    car = sb("car", [128, 16, 2], F32)
    yw = [sb("yw%d" % i, [128, LB], F32) for i in range(3)]
    yo = sb("yo", [128, 2, LB], BF16)
    OP(P, "dve", "memset", [], [("car", k, j) for k in range(16) for j in range(2)], car[:], 0.0)
    uv = uT_d.rearrange("(c p) t -> p c t", p=128) if gather is None else None
    yv = yT_d.rearrange("(c p) t -> p c t", p=128)
    it = 0
    for tb in range(NTB):
        ub = (tb // 2) % 2
        hsl = slice((tb % 2) * LB, (tb % 2 + 1) * LB)
        tsl = slice(tb * LB, (tb + 1) * LB)
        if tb % 2 == 0:
            if gather is None:
                P.dma("sp", ut[:, ub, :, :], uv[:, :, tb * LB:(tb + 2) * LB], writes=[("ut", ub)])
            else:
                G2, idxC = gather
                for kcc in range(4):
                    P.dma("pool", ut[:, ub, kcc, :], G2, writes=[("ut", ub)], idx=idxC[:, tb // 2, kcc:kcc + 1])
        for kc in range(4):
            by = nb()
            for kq in range(4):
                k = kc * 4 + kq
                s = it % 2
                it += 1
                br = nb()
                while br == by:
                    br = nb()
                bi = nb()
                while bi == by:
                    bi = nb()
                OP(P, "pe", "matmul", ["BTr", ("ut", ub)], [("ps", br)], ps[br][:, :], BTr[:, k, :], ut[:, ub, kc, hsl], start=True, stop=True)
                OP(P, "pe", "matmul", ["BTi", ("ut", ub)], [("ps", bi)], ps[bi][:, :], BTi[:, k, :], ut[:, ub, kc, hsl], start=True, stop=True)
                tc0 = tabC[:, k, 0:LB]; ts0 = tabS[:, k, 0:LB]
                tc1 = tabC[:, k, 1:LB + 1]; ts1 = tabS[:, k, 1:LB + 1]
                OP(P, "dve", "tensor_tensor", [("ps", br), "tabC"], [("wk", 0)], out=wk[0][:], in0=ps[br][:, :], in1=tc0, op=ALU.mult)
                OP(P, "dve", "tensor_tensor", [("ps", bi), "tabS"], [("wk", 1)], out=wk[1][:], in0=ps[bi][:, :], in1=ts0, op=ALU.mult)
                OP(P, "pool", "tensor_tensor", [("wk", 0), ("wk", 1)], [("cre", s)], out=cre[:, s, :], in0=wk[0][:], in1=wk[1][:], op=ALU.add)
                OP(P, "dve", "tensor_tensor", [("ps", bi), "tabC"], [("wk", 2)], out=wk[2][:], in0=ps[bi][:, :], in1=tc0, op=ALU.mult)
                OP(P, "dve", "tensor_tensor", [("ps", br), "tabS"], [("wk", 3)], out=wk[3][:], in0=ps[br][:, :], in1=ts0, op=ALU.mult)
                OP(P, "pool", "tensor_tensor", [("wk", 2), ("wk", 3)], [("cim", s)], out=cim[:, s, :], in0=wk[2][:], in1=wk[3][:], op=ALU.subtract)
                rb = rho[:, k:k + 1].to_broadcast([128, LB])
                OP(P, "dve", "tensor_tensor_scan", [("cre", s), "rho", ("car", k, 0)], [("vre", s)], out=vre[:, s, :], data0=rb, data1=cre[:, s, :],
                   initial=car[:, k, 0:1], op0=ALU.mult, op1=ALU.add)
                OP(P, "dve", "tensor_tensor_scan", [("cim", s), "rho", ("car", k, 1)], [("vim", s)], out=vim[:, s, :], data0=rb, data1=cim[:, s, :],
                   initial=car[:, k, 1:2], op0=ALU.mult, op1=ALU.add)
                OP(P, "dve", "tensor_tensor", [("vre", s), "tabC"], [("wk", 0)], out=wk[0][:], in0=vre[:, s, :], in1=tc1, op=ALU.mult)
                OP(P, "dve", "tensor_tensor", [("vim", s), "tabS"], [("wk", 1)], out=wk[1][:], in0=vim[:, s, :], in1=ts1, op=ALU.mult)
                OP(P, "pool", "tensor_tensor", [("wk", 0), ("wk", 1)], [("sre", s)], out=sre[:, s, :], in0=wk[0][:], in1=wk[1][:], op=ALU.subtract)
                OP(P, "dve", "tensor_tensor", [("vre", s), "tabS"], [("wk", 2)], out=wk[2][:], in0=vre[:, s, :], in1=ts1, op=ALU.mult)
                OP(P, "dve", "tensor_tensor", [("vim", s), "tabC"], [("wk", 3)], out=wk[3][:], in0=vim[:, s, :], in1=tc1, op=ALU.mult)
                OP(P, "dve", "tensor_tensor", [("wk", 2), ("wk", 3)], [("sim", s)], out=sim[:, s, :], in0=wk[2][:], in1=wk[3][:], op=ALU.add)
                OP(P, "act", "activation", [("sre", s)], [("car", k, 0)], out=car[:, k, 0:1], in_=sre[:, s, LB - 1:LB], func=AF.Copy)
                OP(P, "act", "activation", [("sim", s)], [("car", k, 1)], out=car[:, k, 1:2], in_=sim[:, s, LB - 1:LB], func=AF.Copy)
                OP(P, "act", "activation", [("sre", s)], [("sreb", s)], out=sreb[:, s, :], in_=sre[:, s, :], func=AF.Copy)
                OP(P, "act", "activation", [("sim", s)], [("simb", s)], out=simb[:, s, :], in_=sim[:, s, :], func=AF.Copy)
                OP(P, "pe", "matmul", ["CTr", ("sreb", s)], [("ps", by)], ps[by][:, :], CTr[:, k, :], sreb[:, s, :], start=(kq == 0), stop=False)
                OP(P, "pe", "matmul", ["CTi", ("simb", s)], [("ps", by)], ps[by][:, :], CTi[:, k, :], simb[:, s, :], start=False, stop=(kq == 3))
            yb = (tb * 4 + kc) % 2
            OP(P, "dve", "scalar_tensor_tensor", [("ps", by), ("ut", ub), "dcol"], [("yw", 0)], out=yw[0][:], in0=ut[:, ub, kc, hsl],
               scalar=dcol[:, kc:kc + 1], in1=ps[by][:, :], op0=ALU.mult, op1=ALU.add)
            if USE_GELU_TANH_AF:
                OP(P, "act", "activation", [("yw", 0)], [("yo", yb)], out=yo[:, yb, :], in_=yw[0][:], func=AF.Gelu_apprx_tanh)
            else:
                OP(P, "dve", "tensor_tensor", [("yw", 0)], [("yw", 1)], out=yw[1][:], in0=yw[0][:], in1=yw[0][:], op=ALU.mult)
                OP(P, "dve", "tensor_scalar", [("yw", 1)], [("yw", 1)], out=yw[1][:], in0=yw[1][:], scalar1=0.044715 * math.sqrt(2 / math.pi),
                   scalar2=math.sqrt(2 / math.pi), op0=ALU.mult, op1=ALU.add)
                OP(P, "dve", "tensor_tensor", [("yw", 0), ("yw", 1)], [("yw", 1)], out=yw[1][:], in0=yw[1][:], in1=yw[0][:], op=ALU.mult)
                OP(P, "act", "activation", [("yw", 1)], [("yw", 2)], out=yw[2][:], in_=yw[1][:], func=AF.Tanh)
                OP(P, "dve", "tensor_scalar", [("yw", 2)], [("yw", 2)], out=yw[2][:], in0=yw[2][:], scalar1=1.0, scalar2=0.5, op0=ALU.add, op1=ALU.mult)
                OP(P, "dve", "tensor_tensor", [("yw", 2), ("yw", 0)], [("yo", yb)], out=yo[:, yb, :], in0=yw[2][:], in1=yw[0][:], op=ALU.mult)
            P.dma("sp", yv[:, kc, tsl], yo[:, yb, :], reads=[("yo", yb)], writes=[("yT", tb, kc)])
    P.flush()
    es.close()


import ml_dtypes
from concourse.bass_utils import run_bass_kernel_spmd

BF16_NP = ml_dtypes.bfloat16
NCORES = 8
OFF = dict(aq=0, ak=2048, av=4096, az=6144, aa=8192, ab=8208, fq=8224, fk=10272, fv=12320, ff=14368)


def core_cols(c):
    cols = []
    for h in (2 * c, 2 * c + 1):
        for nm in ("aq", "ak", "av", "az"):
            cols += list(range(OFF[nm] + h * 128, OFF[nm] + (h + 1) * 128))
    for h in (2 * c, 2 * c + 1):
        for nm in ("fq", "fk", "fv"):
            cols += list(range(OFF[nm] + h * 128, OFF[nm] + (h + 1) * 128))
    cols += [OFF["aa"] + 2 * c, OFF["aa"] + 2 * c + 1, OFF["ab"] + 2 * c, OFF["ab"] + 2 * c + 1, OFF["ff"] + 2 * c, OFF["ff"] + 2 * c + 1]
    return np.array(cols)


def host_consts_A():
    cf = np.zeros((128, 8, 128), np.float32)
    p = np.arange(128)[:, None]
    c = np.arange(128)[None, :]
    same = (p // 64) == (c // 64)
    cf[:, 0, :] = np.eye(128)
    cf[:, 1, :] = (p < c)
    cf[:, 2, :] = np.where(p <= c, 0.0, -30000.0)
    cf[:, 3, :] = 1.0
    cf[:, 4, :] = (c < p) & same
    cf[:, 5, :] = (c >= p) & same
    cf[:, 6, :] = (c > p) & same
    r = np.ones((128, 128), np.float32)
    r[:, 0] = 0
    r[:, 64] = 0
    cf[:, 7, :] = r
    return cf, np.eye(128, dtype=np.float32).astype(BF16_NP)


def load_consts_A(P, nc, cpf, cpb):
    cf = nc.alloc_sbuf_tensor("c_f", [128, 8, 128], F32)
    cb = nc.alloc_sbuf_tensor("c_b", [128, 128], BF16)
    P.dma("sp", cf[:], cpf, writes=["cf"])
    P.dma("sp", cb[:], cpb, writes=["cb"])
    return dict(ident_f=cf[:, 0, :], lt64=cf[0:64, 1, 0:64], negmask=cf[:, 2, :], ones_f=cf[:, 3, :], maskL=cf[:, 4, :], maskUi=cf[:, 5, :],
                maskUs=cf[:, 6, :], rst=cf[0:64, 7, :], ident_bf=cb[:, :])


def gdn_par(inp, c):
    gp = np.zeros((128, 2, 16), np.float32)
    cw = inp["gdn_conv_w"][0]
    for hg in range(2):
        h = 2 * c + hg
        for i in range(3):
            gp[:, hg, 4 * i:4 * i + 4] = cw[:, i * 2048 + h * 128: i * 2048 + (h + 1) * 128].T
        gp[:, hg, 12] = inp["gdn_o_norm"][0]
        gp[:, hg, 13] = inp["gdn_A_log"][0][h]
        gp[:, hg, 14] = inp["gdn_dt_bias"][0][h]
    return gp


def s5_par(inp, c):
    G = slice(32 * c, 32 * c + 32)
    are = inp["s5_A_re"][0][G]
    aim = inp["s5_A_im"][0][G]
    ls = inp["s5_log_step"][0][G]
    apar = np.zeros((128, 16, 3), np.float32)
    bpk = np.zeros((128, 2, 16, 16), np.float32)
    cpk = np.zeros((128, 2, 16, 16), np.float32)
    Bre = inp["s5_B_re"][0][G]
    Bim = inp["s5_B_im"][0][G]
    Cre = inp["s5_C_re"][0][G]
    Cim = inp["s5_C_im"][0][G]
    for k in range(16):
        for gl in range(2):
            g = 2 * k + gl
            rows = slice(gl * 64, gl * 64 + 64)
            apar[rows, k, 0] = are[g]
            apar[rows, k, 1] = aim[g]
            apar[rows, k, 2] = ls[g]
            bpk[rows, 0, k, :] = Bre[g]
            bpk[rows, 1, k, :] = Bim[g]
            cpk[rows, 0, k, :] = Cre[g].T
            cpk[rows, 1, k, :] = Cim[g].T
    dcol = np.ascontiguousarray(inp["s5_D"][0][512 * c:512 * c + 512].reshape(4, 128).T)
    return apar, bpk, cpk, dcol


def _ein(nc, name, shape, dt=F32):
    return nc.dram_tensor(name, shape, dt, kind="ExternalInput").ap()


def _eout(nc, name, shape, dt=F32):
    return nc.dram_tensor(name, shape, dt, kind="ExternalOutput").ap()


def build_A():
    nc = bass.Bass("TRN2", target_bir_lowering=False)
    xT = _ein(nc, "xT", [4096, 8192])
    wsl = _ein(nc, "wsl", [4096, NCOL])
    g0col = _ein(nc, "g0col", [128, 32])
    gpar_d = _ein(nc, "gpar", [128, 2, 16])
    nfb_d = _ein(nc, "nfb", [128, 2])
    cpf = _ein(nc, "cpf", [128, 8, 128])
    cpb = _ein(nc, "cpb", [128, 128], BF16)
    oT = _eout(nc, "oT", [4, 128, 8192], BF16)
    gq = nc.dram_tensor("gq", [8, 128, 8192], F32).ap()
    fq = nc.dram_tensor("fq", [6, 128, 8192], BF16).ap()
    rows = nc.dram_tensor("rows", [6, 8192], F32).ap()
    P = Prog(nc)
    ps = [nc.alloc_psum_tensor("ps%d" % i, [128, 512], F32) for i in range(8)]
    cst = load_consts_A(P, nc, cpf, cpb)
    gpt = nc.alloc_sbuf_tensor("gpt", [128, 2, 16], F32)
    nfbt = nc.alloc_sbuf_tensor("nfbt", [128, 2], F32)
    P.dma("sp", gpt[:], gpar_d, writes=["gpar"])
    P.dma("sp", nfbt[:], nfb_d, writes=["nfb"])
    P.flush()
    emit_A1(P, nc, ps, xT=xT, wsl=wsl, g0col=g0col, gq=gq, fq=fq, rows=rows)
    for hg in range(2):
        emit_gdn_head(P, nc, ps, cst, gq_d=gq[4 * hg:4 * hg + 4], a_d=rows[hg:hg + 1, :], b_d=rows[2 + hg:3 + hg, :], gpar=gpt[:, hg, :],
                      oT_d=oT[hg], pfx="gd%d_" % hg)
    for hf in range(2):
        emit_fox_head(P, nc, ps, cst, qT_d=fq[3 * hf + 0], kT_d=fq[3 * hf + 1], vT_d=fq[3 * hf + 2], f_d=rows[4 + hf:5 + hf, :],
                      nfb_col=nfbt[0:64, hf:hf + 1], oT_d=oT[2 + hf], pfx="fx%d_" % hf)
    return nc


def build_tok(glu, with_hn):
    nc = bass.Bass("TRN2", target_bir_lowering=False)
    inT = _ein(nc, "inT", [D, TOK], BF16)
    xres = _ein(nc, "xres", [TOK, D])
    wm = [_ein(nc, "wm%d" % i, [D, D]) for i in range(2 if glu else 1)]
    gains = _ein(nc, "gains", [4, D])
    wg = _ein(nc, "wg", [D, HID])
    wu = _ein(nc, "wu", [D, HID])
    wd = _ein(nc, "wd", [HID, D])
    ident_d = _ein(nc, "ident_in", [128, 128], BF16)
    xout = _eout(nc, "xout", [TOK, D])
    hnT = _eout(nc, "hnT", [D, TOK], BF16) if with_hn else None
    scr = {k: nc.dram_tensor("scr_" + k, [TOK, D], F32).ap() for k in ("m", "x1", "f")}
    P = Prog(nc)
    ps = [nc.alloc_psum_tensor("ps%d" % i, [128, 512], F32) for i in range(8)]
    actT = nc.alloc_sbuf_tensor("actT", [128, 32, TOK], BF16)
    ident = nc.alloc_sbuf_tensor("ident", [128, 128], BF16)
    P.dma("sp", ident[:], ident_d, writes=["ident"])
    P.flush()
    emit_tok_block(P, nc, ps, actT, ident, inT=inT, xres=xres, wmix=wm,
                   gains=(gains[0:1, :], gains[1:2, :], gains[2:3, :], gains[3:4, :] if with_hn else None),
                   wg=wg, wu=wu, wd=wd, xout=xout, hnT_out=hnT, scr=scr, pfx="t_")
    return nc


def build_C():
    nc = bass.Bass("TRN2", target_bir_lowering=False)
    uT = _ein(nc, "uT", [512, 8192], BF16)
    apar = _ein(nc, "apar", [128, 16, 3])
    bpk = _ein(nc, "bpk", [128, 2, 16, 16])
    cpk = _ein(nc, "cpk", [128, 2, 16, 16])
    dcol = _ein(nc, "dcol", [128, 4])
    cpf = _ein(nc, "cpf", [128, 8, 128])
    cpb = _ein(nc, "cpb", [128, 128], BF16)
    yT = _eout(nc, "yT", [512, 8192], BF16)
    P = Prog(nc)
    ps = [nc.alloc_psum_tensor("ps%d" % i, [128, 512], F32) for i in range(8)]
    cst = load_consts_A(P, nc, cpf, cpb)
    P.flush()
    emit_s5(P, nc, ps, cst, uT_d=uT, apar_d=apar, b_d=bpk, c_d=cpk, dcol_d=dcol, yT_d=yT)
    return nc


def kernel(**inputs):
    inp = {k: np.asarray(v) for k, v in inputs.items()}
    x = inp["x"][0]
    g = inp["norm_g"]
    cf, cb = host_consts_A()
    cores = list(range(NCORES))
    xT = np.ascontiguousarray(x.T)
    w_in = inp["w_in"][0]
    g0col = np.ascontiguousarray(g[0, 0].reshape(32, 128).T)
    fb = inp["fox_f_bias"][0]
    maps = []
    for c in cores:
        maps.append(dict(xT=xT, wsl=np.ascontiguousarray(w_in[:, core_cols(c)]), g0col=g0col, gpar=gdn_par(inp, c),
                         nfb=np.tile(-fb[2 * c:2 * c + 2][None, :], (128, 1)).astype(np.float32), cpf=cf, cpb=cb))
    resA = run_bass_kernel_spmd(build_A(), maps, core_ids=cores)
    oT_full = np.zeros((4096, 8192), dtype=BF16_NP)
    for c in cores:
        o = resA.results[c]["oT"]
        for hg in range(2):
            h = 2 * c + hg
            oT_full[h * 128:(h + 1) * 128] = o[hg]
            oT_full[2048 + h * 128:2048 + (h + 1) * 128] = o[2 + hg]
    ident = cb
    gainsB = np.stack([g[0, 1], g[0, 2], g[0, 3], g[1, 0]]).astype(np.float32)
    maps = []
    for c in cores:
        sl = slice(c * TOK, (c + 1) * TOK)
        maps.append(dict(inT=np.ascontiguousarray(oT_full[:, sl]), xres=np.ascontiguousarray(x[sl]), wm0=inp["w_out"][0], gains=gainsB,
                         wg=inp["ffn_w_gate"][0], wu=inp["ffn_w_up"][0], wd=inp["ffn_w_down"][0], ident_in=ident))
    resB = run_bass_kernel_spmd(build_tok(False, True), maps, core_ids=cores)
    x2 = [resB.results[c]["xout"] for c in cores]
    hnT_full = np.concatenate([resB.results[c]["hnT"] for c in cores], axis=1)
    maps = []
    for c in cores:
        apar, bpk, cpk, dcol = s5_par(inp, c)
        maps.append(dict(uT=np.ascontiguousarray(hnT_full[512 * c:512 * c + 512]), apar=apar, bpk=bpk, cpk=cpk, dcol=dcol, cpf=cf, cpb=cb))
    resC = run_bass_kernel_spmd(build_C(), maps, core_ids=cores)
    yT_full = np.concatenate([resC.results[c]["yT"] for c in cores], axis=0)
    gainsD = np.stack([g[1, 1], g[1, 2], g[1, 3], g[1, 3]]).astype(np.float32)
    maps = []
    for c in cores:
        sl = slice(c * TOK, (c + 1) * TOK)
        maps.append(dict(inT=np.ascontiguousarray(yT_full[:, sl]), xres=x2[c], wm0=inp["s5_w_glu_a"][0], wm1=inp["s5_w_glu_b"][0], gains=gainsD,
                         wg=inp["ffn_w_gate"][1], wu=inp["ffn_w_up"][1], wd=inp["ffn_w_down"][1], ident_in=ident))
    resD = run_bass_kernel_spmd(build_tok(True, False), maps, core_ids=cores)
    out = np.concatenate([resD.results[c]["xout"] for c in cores], axis=0)
    return out[None].astype(np.float32)
```
